# Optimizing a Trainium2 kernel written in Bass

```python
import numpy as np
import jax
import jax.numpy as jnp
from jax import lax

D_MODEL = 1024
BATCH = 16
SEQ = 2048
DEPTH = 4

D_MIX = D_MODEL
HEAD_DIM = 64
POOL_WINDOWS = (2, 4, 8, 16)
POOL_GROUPS = 4
POOL_WIDTH = D_MIX // 4
POOL_GDIM = POOL_WIDTH // POOL_GROUPS
NSA_WIDTH = D_MIX // 2
N_HEADS = NSA_WIDTH // HEAD_DIM
KV_HEADS = 2
HEADS_PER_KV = N_HEADS // KV_HEADS
KV_WIDTH = KV_HEADS * HEAD_DIM
CONV_WIDTH = D_MIX - POOL_WIDTH - NSA_WIDTH
CONV_K = 3
CMP_BLOCK = 32
CMP_STRIDE = 16
SEL_BLOCK = 64
SEL_TOPN = 8
WINDOW = 512
Q_BLOCK = 64
D_FF = 2816
FFN_CONV_K = 3
IN_COLS = POOL_WIDTH + NSA_WIDTH + 6 * KV_WIDTH + 3 * N_HEADS + 3 * CONV_WIDTH
EPS = 1e-6
NEG_INF = -1e30
FORCE_SCORE = 1e6

kernel_name = 'hybrid_pool_nsa_shortconv_block'


def _rmsnorm(x, g):
    xf = x.astype(jnp.float32)
    y = xf * lax.rsqrt(jnp.mean(xf * xf, axis=-1, keepdims=True) + EPS)
    return (y * g.astype(jnp.float32)).astype(x.dtype)


def _causal_dwconv(x, w):
    width = w.shape[0]
    return lax.conv_general_dilated(
        x, w[:, None, :].astype(x.dtype), window_strides=(1,),
        padding=[(width - 1, 0)], dimension_numbers=('NWC', 'WIO', 'NWC'),
        feature_group_count=x.shape[-1])


def _alibi_slopes():
    return 2.0 ** (-8.0 * jnp.arange(1, N_HEADS + 1, dtype=jnp.float32) / N_HEADS)


def _pool_mixer(u, pool_w, pool_scale):
    b, t, _ = u.shape
    ug = u.reshape(b, t, POOL_GROUPS, POOL_GDIM).astype(jnp.float32)
    cs = jnp.cumsum(ug, axis=1)
    csp = jnp.concatenate([jnp.zeros_like(cs[:, :1]), cs], axis=1)
    pos = jnp.arange(t)
    means = []
    for gi, w in enumerate(POOL_WINDOWS):
        start = jnp.maximum(pos + 1 - w, 0)
        cnt = (pos + 1 - start).astype(jnp.float32)[None, :, None]
        means.append((cs[:, :, gi] - csp[:, start, gi]) / cnt)
    diff = (jnp.stack(means, axis=2) - ug).astype(u.dtype)
    y = jnp.einsum('btgc,gcd->btgd', diff, pool_w).reshape(b, t, POOL_WIDTH)
    return y * pool_scale


def _compress(kv, pos_emb, w1, w2):
    t = kv.shape[2]
    n_cmp = (t - CMP_BLOCK) // CMP_STRIDE + 1
    idx = np.arange(n_cmp)[:, None] * CMP_STRIDE + np.arange(CMP_BLOCK)[None, :]
    blocks = kv[:, :, idx] + pos_emb
    flat = blocks.reshape(blocks.shape[:3] + (CMP_BLOCK * HEAD_DIM,))
    return jax.nn.gelu(flat @ w1) @ w2


def _cmp_to_sel_matrix(t):
    n_cmp = (t - CMP_BLOCK) // CMP_STRIDE + 1
    n_sel = t // SEL_BLOCK
    c0 = np.arange(n_cmp) * CMP_STRIDE
    s0 = np.arange(n_sel) * SEL_BLOCK
    ov = np.minimum(c0[:, None] + CMP_BLOCK, s0[None, :] + SEL_BLOCK) - np.maximum(c0[:, None], s0[None, :])
    return jnp.asarray(np.clip(ov, 0, None) / CMP_BLOCK, dtype=jnp.float32)


def _nsa_mixer(q, kv, gate_logits, q_g, k_g, cmp_pos, cmp_w1, cmp_w2):
    b, t, _ = q.shape
    G, HG, DH = KV_HEADS, HEADS_PER_KV, HEAD_DIM
    scale = DH ** -0.5
    q = _rmsnorm(q.reshape(b, t, N_HEADS, DH), q_g)
    q = q.reshape(b, t, G, HG, DH).transpose(0, 2, 3, 1, 4)
    k_cmp_raw, v_cmp_raw, k_slc, v_slc, k_win, v_win = [
        a.reshape(b, t, G, DH).transpose(0, 2, 1, 3) for a in jnp.split(kv, 6, axis=-1)]
    slopes = _alibi_slopes().reshape(G, HG)
    pos = jnp.arange(t)

    k_cmp = _rmsnorm(_compress(k_cmp_raw, cmp_pos[0], cmp_w1[0], cmp_w2[0]), k_g[0])
    v_cmp = _compress(v_cmp_raw, cmp_pos[1], cmp_w1[1], cmp_w2[1])
    n_cmp = k_cmp.shape[2]
    cmp_end = jnp.arange(n_cmp) * CMP_STRIDE + CMP_BLOCK - 1
    dist_c = (pos[:, None] - cmp_end[None, :]).astype(jnp.float32)
    valid_c = dist_c >= 0
    s_c = jnp.einsum('bghtd,bgnd->bghtn', q, k_cmp).astype(jnp.float32) * scale \
        - slopes[:, :, None, None] * dist_c
    p_cmp = jax.nn.softmax(jnp.where(valid_c, s_c, NEG_INF), axis=-1) \
        * jnp.any(valid_c, axis=-1)[:, None].astype(jnp.float32)
    o_cmp = jnp.einsum('bghtn,bgnd->bghtd', p_cmp.astype(v_cmp.dtype), v_cmp)

    n_sel_blocks = t // SEL_BLOCK
    n_top = min(SEL_TOPN, n_sel_blocks)
    imp = jnp.einsum('bghtn,nj->bgtj', p_cmp, _cmp_to_sel_matrix(t))
    blk = jnp.arange(n_sel_blocks)[None, :]
    cur = (pos // SEL_BLOCK)[:, None]
    forced = (blk == 0) | (blk == cur) | (blk == cur - 1)
    imp = jnp.where(forced, FORCE_SCORE, jnp.where(blk <= cur, imp, -FORCE_SCORE))
    _, sel_idx = lax.top_k(imp, n_top)

    k_slc = _rmsnorm(k_slc, k_g[1]).reshape(b, G, n_sel_blocks, SEL_BLOCK, DH)
    v_slc = v_slc.reshape(b, G, n_sel_blocks, SEL_BLOCK, DH)
    pad = ((0, 0), (0, 0), (WINDOW, 0), (0, 0))
    k_win = jnp.pad(_rmsnorm(k_win, k_g[2]), pad)
    v_win = jnp.pad(v_win, pad)
    bi = jnp.arange(b)[:, None, None, None]
    gi = jnp.arange(G)[None, :, None, None]
    in_blk = jnp.arange(SEL_BLOCK)
    win_off = jnp.arange(WINDOW + Q_BLOCK)

    def q_block(c):
        t0 = c * Q_BLOCK
        qc = lax.dynamic_slice_in_dim(q, t0, Q_BLOCK, axis=3)
        tq = t0 + jnp.arange(Q_BLOCK)
        idx = lax.dynamic_slice_in_dim(sel_idx, t0, Q_BLOCK, axis=2)
        ks = k_slc[bi, gi, idx]
        vs = v_slc[bi, gi, idx]
        kpos = idx[..., None] * SEL_BLOCK + in_blk
        d_s = (tq[None, None, :, None, None] - kpos)[:, :, None].astype(jnp.float32)
        s_s = jnp.einsum('bghqd,bgqnld->bghqnl', qc, ks).astype(jnp.float32) * scale \
            - slopes[None, :, :, None, None, None] * d_s
        s_s = jnp.where(d_s >= 0, s_s, NEG_INF).reshape(b, G, HG, Q_BLOCK, n_top * SEL_BLOCK)
        p_s = jax.nn.softmax(s_s, axis=-1).reshape(b, G, HG, Q_BLOCK, n_top, SEL_BLOCK)
        o_s = jnp.einsum('bghqnl,bgqnld->bghqd', p_s.astype(vs.dtype), vs)
        kw = lax.dynamic_slice_in_dim(k_win, t0, WINDOW + Q_BLOCK, axis=2)
        vw = lax.dynamic_slice_in_dim(v_win, t0, WINDOW + Q_BLOCK, axis=2)
        kp = t0 - WINDOW + win_off
        d_w = tq[:, None] - kp[None, :]
        ok = (d_w >= 0) & (d_w < WINDOW) & (kp[None, :] >= 0)
        s_w = jnp.einsum('bghqd,bgkd->bghqk', qc, kw).astype(jnp.float32) * scale \
            - slopes[:, :, None, None] * d_w.astype(jnp.float32)
        p_w = jax.nn.softmax(jnp.where(ok, s_w, NEG_INF), axis=-1)
        o_w = jnp.einsum('bghqk,bgkd->bghqd', p_w.astype(vw.dtype), vw)
        return o_s, o_w

    o_slc, o_win = lax.map(q_block, jnp.arange(t // Q_BLOCK))

    def unblock(o):
        return o.transpose(1, 2, 3, 0, 4, 5).reshape(b, G, HG, t, DH)

    gates = jax.nn.sigmoid(gate_logits.astype(jnp.float32)).reshape(b, t, 3, G, HG)
    gates = gates.transpose(2, 0, 3, 4, 1)[..., None].astype(q.dtype)
    o = gates[0] * o_cmp + gates[1] * unblock(o_slc) + gates[2] * unblock(o_win)
    return o.transpose(0, 3, 1, 2, 4).reshape(b, t, NSA_WIDTH)


def setup_inputs(seed: int = 0) -> dict:
    key = jax.random.key(seed)
    ks = jax.random.split(key, 16)

    def nrm(k, shape, s):
        return jax.random.normal(k, shape, jnp.float32) * s

    return {
        'x': nrm(ks[0], (BATCH, SEQ, D_MODEL), 1.0),
        'norm1_g': 1.0 + nrm(ks[1], (DEPTH, D_MODEL), 0.02),
        'w_in': nrm(ks[2], (DEPTH, D_MODEL, IN_COLS), D_MODEL ** -0.5),
        'pool_w': nrm(ks[3], (DEPTH, POOL_GROUPS, POOL_GDIM, POOL_GDIM), POOL_GDIM ** -0.5),
        'pool_scale': 1.0 + nrm(ks[4], (DEPTH, POOL_WIDTH), 0.02),
        'q_norm_g': 1.0 + nrm(ks[5], (DEPTH, HEAD_DIM), 0.02),
        'k_norm_g': 1.0 + nrm(ks[6], (DEPTH, 3, HEAD_DIM), 0.02),
        'cmp_pos': nrm(ks[7], (DEPTH, 2, CMP_BLOCK, HEAD_DIM), 0.1),
        'cmp_w1': nrm(ks[8], (DEPTH, 2, CMP_BLOCK * HEAD_DIM, HEAD_DIM), (CMP_BLOCK * HEAD_DIM) ** -0.5),
        'cmp_w2': nrm(ks[9], (DEPTH, 2, HEAD_DIM, HEAD_DIM), HEAD_DIM ** -0.5),
        'sconv_w': nrm(ks[10], (DEPTH, CONV_K, CONV_WIDTH), CONV_K ** -0.5),
        'w_out': nrm(ks[11], (DEPTH, D_MIX, D_MODEL), D_MIX ** -0.5),
        'norm2_g': 1.0 + nrm(ks[12], (DEPTH, D_MODEL), 0.02),
        'ffn_up': nrm(ks[13], (DEPTH, D_MODEL, 2 * D_FF), D_MODEL ** -0.5),
        'ffn_conv': nrm(ks[14], (DEPTH, FFN_CONV_K, D_FF), FFN_CONV_K ** -0.5),
        'ffn_down': nrm(ks[15], (DEPTH, D_FF, D_MODEL), D_FF ** -0.5),
    }


def reference(x, norm1_g, w_in, pool_w, pool_scale, q_norm_g, k_norm_g, cmp_pos, cmp_w1, cmp_w2,
              sconv_w, w_out, norm2_g, ffn_up, ffn_conv, ffn_down):
    sizes = [POOL_WIDTH, NSA_WIDTH, 6 * KV_WIDTH, 3 * N_HEADS, CONV_WIDTH, CONV_WIDTH]
    split_at = [int(v) for v in np.cumsum(sizes)]
    for l in range(DEPTH):
        h = _rmsnorm(x, norm1_g[l])
        u_pool, u_q, u_kv, u_gate, c_b, c_c, c_x = jnp.split(h @ w_in[l], split_at, axis=-1)
        y_pool = _pool_mixer(u_pool, pool_w[l], pool_scale[l])
        y_nsa = _nsa_mixer(u_q, u_kv, u_gate, q_norm_g[l], k_norm_g[l], cmp_pos[l], cmp_w1[l], cmp_w2[l])
        y_conv = c_b * _causal_dwconv(c_c * c_x, sconv_w[l])
        x = x + jnp.concatenate([y_pool, y_nsa, y_conv], axis=-1) @ w_out[l]
        h = _rmsnorm(x, norm2_g[l])
        a, g = jnp.split(h @ ffn_up[l], 2, axis=-1)
        x = x + (jax.nn.silu(_causal_dwconv(a, ffn_conv[l])) * g) @ ffn_down[l]
    return x
```

```python
import contextlib
import os
import numpy as np
import concourse.bass as bass
import concourse.mybir as mybir
from concourse.bass_utils import run_bass_kernel_spmd

F32 = mybir.dt.float32
BF16 = mybir.dt.bfloat16
ALU = mybir.AluOpType
AF = mybir.ActivationFunctionType

ENGS = ("pe", "act", "dve", "pool", "sp")
NDMA = 8
ATTACH_WAIT = os.environ.get("ATTACH_WAIT", "1") == "1"


class Prog:
    def __init__(self, nc):
        self.nc = nc
        self.q = {e: [] for e in ENGS}
        self.cnt = {e: 0 for e in ENGS}
        self.dcnt = {e: 0 for e in ENGS}
        self.seen = {e: {} for e in ENGS}
        self.lastw = {}
        self.readers = {}

    def _need(self, eng, tok, waits):
        if tok is None:
            return
        kind, e2, v = tok
        if kind == "c":
            if e2 == eng and eng == "pe":
                return
            key = ("c", e2)
            val = v
        else:
            key = ("d", e2, v % NDMA)
            val = v // NDMA + 1
        if self.seen[eng].get(key, 0) >= val:
            return
        self.seen[eng][key] = val
        waits.append(tok)

    def _deps(self, eng, reads, writes):
        waits = []
        for b in reads:
            self._need(eng, self.lastw.get(b), waits)
        for b in writes:
            self._need(eng, self.lastw.get(b), waits)
            r = self.readers.get(b)
            if r:
                for t in r.values():
                    self._need(eng, t, waits)
        return waits

    def _commit(self, tok, reads, writes):
        kind, e, v = tok
        rk = (kind, e) if kind == "c" else (kind, e, v % NDMA)
        for b in reads:
            self.readers.setdefault(b, {})[rk] = tok
        for b in writes:
            self.lastw[b] = tok
            self.readers[b] = {}

    def op(self, eng, fn, reads=(), writes=()):
        waits = self._deps(eng, reads, writes)
        self.cnt[eng] += 1
        tok = ("c", eng, self.cnt[eng])
        self.q[eng].append(("c", fn, waits, None))
        self._commit(tok, reads, writes)
        return tok

    def dma(self, eng, fn, reads=(), writes=()):
        waits = self._deps(eng, reads, writes)
        j = self.dcnt[eng]
        self.dcnt[eng] += 1
        if j >= NDMA:
            self._need(eng, ("d", eng, j - NDMA), waits)
        tok = ("d", eng, j)
        self.q[eng].append(("d", fn, waits, j))
        self._commit(tok, reads, writes)
        return tok

    def _all_tokens(self):
        toks = []
        for e in ENGS:
            if self.cnt[e]:
                toks.append(("c", e, self.cnt[e]))
            n = self.dcnt[e]
            for s in range(min(NDMA, n)):
                j = n - 1
                while j % NDMA != s:
                    j -= 1
                toks.append(("d", e, j))
        return toks

    def barrier(self):
        toks = self._all_tokens()
        for e in ENGS:
            waits = []
            for t in toks:
                self._need(e, t, waits)
            if waits:
                self.q[e].append(("w", None, waits, None))

    def emit(self):
        nc = self.nc
        toks = [t for t in self._all_tokens() if t[0] == "d"]
        for e in ENGS:
            waits = []
            for t in toks:
                self._need(e, t, waits)
            if waits:
                self.q[e].append(("w", None, waits, None))
        sig = {e: set() for e in ENGS}
        for e in ENGS:
            for (_, _, waits, _) in self.q[e]:
                for (k2, e2, v) in waits:
                    if k2 == "c":
                        sig[e2].add(v)
        rank = {e: {p: i + 1 for i, p in enumerate(sorted(sig[e]))} for e in ENGS}
        with contextlib.ExitStack() as st:
            sems = {e: st.enter_context(nc.semaphore("s_" + e)) for e in ENGS}
            dsems = {e: [st.enter_context(nc.semaphore("d_%s%d" % (e, i))) for i in range(NDMA)]
                     for e in ENGS}
            block = st.enter_context(nc.Block())
            hooks = {"pe": block.tensor, "act": block.scalar, "dve": block.vector,
                     "pool": block.gpsimd, "sp": block.sync}
            for e in ENGS:
                def body(engine, e=e):
                    pos = 0
                    for kind, fn, waits, j in self.q[e]:
                        ws = []
                        for (k2, e2, v) in waits:
                            if k2 == "c":
                                ws.append((sems[e2], rank[e2][v]))
                            else:
                                ws.append((dsems[e2][v % NDMA], 16 * (v // NDMA + 1)))
                        attach = None
                        if kind == "c" and ws and ATTACH_WAIT:
                            attach = ws.pop()
                        for (sm, vv) in ws:
                            engine.wait_ge(sm, vv)
                        if kind == "c":
                            pos += 1
                            ins = fn(engine)
                            if attach is not None:
                                ins._wait_ge(attach[0], attach[1])
                            if pos in rank[e]:
                                ins.then_inc(sems[e], 1)
                        elif kind == "d":
                            fn(engine).then_inc(dsems[e][j % NDMA], 16)
                hooks[e](body)


D = 1024
KC = 8
DFF = 2816
NFC = 22
NH = 8
SLOPES = [2.0 ** (-(h + 1)) for h in range(8)]
NEG = -30000.0
C_Q, C_KSLC, C_KWIN, C_KCMP, C_VCMP, C_VSLC, C_VWIN, C_GATE, C_POOL, C_CB, C_CC, C_CX = (
    0, 512, 640, 768, 896, 1024, 1152, 1280, 1304, 1560, 1816, 2072)
SB_BASE = 16512
SB_END = 229376


def host_consts(T):
    NT = T // 128
    NCMP = (T - 32) // 16 + 1
    t = np.arange(T)
    c = {}
    qa = np.zeros((4, 8, T), np.float32)
    for h in range(8):
        s = SLOPES[h]
        qa[0, h] = -s * 128.0 * (t // 128)
        qa[1, h] = -s * (t % 128)
        qa[2, h] = s
        qa[3, h] = s
    c["c_qal"] = qa
    kw = np.zeros((36, T), np.float32)
    kw[32] = 1.0
    kw[33] = 1.0
    kw[34] = 128.0 * (t // 128)
    kw[35] = t % 128
    ks = kw.copy()
    for j in range(T // 64):
        ks[j, j * 64:(j + 1) * 64] = 1.0
    c["c_kwin"] = kw
    c["c_kslc"] = ks
    n = np.arange(NCMP)
    kc = np.zeros((36, NCMP), np.float32)
    kc[32] = 1.0
    kc[33] = 1.0
    kc[34] = 16.0 * n
    kc[35] = 31.0
    c["c_kcmp"] = kc
    mc = np.where((16 * n[:, None] + 31) <= t[None, :], 0.0, NEG).astype(np.float32)
    c["c_maskc"] = mc
    jj = np.arange(128)[:, None]
    ii = np.arange(128)[None, :]
    c["c_cmd"] = np.where(jj <= ii, 0.0, NEG).astype(np.float32)
    c["c_cmw"] = np.where(ii < jj, 0.0, NEG).astype(np.float32)
    c["c_ident"] = np.eye(128, dtype=np.float32)
    NS = T // 64
    c0 = n * 16
    s0 = np.arange(NS) * 64
    ov = np.minimum(c0[:, None] + 32, s0[None, :] + 64) - np.maximum(c0[:, None], s0[None, :])
    M = np.zeros((NCMP, 33), np.float32)
    M[:, :32][:, :NS] = np.clip(ov, 0, None) / 32.0
    M[:, 32] = 1.0
    c["c_maug"] = M
    blk = np.arange(32)[None, :]
    cur = (t // 64)[:, None]
    forced = (blk == 0) | (blk == cur) | (blk == cur - 1)
    causal = blk <= cur
    keep = (causal & ~forced).astype(np.float32)
    add = np.where(forced, 1e6, np.where(causal, 0.0, -1e6)).astype(np.float32)
    c["c_fkeep"] = keep.reshape(NT, 128, 32).transpose(1, 0, 2).copy()
    c["c_fadd"] = add.reshape(NT, 128, 32).transpose(1, 0, 2).copy()
    selb = np.zeros((25, 3, 64), np.float32)
    selb[0, :, :] = -1.0
    for b in range(3):
        selb[1 + b * 8:1 + (b + 1) * 8, b, :] = -1.0
    c["c_selb"] = selb
    oh = np.zeros((25, 2, 4), np.float32)
    for r in range(24):
        hh = r % 8
        oh[1 + r, hh // 4, hh % 4] = 1.0
    c["c_oh"] = oh
    fix = np.ones((128, 2, 16), np.float32)
    wins = (2, 4, 8, 16)
    for ch in range(2):
        for half in range(2):
            w = wins[2 * ch + half]
            tt = np.arange(16)
            fix[half * 64:(half + 1) * 64, ch, :] = w / np.minimum(tt + 1, w)
    c["c_pfix"] = fix
    return c


CONST_SHAPES = None


def build_program(T, NSEQ, NL, dbg=None, upto=99):
    NT = T // 128
    NTC = T // 512
    NCMP = (T - 32) // 16 + 1
    nc = bass.Bass("TRN2", target_bir_lowering=False)
    P = Prog(nc)

    def din(name, shape):
        return nc.dram_tensor(name, list(shape), F32, kind="ExternalInput").ap()

    xT_d = din("xT", [NSEQ, D, T])
    yT_d = nc.dram_tensor("yT", [NSEQ, D, T], F32, kind="ExternalOutput").ap()
    w_in_d = din("w_in", [NL, D, 2328])
    w_out_d = din("w_out", [NL, D, D])
    ffn_up_d = din("ffn_up", [NL, D, 2 * DFF])
    ffn_down_d = din("ffn_down", [NL, DFF, D])
    cmp_w1_d = din("cmp_w1", [NL, 2, 2048, 64])
    cmp_w2_d = din("cmp_w2", [NL, 2, 64, 64])
    cmp_posT_d = din("cmp_posT", [NL, 2, 64, 32])
    pool_w_d = din("pool_w", [NL, 4, 64, 64])
    g1_d = din("g1", [NL, 128, 8])
    g2_d = din("g2", [NL, 128, 8])
    qg_d = din("qg", [NL, 64, 1])
    kg_d = din("kg", [NL, 64, 3])
    pscale_d = din("pscale", [NL, 128, 2])
    sconv_d = din("sconv", [NL, 128, 2, 3])
    fconv_d = din("fconv", [NL, 128, NFC, 3])
    hc = host_consts(T)
    cd = {k: din(k, v.shape) for k, v in hc.items()}
    dbg_d = {}
    if dbg:
        for k, shp in dbg.items():
            dbg_d[k] = nc.dram_tensor("dbg_" + k, list(shp), F32, kind="ExternalOutput").ap()

    cur = [SB_BASE]

    def alloc(name, shape, dt, at=None):
        per = int(np.prod(shape[1:])) * (4 if dt == F32 else 2)
        per = (per + 31) // 32 * 32
        if at is None:
            off = cur[0]
            cur[0] += per
        else:
            off = at
        assert off + per <= SB_END, (name, off, per)
        return nc.alloc_sbuf_tensor_at(name, list(shape), dt, offset=off), off, per

    def A(name, shape, dt):
        return alloc(name, shape, dt)[0]

    xT = A("xT_sb", [128, KC, T], F32)
    hT = A("hT", [128, KC, T], BF16)
    ident = A("ident", [128, 128], BF16)
    zeros_b = A("zeros_b", [128, 128], BF16)
    ones_f = A("ones_f", [128, 128], F32)
    negones = A("negones", [128, 64], F32)
    cmd = A("cmd", [128, 128], BF16)
    cmw = A("cmw", [128, 128], BF16)
    maug = A("maug", [128, 33], BF16)
    maskc = A("maskc", [128, T], BF16)
    fkeep = A("fkeep", [128, NT, 32], F32)
    fadd = A("fadd", [128, NT, 32], F32)
    selb = A("selb", [128, 3, 64], F32)
    oh = A("oh", [128, 2, 4], F32)
    pfix = A("pfix", [128, 2, 16], F32)
    eps6 = A("eps6", [128, 1], F32)
    tiny = A("tiny", [128, 1], F32)
    onec = A("onec", [128, 1], F32)
    g1 = A("g1s", [128, 8], F32)
    g2 = A("g2s", [128, 8], F32)
    qg = A("qgs", [64, 1], F32)
    qg8 = A("qg8", [64, 1], F32)
    kg = A("kgs", [64, 3], F32)
    pscale = A("pscales", [128, 2], F32)
    sconv = A("sconvs", [128, 2, 3], F32)
    fconv = A("fconvs", [128, NFC, 3], F32)
    W2 = A("W2", [64, 2, 64], BF16)
    posT = A("posT", [64, 2, 32], BF16)
    PWblk = A("PWblk", [128, 2, 128], BF16)
    cb = A("cb", [64, 2], F32)
    kcA = A("kcA", [128, 2, NCMP], BF16)
    vcA = A("vcA", [128, 2, 65], BF16)
    MB = A("MB", [128, 96], BF16)
    small = A("small", [128, 64], F32)
    imp_t = A("imp_t", [128, 4, 32], F32)
    phase_base = cur[0]
    qA = A("qA", [128, NH, T], BF16)
    kA = A("kA", [128, 2, 2, T], BF16)
    Vaug = A("Vaug", [128, NT, 2, 2, 65], BF16)
    lngate = A("lngate", [128, T], F32)
    local_base = cur[0]
    def region(base):
        st = [base]

        def R(name, shape, dt):
            t_, off, per = alloc(name, shape, dt, at=st[0])
            st[0] += per
            return t_
        return R, st

    R2, st2 = region(phase_base)
    wb_pc = R2("wb_pc", [128, KC, 1024], BF16)
    WoPC = R2("WoPC", [128, 4, D], BF16)
    yPC = [R2("yPC0", [128, 4, 512], BF16), R2("yPC1", [128, 4, 512], BF16)]
    ubuf = R2("ubuf", [128, 2, 528], F32)
    sA = [R2("sA0", [128, 528], F32), R2("sA1", [128, 528], F32)]
    sB = [R2("sB0", [128, 528], F32), R2("sB1", [128, 528], F32)]
    dT = [R2("dT0", [128, 512], BF16), R2("dT1", [128, 512], BF16)]
    cxs = [R2("cxs0", [128, 512], F32), R2("cxs1", [128, 512], F32)]
    cbs = [R2("cbs0", [128, 512], F32), R2("cbs1", [128, 512], F32)]
    pbuf = R2("pbuf", [128, 2, 514], F32)
    ybuf = [R2("ybuf0", [128, 512], F32), R2("ybuf1", [128, 512], F32)]
    sq_n = R2("sq_n", [128, 2, 512], F32)
    rstd_n = R2("rstd_n", [128, 512], F32)
    R1, st1 = region(local_base)
    wbuf = [R1("wbuf0", [128, KC, 512], BF16), R1("wbuf1", [128, KC, 512], BF16)]
    W1 = R1("W1", [64, 32, 64], BF16)
    sq1 = R1("sq1", [128, 512], F32)
    sq2 = R1("sq2", [64, 512], F32)
    rstd1 = R1("rstd1", [64, 512], F32)
    rawT = R1("rawT", [64, T], BF16)
    gtmp = R1("gtmp", [64, 4, 128], F32)
    hidg = R1("hidg", [64, 128], BF16)
    RT, stT = region(local_base)
    Pt = [RT("Pt%d" % i, [128, 512], BF16) for i in range(5)]
    comb = [RT("comb%d" % i, [128, 512], F32) for i in range(4)]
    bc_sb = [RT("bc_sb0", [64, 512], F32), RT("bc_sb1", [64, 512], F32)]
    acc = [RT("acc%d" % i, [64, 512], F32) for i in range(4)]
    tmpo = RT("tmpo", [64, 512], F32)
    WoN = [RT("WoN0", [64, NH, 128], BF16), RT("WoN1", [64, NH, 128], BF16)]
    RF, stF = region(phase_base)
    mT = RF("mT", [128, 11, T], BF16)
    abuf = RF("abuf", [128, T + 2], F32)
    yb = [RF("yb0", [128, 512], F32), RF("yb1", [128, 512], F32)]
    sil = [RF("sil0", [128, 512], F32), RF("sil1", [128, 512], F32)]
    wup = [RF("wup0", [128, KC, 256], BF16), RF("wup1", [128, KC, 256], BF16)]
    wdn = [RF("wdn0", [128, 11, 128], BF16), RF("wdn1", [128, 11, 128], BF16)]
    sq_f = RF("sq_f", [128, 2, 512], F32)
    rstd_f = RF("rstd_f", [128, 512], F32)
    for s_ in (st2, st1, stT, stF):
        assert s_[0] <= SB_END, s_

    ps = [nc.alloc_psum_tensor("bank%d" % i, [128, 512], F32) for i in range(8)]
    rr = {}

    def rot(key, banks):
        i = rr.get(key, 0)
        rr[key] = i + 1
        return banks[i % len(banks)]

    def PS(i):
        return "ps%d" % i

    def mm(out, lhsT, rhs, start, stop, reads, writes):
        P.op("pe", lambda e: e.matmul(out, lhsT=lhsT, rhs=rhs, start=start, stop=stop), reads, writes)

    def act(out, in_, func, reads, writes, scale=None, bias=None):
        kw = {}
        if scale is not None:
            kw["scale"] = scale
        if bias is not None:
            kw["bias"] = bias
        P.op("act", lambda e: e.activation(out=out, in_=in_, func=func, **kw), reads, writes)

    def tt(out, in0, in1, op, reads, writes, eng="dve"):
        P.op(eng, lambda e: e.tensor_tensor(out=out, in0=in0, in1=in1, op=op), reads, writes)

    def ts(out, in0, s1, op0, reads, writes, s2=None, op1=None, eng="dve"):
        if op1 is None:
            P.op(eng, lambda e: e.tensor_scalar(out=out, in0=in0, scalar1=s1, scalar2=None, op0=op0), reads, writes)
        else:
            P.op(eng, lambda e: e.tensor_scalar(out=out, in0=in0, scalar1=s1, scalar2=s2, op0=op0, op1=op1), reads, writes)

    def stt(out, in0, scalar, in1, op0, op1, reads, writes):
        P.op("dve", lambda e: e.scalar_tensor_tensor(out=out, in0=in0, scalar=scalar, in1=in1, op0=op0, op1=op1),
             reads, writes)

    def cp(out, in_, reads, writes, eng="dve"):
        P.op(eng, lambda e: e.tensor_copy(out=out, in_=in_), reads, writes)

    def dma(eng, out, in_, reads, writes):
        P.dma(eng, lambda e: e.dma_start(out=out, in_=in_), reads, writes)

    def bcast_mid(ap2, n):
        p, f = ap2.shape
        return ap2.unsqueeze(1).to_broadcast([p, n, f])

    def load_consts():
        dma("pool", ident[:], cd["c_ident"], [], ["ident"])
        dma("pool", cmd[:], cd["c_cmd"], [], ["cmd"])
        dma("pool", cmw[:], cd["c_cmw"], [], ["cmw"])
        dma("pool", maug[0:NCMP, :], cd["c_maug"], [], ["maug"])
        dma("pool", maskc[0:NCMP, :], cd["c_maskc"], [], ["maskc"])
        dma("sp", fkeep[:], cd["c_fkeep"], [], ["fkeep"])
        dma("sp", fadd[:], cd["c_fadd"], [], ["fadd"])
        dma("sp", selb[64:89], cd["c_selb"], [], ["selb"])
        dma("sp", oh[64:89], cd["c_oh"], [], ["oh"])
        dma("sp", pfix[:], cd["c_pfix"], [], ["pfix"])
        P.op("dve", lambda e: e.memset(ones_f[:], 1.0), [], ["ones_f"])
        P.op("dve", lambda e: e.memset(negones[:], -1.0), [], ["negones"])
        P.op("dve", lambda e: e.memset(eps6[:], 1e-6), [], ["eps6"])
        P.op("dve", lambda e: e.memset(tiny[:], 1e-18), [], ["tiny"])
        P.op("dve", lambda e: e.memset(onec[:], 1.0), [], ["onec"])
        P.op("dve", lambda e: e.memset(MB[:], 0.0), [], ["MB"])
        P.op("dve", lambda e: e.memset(zeros_b[:], 0.0), [], ["zeros_b"])
        P.op("pool", lambda e: e.memset(vcA[:], 1.0), [], ["vcA"])

    def init_attn_consts():
        P.op("pool", lambda e: e.memset(qA[64:96, :, :], 0.0), [], ["qA_m"])
        dma("pool", qA[96:100, :, :], cd["c_qal"], [], ["qA_al"])
        for g in range(2):
            dma("pool", kA[64:100, 0, g, :], cd["c_kslc"], [], ["kA_c"])
            dma("pool", kA[64:100, 1, g, :], cd["c_kwin"], [], ["kA_c"])
        P.op("pool", lambda e: e.memset(Vaug[:, :, :, :, 64:65], 1.0), [], ["Vaug_1"])

    def load_layer_small(l):
        dma("sp", g1[:], g1_d[l], [], ["g1"])
        dma("sp", g2[:], g2_d[l], [], ["g2"])
        dma("sp", qg[:], qg_d[l], [], ["qg"])
        dma("sp", kg[:], kg_d[l], [], ["kg"])
        dma("sp", pscale[:], pscale_d[l], [], ["pscale"])
        dma("sp", sconv[:], sconv_d[l], [], ["sconv"])
        dma("sp", fconv[:], fconv_d[l], [], ["fconv"])
        dma("pool", W2[:], cmp_w2_d[l].rearrange("k e f -> e k f"), [], ["W2"])
        dma("pool", posT[:], cmp_posT_d[l].rearrange("k d l -> d k l"), [], ["posT"])
        P.op("pool", lambda e: e.memset(PWblk[:], 0.0), [], ["PWblk"])
        for gi in range(4):
            c_, hf = gi // 2, gi % 2
            dma("pool", PWblk[hf * 64:(hf + 1) * 64, c_, hf * 64:(hf + 1) * 64], pool_w_d[l, gi], [], ["PWblk"])
        ts(qg8[:], qg[:], 0.125, ALU.mult, ["qg"], ["qg8"])
        dma("pool", kcA[64:100, 0, :], cd["c_kcmp"], [], ["kcA_c"])
        dma("pool", kcA[64:100, 1, :], cd["c_kcmp"], [], ["kcA_c"])

    def rmsnorm_to_hT(gvec, gid, sq, rstd, tag):
        for tc in range(NTC):
            cs = slice(tc * 512, (tc + 1) * 512)
            for k in range(KC):
                sqk = sq[:, k % 2, :]
                act(sqk, xT[:, k, cs], AF.Square, [("xT", k, tc)], [(tag + "sq", k % 2)])
                mm(ps[7][:, :], ones_f[:, :], sqk, k == 0, k == KC - 1,
                   ["ones_f", (tag + "sq", k % 2)], [PS(7)])
            act(rstd[:, :], ps[7][:, :], AF.Ln, [PS(7), "eps6"], [tag + "rstd"], scale=1.0 / D, bias=eps6[:, 0:1])
            act(rstd[:, :], rstd[:, :], AF.Exp, [tag + "rstd"], [tag + "rstd"], scale=-0.5)
            for k in range(KC):
                stt(hT[:, k, cs], xT[:, k, cs], gvec[:, k:k + 1], rstd[:, :], ALU.mult, ALU.mult,
                    [("xT", k, tc), tag + "rstd", gid], [("hT", k, tc)])

    def hT_reads(tc):
        return [("hT", k, tc) for k in range(KC)]

    def phase_pool_conv(l):
        wv = w_in_d[l].rearrange("(k p) c -> p k c", p=128)
        dma("pool", wb_pc[:, :, 0:512], wv[:, :, C_POOL:C_POOL + 512], [], ["wb_pc0"])
        dma("pool", wb_pc[:, :, 512:1024], wv[:, :, C_POOL + 512:C_POOL + 1024], [], ["wb_pc1"])
        wo = w_out_d[l].rearrange("(j p) c -> p j c", p=128)
        dma("pool", WoPC[:, 0:2, :], wo[:, 0:2, :], [], ["WoPC"])
        dma("pool", WoPC[:, 2:4, :], wo[:, 6:8, :], [], ["WoPC"])
        wins = (2, 4, 8, 16)
        RB = [0, 1, 2, 3, 4, 5]

        def proj(col0, cs, tc):
            b_ = rot("p2r", RB)
            for k in range(KC):
                mm(ps[b_][:, :], wb_pc[:, k, col0:col0 + 128], hT[:, k, cs], k == 0, k == KC - 1,
                   ["wb_pc0", "wb_pc1", ("hT", k, tc)], [PS(b_)])
            return b_

        def wout_pc(tc):
            cs = slice(tc * 512, (tc + 1) * 512)
            yp_ = yPC[tc % 2]
            for dc in range(KC):
                ob = rot("p2o", [6, 7])
                for j in range(4):
                    mm(ps[ob][:, :], WoPC[:, j, dc * 128:(dc + 1) * 128], yp_[:, j, :], j == 0, j == 3,
                       ["WoPC", ("yPC", tc % 2, j)], [PS(ob)])
                tt(xT[:, dc, cs], xT[:, dc, cs], ps[ob][:, :], ALU.add, [("xT", dc, tc), PS(ob)], [("xT", dc, tc)])

        for tc in range(NTC + 1):
            if tc < NTC:
                cs = slice(tc * 512, (tc + 1) * 512)
                yp_ = yPC[tc % 2]
                for c_ in range(2):
                    ub = ubuf[:, c_, :]
                    pbc = pbuf[:, c_, :]
                    if tc == 0:
                        P.op("dve", lambda e, ub=ub: e.memset(ub[:, 0:16], 0.0), [], [("ubuf", c_)])
                        P.op("dve", lambda e, pbc=pbc: e.memset(pbc[:, 0:2], 0.0), [], [("pbuf", c_)])
                    else:
                        cp(ub[:, 0:16], ub[:, 512:528], [("ubuf", c_)], [("ubuf", c_)])
                        cp(pbc[:, 0:2], pbc[:, 512:514], [("pbuf", c_)], [("pbuf", c_)])
                ccb = {}
                for c_ in range(2):
                    b_ = proj(c_ * 128, cs, tc)
                    act(ubuf[:, c_, 16:528], ps[b_][:, :], AF.Copy, [PS(b_)], [("ubuf", c_)])
                for c_ in range(2):
                    b_ = proj(C_CX - C_POOL + c_ * 128, cs, tc)
                    act(cxs[c_][:, :], ps[b_][:, :], AF.Copy, [PS(b_)], [("cxs", c_)])
                    ccb[c_] = proj(C_CC - C_POOL + c_ * 128, cs, tc)
                    b_ = proj(C_CB - C_POOL + c_ * 128, cs, tc)
                    act(cbs[c_][:, :], ps[b_][:, :], AF.Copy, [PS(b_)], [("cbs", c_)])
                for c_ in range(2):
                    pbc = pbuf[:, c_, :]
                    tt(pbc[:, 2:514], ps[ccb[c_]][:, :], cxs[c_][:, :], ALU.mult, [PS(ccb[c_]), ("cxs", c_)], [("pbuf", c_)])
                for c_ in range(2):
                    ub = ubuf[:, c_, :]
                    sA_, sB_ = sA[c_], sB[c_]
                    nA, nB = ("sA", c_), ("sB", c_)
                    tt(sA_[:, 1:528], ub[:, 1:528], ub[:, 0:527], ALU.add, [("ubuf", c_)], [nA])
                    tt(sB_[:, 3:528], sA_[:, 3:528], sA_[:, 1:526], ALU.add, [nA], [nB])
                    if c_ == 1:
                        tt(sA_[:, 7:528], sB_[:, 7:528], sB_[:, 3:524], ALU.add, [nB], [nA])
                        tt(sB_[:, 15:528], sA_[:, 15:528], sA_[:, 7:520], ALU.add, [nA], [nB])
                    if tc == 0:
                        tt(sA_[0:64, 16:32], sA_[0:64, 16:32], pfix[0:64, c_, :], ALU.mult, [nA, "pfix"], [nA])
                        tt(sB_[64:128, 16:32], sB_[64:128, 16:32], pfix[64:128, c_, :], ALU.mult, [nB, "pfix"], [nB])
                    stt(dT[c_][0:64, :], sA_[0:64, 16:528], 1.0 / wins[2 * c_], ub[0:64, 16:528], ALU.mult, ALU.subtract,
                        [nA, ("ubuf", c_)], [("dT", c_)])
                    stt(dT[c_][64:128, :], sB_[64:128, 16:528], 1.0 / wins[2 * c_ + 1], ub[64:128, 16:528], ALU.mult,
                        ALU.subtract, [nB, ("ubuf", c_)], [("dT", c_)])
                for c_ in range(2):
                    pbc = pbuf[:, c_, :]
                    yb_ = ybuf[c_]
                    yid = ("ybuf", c_)
                    ts(yb_[:, :], pbc[:, 0:512], sconv[:, c_, 0:1], ALU.mult, [("pbuf", c_), "sconv"], [yid])
                    stt(yb_[:, :], pbc[:, 1:513], sconv[:, c_, 1:2], yb_[:, :], ALU.mult, ALU.add,
                        [("pbuf", c_), "sconv", yid], [yid])
                    stt(yb_[:, :], pbc[:, 2:514], sconv[:, c_, 2:3], yb_[:, :], ALU.mult, ALU.add,
                        [("pbuf", c_), "sconv", yid], [yid])
                    tt(yp_[:, 2 + c_, :], yb_[:, :], cbs[c_][:, :], ALU.mult, [yid, ("cbs", c_)], [("yPC", tc % 2, 2 + c_)])
            if tc >= 1:
                wout_pc(tc - 1)
            if tc < NTC:
                for c_ in range(2):
                    b_ = rot("p2r", RB)
                    mm(ps[b_][:, :], PWblk[:, c_, :], dT[c_][:, :], True, True, ["PWblk", ("dT", c_)], [PS(b_)])
                    act(yPC[tc % 2][:, c_, :], ps[b_][:, :], AF.Copy, [PS(b_), "pscale"], [("yPC", tc % 2, c_)],
                        scale=pscale[:, c_:c_ + 1])

    hn_pending = []

    def headnorm_flush():
        while hn_pending:
            pb, sqb, sqid, gain_ap, gain_id, out_ap, out_id = hn_pending.pop(0)
            sb_ = rot("p1s", [3, 4])
            mm(ps[sb_][0:64, :], ones_f[0:64, 0:64], sqb, True, True, ["ones_f", sqid], [PS(sb_)])
            act(rstd1[:, :], ps[sb_][0:64, :], AF.Ln, [PS(sb_), "eps6"], ["rstd1"], scale=1.0 / 64, bias=eps6[0:64, 0:1])
            act(rstd1[:, :], rstd1[:, :], AF.Exp, ["rstd1"], ["rstd1"], scale=-0.5)
            stt(out_ap, ps[pb][0:64, :], gain_ap, rstd1[:, :], ALU.mult, ALU.mult, [PS(pb), "rstd1", gain_id], [out_id])

    def headnorm_store(pb, gain_ap, gain_id, out_ap, out_id):
        i_ = rot("p1q", [0, 1])
        sqb = sq1[0:64, :] if i_ == 0 else sq2[:, :]
        sqid = ("sqh", i_)
        act(sqb, ps[pb][0:64, :], AF.Square, [PS(pb)], [sqid])
        hn_pending.append((pb, sqb, sqid, gain_ap, gain_id, out_ap, out_id))

    def phase_proj(l):
        wv = w_in_d[l].rearrange("(k p) c -> p k c", p=128)
        dma("pool", wbuf[0][:, :, 0:512], wv[:, :, C_Q:C_Q + 512], [], ["wbuf0"])
        dma("pool", wbuf[1][:, :, 0:512], wv[:, :, C_KSLC:C_KSLC + 512], [], ["wbuf1"])
        for h in range(NH):
            for tc in range(NTC):
                cs = slice(tc * 512, (tc + 1) * 512)
                pb = rot("p1a", [0, 1, 2])
                for k in range(KC):
                    mm(ps[pb][0:64, :], wbuf[0][:, k, h * 64:(h + 1) * 64], hT[:, k, cs], k == 0, k == KC - 1,
                       ["wbuf0", ("hT", k, tc)], [PS(pb)])
                headnorm_flush()
                headnorm_store(pb, qg8[:, 0:1], "qg8", qA[0:64, h, cs], ("qA", h, tc))
        dma("pool", wbuf[0][:, :, 0:280], wv[:, :, C_VSLC:C_VSLC + 280], [], ["wbuf0"])
        for br in range(2):
            for g in range(2):
                co = br * 128 + g * 64
                for tc in range(NTC):
                    cs = slice(tc * 512, (tc + 1) * 512)
                    pb = rot("p1a", [0, 1, 2])
                    for k in range(KC):
                        mm(ps[pb][0:64, :], wbuf[1][:, k, co:co + 64], hT[:, k, cs], k == 0, k == KC - 1,
                           ["wbuf1", ("hT", k, tc)], [PS(pb)])
                    headnorm_flush()
                    headnorm_store(pb, kg[:, 1 + br:2 + br], "kg", kA[0:64, br, g, cs], ("kA", br, g, tc))
        headnorm_flush()
        for kv in range(2):
            dma("pool", W1[:, :, :], cmp_w1_d[l, kv].rearrange("(l d) e -> d l e", d=64), [], ["W1"])
            for li in range(32):
                mm(ps[6][0:64, 0:1], W1[:, li, :], posT[:, kv, li:li + 1], li == 0, li == 31, ["W1", "posT"], [PS(6)])
            cp(cb[:, kv:kv + 1], ps[6][0:64, 0:1], [PS(6)], ["cb"])
            for g in range(2):
                co = 256 + kv * 128 + g * 64
                for tc in range(NTC):
                    cs = slice(tc * 512, (tc + 1) * 512)
                    pb = rot("p1a", [0, 1])
                    for k in range(KC):
                        mm(ps[pb][0:64, :], wbuf[1][:, k, co:co + 64], hT[:, k, cs], k == 0, k == KC - 1,
                           ["wbuf1", ("hT", k, tc)], [PS(pb)])
                    act(rawT[:, cs], ps[pb][0:64, :], AF.Copy, [PS(pb)], ["rawT"])
                for li in range(32):
                    rhs = rawT[:, li:li + 16 * (NCMP - 1) + 1:16]
                    mm(ps[6][0:64, 0:NCMP], W1[:, li, :], rhs, li == 0, li == 31, ["W1", "rawT"], [PS(6)])
                x_ = gtmp[:, 0, 0:NCMP]
                x2 = gtmp[:, 1, 0:NCMP]
                z_ = gtmp[:, 2, 0:NCMP]
                e_ = gtmp[:, 3, 0:NCMP]
                act(x_, ps[6][0:64, 0:NCMP], AF.Identity, [PS(6), "cb"], ["g_x"], bias=cb[:, kv:kv + 1])
                tt(x2, x_, x_, ALU.mult, ["g_x"], ["g_x2"])
                ts(x2, x2, 0.044715, ALU.mult, ["g_x2"], ["g_x2"], s2=1.0, op1=ALU.add)
                tt(z_, x2, x_, ALU.mult, ["g_x2", "g_x"], ["g_z"])
                act(e_, z_, AF.Exp, ["g_z"], ["g_e"], scale=-1.5957691216057308)
                ts(e_, e_, 1.0, ALU.add, ["g_e"], ["g_e"])
                P.op("dve", lambda e, e_=e_: e.reciprocal(out=e_, in_=e_), ["g_e"], ["g_e"])
                tt(hidg[:, 0:NCMP], x_, e_, ALU.mult, ["g_x", "g_e"], ["hidg"])
                if kv == 0:
                    mm(ps[7][0:64, 0:NCMP], W2[:, 0, :], hidg[:, 0:NCMP], True, True, ["W2", "hidg"], [PS(7)])
                    act(sq1[0:64, 0:NCMP], ps[7][0:64, 0:NCMP], AF.Square, [PS(7)], [("sqh", 0)])
                    sb_ = rot("p1s", [3, 4])
                    mm(ps[sb_][0:64, 0:NCMP], ones_f[0:64, 0:64], sq1[0:64, 0:NCMP], True, True, ["ones_f", ("sqh", 0)], [PS(sb_)])
                    act(rstd1[:, 0:NCMP], ps[sb_][0:64, 0:NCMP], AF.Ln, [PS(sb_), "eps6"], ["rstd1"], scale=1.0 / 64,
                        bias=eps6[0:64, 0:1])
                    act(rstd1[:, 0:NCMP], rstd1[:, 0:NCMP], AF.Exp, ["rstd1"], ["rstd1"], scale=-0.5)
                    stt(kcA[0:64, g, :], ps[7][0:64, 0:NCMP], kg[:, 0:1], rstd1[:, 0:NCMP], ALU.mult, ALU.mult,
                        [PS(7), "rstd1", "kg"], [("kcA", g)])
                else:
                    mm(ps[7][0:NCMP, 0:64], hidg[:, 0:NCMP], W2[:, 1, :], True, True, ["W2", "hidg"], [PS(7)])
                    cp(vcA[0:NCMP, g, 0:64], ps[7][0:NCMP, 0:64], [PS(7)], [("vcA", g)])
        for ti in range(NT):
            tsl = slice(ti * 128, (ti + 1) * 128)
            vb = rot("p1v", [4, 5])
            for br in range(2):
                for k in range(KC):
                    mm(ps[vb][:, br * 128:(br + 1) * 128], hT[:, k, tsl], wbuf[0][:, k, br * 128:(br + 1) * 128],
                       k == 0, k == KC - 1, ["wbuf0", ("hT", k, ti // 4)], [PS(vb)])
            cp(Vaug[:, ti, :, :, 0:64], ps[vb][:, 0:256].rearrange("p (b g d) -> p b g d", b=2, g=2),
               [PS(vb)], [("Vaug", ti)])
        for tc in range(NTC):
            cs = slice(tc * 512, (tc + 1) * 512)
            pb = rot("p1a", [0, 1])
            for k in range(KC):
                mm(ps[pb][0:89, :], wbuf[0][:, k, 191:280], hT[:, k, cs], k == 0, k == KC - 1,
                   ["wbuf0", ("hT", k, tc)], [PS(pb)])
            act(sq1[64:89, :], ps[pb][64:89, :], AF.Exp, [PS(pb)], ["sq1"], scale=-1.0)
            act(lngate[64:89, cs], sq1[64:89, :], AF.Ln, ["sq1", "onec"], [("lngate", tc)], scale=1.0, bias=onec[64:89, 0:1])

    def epi_ln(ob, cbuf):
        act(cbuf[64:65, :], ps[ob][64:65, :], AF.Ln, [PS(ob), "tiny"], [("lnD", id(cbuf))], scale=1.0, bias=tiny[64:65, 0:1])

    def epi_mm(b, cbuf):
        mm(ps[6][0:64, :], selb[64:89, b, :], cbuf[64:89, :], True, True,
           ["selb", ("lnD", id(cbuf)), ("GMq", id(cbuf))], [PS(6)])

    def epi_fin(ob, b, g, qt, first, last, aj):
        qs = slice(qt * 128, (qt + 1) * 128)
        bi = rot("bcs", [0, 1])
        bcs = bc_sb[bi]
        bid = ("bc_sb", bi)
        ac = acc[aj]
        aid = ("acc", aj)
        act(bcs[:, :], ps[6][0:64, :], AF.Exp, [PS(6)], [bid])
        o_out = hT[0:64, 4 * g:4 * g + 4, qs]
        o_id = [("hT", k, qt // 4) for k in range(4 * g, 4 * g + 4)]
        acc3 = ac[:, :].rearrange("p (h q) -> p h q", h=4)
        tmp3 = tmpo[:, :].rearrange("p (h q) -> p h q", h=4)
        if first:
            tt(ac[:, :], ps[ob][0:64, :], bcs[:, :], ALU.mult, [PS(ob), bid], [aid])
        else:
            tt(tmpo[:, :], ps[ob][0:64, :], bcs[:, :], ALU.mult, [PS(ob), bid], ["tmpo"])
            if last:
                tt(o_out, acc3, tmp3, ALU.add, [aid, "tmpo"], o_id)
            else:
                tt(ac[:, :], ac[:, :], tmpo[:, :], ALU.add, [aid, "tmpo"], [aid])

    def phase_attn(l):
        LAG = 2
        NB = 4
        NDUM = int(os.environ.get("NDUM", "0"))
        DUMN = int(os.environ.get("DUMN", "128"))
        wo = w_out_d[l]
        tasks = []
        for r_ in range(NT // 2):
            tasks += [(0, NT - 1 - r_), (0, r_), (1, NT - 1 - r_), (1, r_)]
        done_tc = {}
        tiles = []
        tstart = []
        for j, (g, qt) in enumerate(tasks):
            tstart.append(len(tiles))
            for kt in range(qt + 1):
                tiles.append((j, 0, kt, kt == 0, kt == qt))
            k0 = max(0, qt - 4)
            for kt in range(k0, qt + 1):
                tiles.append((j, 1, kt, kt == k0, kt == qt))
        n = len(tiles)
        deferred = {}

        def at(step, fn):
            deferred.setdefault(step, []).append(fn)

        obank = {}
        eps = []

        def ep_new(ob, b, j, first, last, cbuf):
            e_ = {"ob": ob, "b": b, "j": j, "first": first, "last": last, "cbuf": cbuf, "stage": 0}
            eps.append(e_)
            return e_

        def ep_to(e_, target):
            while e_["stage"] < target:
                nxt = e_["stage"] + 1
                for p_ in eps:
                    if p_ is e_:
                        break
                    if nxt == 1 and p_["cbuf"] is e_["cbuf"]:
                        ep_to(p_, 2)
                    if nxt == 2:
                        ep_to(p_, 3)
                if nxt == 1:
                    epi_ln(e_["ob"], e_["cbuf"])
                elif nxt == 2:
                    epi_mm(e_["b"], e_["cbuf"])
                else:
                    g_, qt_ = tasks[e_["j"]]
                    epi_fin(e_["ob"], e_["b"], g_, qt_, e_["first"], e_["last"], e_["j"] % NB)
                    owners.pop(e_["ob"], None)
                    if e_["last"]:
                        done_tc[qt_ // 4] = done_tc.get(qt_ // 4, 0) + 1
                        if done_tc[qt_ // 4] == 8:
                            sched_wout(qt_ // 4)
                e_["stage"] = nxt
            while eps and eps[0]["stage"] == 3:
                eps.pop(0)

        owners = {}

        def take_obank(hold=True):
            while True:
                for _ in range(3):
                    ob = rot("to", [3, 4, 5])
                    if ob not in owners:
                        if hold:
                            owners[ob] = True
                        return ob
                assert eps, "no free O bank and nothing to force"
                ep_to(eps[0], 3)

        cur_step = [0]

        def sched_wout(tc):
            cs = slice(tc * 512, (tc + 1) * 512)

            def piece(dc):
                def f():
                    wi = rot("won", [0, 1])
                    wb_ = WoN[wi]
                    wid = ("WoN", wi)
                    dma("pool", wb_[:, :, :],
                        wo[256:768, dc * 128:(dc + 1) * 128].rearrange("(h d) c -> d h c", d=64), [], [wid])
                    ob = take_obank(hold=False)
                    for h in range(NH):
                        mm(ps[ob][:, :], wb_[:, h, :], hT[0:64, h, cs], h == 0, h == NH - 1,
                           [wid, ("hT", h, tc)], [PS(ob)])
                    tt(xT[:, dc, cs], xT[:, dc, cs], ps[ob][:, :], ALU.add, [("xT", dc, tc), PS(ob)], [("xT", dc, tc)])
                return f
            for dc in range(KC):
                at(cur_step[0] + 2 + 2 * dc, piece(dc))

        def q_ops(g, qt):
            qs = slice(qt * 128, (qt + 1) * 128)
            q_rhs = qA[0:100, 4 * g:4 * g + 4, qs]
            q_ids = [("qA", h, qt // 4) for h in range(4 * g, 4 * g + 4)] + ["qA_al"]
            return qs, q_rhs, q_ids

        chains = {}

        def chain_run(j, k):
            c_ = chains[j]
            while c_["done"] <= k:
                c_["fns"][c_["done"]]()
                c_["done"] += 1

        def cmp_chain(j):
            g, qt = tasks[j]
            qs, q_rhs, q_ids = q_ops(g, qt)
            cbuf = comb[j % NB]
            st_ = {}

            def A_():
                for e2_ in list(eps):
                    if e2_["cbuf"] is cbuf:
                        ep_to(e2_, 3)
                tt(cbuf[64:89, :].rearrange("p (h q) -> p h q", h=4), bcast_mid(lngate[64:89, qs], 4),
                   oh[64:89, g, :].unsqueeze(2).to_broadcast([25, 4, 128]), ALU.mult,
                   [("lngate", qt // 4), "oh"], [("GMq", id(cbuf)), ("lnD", id(cbuf))])
                sb_ = rot("ts", [0, 1, 2])
                mm(ps[sb_][0:NCMP, :], kcA[0:100, g, :], q_rhs, True, False, q_ids + [("kcA", g), "kcA_c"], [PS(sb_)])
                mm(ps[sb_][0:NCMP, :], ident[0:NCMP, 0:NCMP], bcast_mid(maskc[0:NCMP, qs], 4), False, True,
                   ["ident", "maskc"], [PS(sb_)])
                pi = rot("tp", list(range(5)))
                act(Pt[pi][0:NCMP, :], ps[sb_][0:NCMP, :], AF.Exp, [PS(sb_)], [("Pt", pi)])
                st_["pi"] = pi

            def B_():
                if j - 1 in chains:
                    chain_run(j - 1, 5)
                pi = st_["pi"]
                ob = take_obank()
                st_["ob"] = ob
                mm(ps[ob][0:65, :], vcA[0:NCMP, g, :], Pt[pi][0:NCMP, :], True, True,
                   [("vcA", g), "vcA", ("Pt", pi)], [PS(ob)])
                for hh in range(4):
                    mm(ps[7][:, hh * 33:(hh + 1) * 33], Pt[pi][0:NCMP, hh * 128:(hh + 1) * 128], maug[0:NCMP, :],
                       True, True, [("Pt", pi), "maug"], [PS(7)])
                U3 = ps[7][:, 0:132].rearrange("p (h c) -> p h c", c=33)
                den = small[:, 0:4]
                ts(den, U3[:, :, 32], 1e-18, ALU.add, [PS(7)], ["den"])
                P.op("dve", lambda e, den=den: e.reciprocal(out=den, in_=den), ["den"], ["den"])
                tt(imp_t[:, :, :], U3[:, :, 0:32], den.unsqueeze(2).to_broadcast([128, 4, 32]), ALU.mult,
                   [PS(7), "den"], ["imp_t"])
                tt(imp_t[:, 0:2, :], imp_t[:, 0:2, :], imp_t[:, 2:4, :], ALU.add, ["imp_t"], ["imp_t"])
                imp = small[:, 8:40]
                tt(imp, imp_t[:, 0, :], imp_t[:, 1, :], ALU.add, ["imp_t"], ["imp"])
                tt(imp, imp, fkeep[:, qt, :], ALU.mult, ["imp", "fkeep"], ["imp"])
                tt(imp, imp, fadd[:, qt, :], ALU.add, ["imp", "fadd"], ["imp"])
                m8 = small[:, 40:48]
                P.op("dve", lambda e, m8=m8, imp=imp: e.max(out=m8, in_=imp), ["imp"], ["m8"])
                ts(imp, imp, m8[:, 7:8], ALU.is_ge, ["imp", "m8"], ["imp"], s2=1.0, op1=ALU.subtract)
                ts(MB[:, 64:96], imp, -NEG, ALU.mult, ["imp"], ["MB"])

            def C1_():
                st_["ep"] = ep_new(st_["ob"], 0, j, True, False, cbuf)
                ep_to(st_["ep"], 1)

            def C2_():
                ep_to(st_["ep"], 2)

            def C3_():
                ep_to(st_["ep"], 3)

            def D_():
                mm(ps[7][0:96, 256:384], MB[:, 0:96], ident[:, :], True, True, ["MB", "ident"], [PS(7)])
                act(qA[64:96, 4 * g:4 * g + 4, qs], bcast_mid(ps[7][64:96, 256:384], 4), AF.Copy, [PS(7)],
                    [("qAm", g, qt)])

            chains[j] = {"fns": [A_, B_, C1_, C2_, C3_, D_], "done": 0}
            if j < 2:
                chain_run(j, 5)
            else:
                s0 = tstart[j - 2]
                lim = tstart[j] - 1
                for k_, dstep in enumerate((0, 2, 3, 4, 5, 8)):
                    at(min(s0 + dstep, lim), lambda k_=k_: chain_run(j, k_))

        info = {}

        def emit_qk(i):
            j, br, kt, first, last = tiles[i]
            g, qt = tasks[j]
            qs, q_rhs, q_ids = q_ops(g, qt)
            ks_ = slice(kt * 128, (kt + 1) * 128)
            sb_ = rot("ts", [0, 1, 2])
            diag = kt == qt
            edge = (br == 1) and (kt == qt - 4)
            extra = [("qAm", g, qt)] if br == 0 else []
            mm(ps[sb_][:, :], kA[0:100, br, g, ks_], q_rhs, True, not (diag or edge),
               q_ids + extra + [("kA", br, g, kt // 4), "kA_c"], [PS(sb_)])
            if diag:
                mm(ps[sb_][:, :], ident[:, :], bcast_mid(cmd[:, :], 4), False, True, ["ident", "cmd"], [PS(sb_)])
            if edge:
                mm(ps[sb_][:, :], ident[:, :], bcast_mid(cmw[:, :], 4), False, True, ["ident", "cmw"], [PS(sb_)])
            pi = rot("tp", list(range(5)))
            act(Pt[pi][:, :], ps[sb_][:, :], AF.Exp, [PS(sb_)], [("Pt", pi)])
            info[i] = pi

        def emit_pv(i, step):
            j, br, kt, first, last = tiles[i]
            g, qt = tasks[j]
            pi = info.pop(i)
            if first:
                obank[(j, br)] = take_obank()
            ob = obank[(j, br)]
            mm(ps[ob][0:65, :], Vaug[:, kt, br, g, :], Pt[pi][:, :], first, last,
               [("Vaug", kt), "Vaug_1", ("Pt", pi)], [PS(ob)])
            if not last:
                for _d in range(NDUM):
                    mm(ps[ob][0:65, 0:DUMN], zeros_b[:, 0:65], ident[:, 0:DUMN], False, False, ["zeros_b", "ident"], [PS(ob)])
            if last:
                cbuf = comb[j % NB]
                ep_ = ep_new(ob, 1 + br, j, False, br == 1, cbuf)
                ep_to(ep_, 1)
                at(step + 1, lambda: ep_to(ep_, 2))
                at(step + 2, lambda: ep_to(ep_, 3))

        for j in range(len(tasks)):
            cmp_chain(j)
        i = 0
        while i < n + LAG or deferred:
            cur_step[0] = i
            if i < n:
                emit_qk(i)
            if 0 <= i - LAG < n:
                emit_pv(i - LAG, i)
            for fn in deferred.pop(i, []):
                fn()
            i += 1
        for e_ in list(eps):
            ep_to(e_, 3)
        while deferred:
            k_ = min(deferred)
            for fn in deferred.pop(k_):
                fn()

    def phase_ffn(l):
        up = ffn_up_d[l].rearrange("(k p) c -> p k c", p=128)
        dn = ffn_down_d[l]
        P.op("dve", lambda e: e.memset(abuf[:, 0:2], 0.0), [], ["abuf_h"])
        for half in range(2):
            for fi in range(11):
                fc = half * 11 + fi
                wb = wup[fc % 2]
                wid = ("wup", fc % 2)
                dma("pool", wb[:, :, 0:128], up[:, :, fc * 128:(fc + 1) * 128], [], [wid])
                dma("pool", wb[:, :, 128:256], up[:, :, DFF + fc * 128:DFF + (fc + 1) * 128], [], [wid])
                for tc in range(NTC):
                    cs = slice(tc * 512, (tc + 1) * 512)
                    ab = rot("fa", [0, 1])
                    gb = rot("fg", [2, 3])
                    for k in range(KC):
                        mm(ps[ab][:, :], wb[:, k, 0:128], hT[:, k, cs], k == 0, k == KC - 1, [wid, ("hT", k, tc)], [PS(ab)])
                    for k in range(KC):
                        mm(ps[gb][:, :], wb[:, k, 128:256], hT[:, k, cs], k == 0, k == KC - 1, [wid, ("hT", k, tc)], [PS(gb)])
                    act(abuf[:, 2 + tc * 512:2 + (tc + 1) * 512], ps[ab][:, :], AF.Copy, [PS(ab)], [("abuf", tc)])
                    rd = [("abuf", tc), "abuf_h", "fconv"] + ([("abuf", tc - 1)] if tc else [])
                    y_ = yb[tc % 2]
                    yid = ("yb", tc % 2)
                    o = tc * 512
                    act(y_[:, :], abuf[:, o:o + 512], AF.Copy, rd, [yid], scale=fconv[:, fc, 0:1])
                    stt(y_[:, :], abuf[:, o + 1:o + 513], fconv[:, fc, 1:2], y_[:, :], ALU.mult, ALU.add, rd + [yid], [yid])
                    stt(y_[:, :], abuf[:, o + 2:o + 514], fconv[:, fc, 2:3], y_[:, :], ALU.mult, ALU.add, rd + [yid], [yid])
                    s_ = sil[tc % 2]
                    sid = ("sil", tc % 2)
                    act(s_[:, :], y_[:, :], AF.Silu, [yid], [sid])
                    tt(mT[:, fi, cs], s_[:, :], ps[gb][:, :], ALU.mult, [sid, PS(gb)], [("mT", fi, tc)])
            for dc in range(KC):
                wd = wdn[dc % 2]
                wdid = ("wdn", dc % 2)
                dma("pool", wd[:, :, :],
                    dn[half * 1408:(half + 1) * 1408, dc * 128:(dc + 1) * 128].rearrange("(f p) c -> p f c", p=128),
                    [], [wdid])
                for tc in range(NTC):
                    cs = slice(tc * 512, (tc + 1) * 512)
                    ob = rot("fd", [4, 5])
                    for fi in range(11):
                        mm(ps[ob][:, :], wd[:, fi, :], mT[:, fi, cs], fi == 0, fi == 10, [wdid, ("mT", fi, tc)], [PS(ob)])
                    tt(xT[:, dc, cs], xT[:, dc, cs], ps[ob][:, :], ALU.add, [("xT", dc, tc), PS(ob)], [("xT", dc, tc)])

    load_consts()
    for s in range(NSEQ):
        xv = xT_d[s].rearrange("(k p) t -> p k t", p=128)
        for k in range(KC):
            dma("sp", xT[:, k, :], xv[:, k, :], [], [("xT", k, tc) for tc in range(NTC)])
        for l in range(NL):
            if upto >= 1:
                load_layer_small(l)
            if upto >= 2:
                rmsnorm_to_hT(g1, "g1", sq_n, rstd_n, "n1")
            if upto >= 3:
                phase_pool_conv(l)
            P.barrier()
            if upto >= 4:
                init_attn_consts()
            if upto >= 5:
                phase_proj(l)
            P.barrier()
            if upto >= 6:
                phase_attn(l)
            P.barrier()
            if upto >= 8:
                rmsnorm_to_hT(g2, "g2", sq_f, rstd_f, "n2")
            if upto >= 9:
                phase_ffn(l)
            P.barrier()
        yv = yT_d[s].rearrange("(k p) t -> p k t", p=128)
        for k in range(KC):
            dma("sp", yv[:, k, :], xT[:, k, :], [("xT", k, tc) for tc in range(NTC)], [])
    P.emit()
    return nc, hc


W_IN_PERM = None


def _perm_cols():
    o = {"pool": 0, "q": 256, "kcmp": 768, "vcmp": 896, "kslc": 1024, "vslc": 1152, "kwin": 1280,
         "vwin": 1408, "gate": 1536, "cb": 1560, "cc": 1816, "cx": 2072}
    order = [("q", 512), ("kslc", 128), ("kwin", 128), ("kcmp", 128), ("vcmp", 128), ("vslc", 128),
             ("vwin", 128), ("gate", 24), ("pool", 256), ("cb", 256), ("cc", 256), ("cx", 256)]
    idx = np.concatenate([np.arange(o[n], o[n] + w) for n, w in order])
    assert idx.size == 2328
    return idx


def prep_weights(inp, NL):
    f = lambda a: np.ascontiguousarray(np.asarray(a, dtype=np.float32))
    w = {}
    w["w_in"] = f(np.asarray(inp["w_in"])[:NL][:, :, _perm_cols()])
    w["w_out"] = f(np.asarray(inp["w_out"])[:NL])
    w["ffn_up"] = f(np.asarray(inp["ffn_up"])[:NL])
    w["ffn_down"] = f(np.asarray(inp["ffn_down"])[:NL])
    w["cmp_w1"] = f(np.asarray(inp["cmp_w1"])[:NL])
    w["cmp_w2"] = f(np.asarray(inp["cmp_w2"])[:NL])
    w["cmp_posT"] = f(np.asarray(inp["cmp_pos"])[:NL].transpose(0, 1, 3, 2))
    w["pool_w"] = f(np.asarray(inp["pool_w"])[:NL])
    w["g1"] = f(np.asarray(inp["norm1_g"])[:NL].reshape(NL, 8, 128).transpose(0, 2, 1))
    w["g2"] = f(np.asarray(inp["norm2_g"])[:NL].reshape(NL, 8, 128).transpose(0, 2, 1))
    w["qg"] = f(np.asarray(inp["q_norm_g"])[:NL].reshape(NL, 64, 1))
    w["kg"] = f(np.asarray(inp["k_norm_g"])[:NL].transpose(0, 2, 1))
    w["pscale"] = f(np.asarray(inp["pool_scale"])[:NL].reshape(NL, 2, 128).transpose(0, 2, 1))
    w["sconv"] = f(np.asarray(inp["sconv_w"])[:NL].reshape(NL, 3, 2, 128).transpose(0, 3, 2, 1))
    w["fconv"] = f(np.asarray(inp["ffn_conv"])[:NL].reshape(NL, 3, NFC, 128).transpose(0, 3, 2, 1))
    return w


_CACHE = {}


def run_model(inp, T, NSEQ, NL, n_cores, dbg=None, upto=99):
    key = (T, NSEQ, NL, tuple(sorted(dbg.items())) if dbg else None, upto)
    if key not in _CACHE:
        _CACHE[key] = build_program(T, NSEQ, NL, dbg, upto)
    nc, hc = _CACHE[key]
    w = prep_weights(inp, NL)
    x = np.asarray(inp["x"], dtype=np.float32)
    in_maps = []
    for c in range(n_cores):
        m = dict(w)
        m.update(hc)
        m["xT"] = np.ascontiguousarray(x[c * NSEQ:(c + 1) * NSEQ].transpose(0, 2, 1))
        in_maps.append(m)
    res = run_bass_kernel_spmd(nc, in_maps, core_ids=list(range(n_cores)))
    outs = [r["yT"].transpose(0, 2, 1) for r in res.results]
    return np.ascontiguousarray(np.concatenate(outs, axis=0)), res


def kernel(**inputs):
    out, _ = run_model(inputs, 2048, 2, 4, 8)
    return out.astype(np.float32)
```

```python
import contextlib
import os
import numpy as np
import concourse.bass as bass
import concourse.mybir as mybir
from concourse.bass_utils import run_bass_kernel_spmd

F32 = mybir.dt.float32
BF16 = mybir.dt.bfloat16
ALU = mybir.AluOpType
AF = mybir.ActivationFunctionType

ENGS = ("pe", "act", "dve", "pool", "sp")
NDMA = 8
ATTACH_WAIT = os.environ.get("ATTACH_WAIT", "1") == "1"


class Prog:
    def __init__(self, nc):
        self.nc = nc
        self.q = {e: [] for e in ENGS}
        self.cnt = {e: 0 for e in ENGS}
        self.dcnt = {e: 0 for e in ENGS}
        self.seen = {e: {} for e in ENGS}
        self.lastw = {}
        self.readers = {}

    def _need(self, eng, tok, waits):
        if tok is None:
            return
        kind, e2, v = tok
        if kind == "c":
            if e2 == eng and eng == "pe":
                return
            key = ("c", e2)
            val = v
        else:
            key = ("d", e2, v % NDMA)
            val = v // NDMA + 1
        if self.seen[eng].get(key, 0) >= val:
            return
        self.seen[eng][key] = val
        waits.append(tok)

    def _deps(self, eng, reads, writes):
        waits = []
        for b in reads:
            self._need(eng, self.lastw.get(b), waits)
        for b in writes:
            self._need(eng, self.lastw.get(b), waits)
            r = self.readers.get(b)
            if r:
                for t in r.values():
                    self._need(eng, t, waits)
        return waits

    def _commit(self, tok, reads, writes):
        kind, e, v = tok
        rk = (kind, e) if kind == "c" else (kind, e, v % NDMA)
        for b in reads:
            self.readers.setdefault(b, {})[rk] = tok
        for b in writes:
            self.lastw[b] = tok
            self.readers[b] = {}

    def op(self, eng, fn, reads=(), writes=()):
        waits = self._deps(eng, reads, writes)
        self.cnt[eng] += 1
        tok = ("c", eng, self.cnt[eng])
        self.q[eng].append(("c", fn, waits, None))
        self._commit(tok, reads, writes)
        return tok

    def dma(self, eng, fn, reads=(), writes=()):
        waits = self._deps(eng, reads, writes)
        j = self.dcnt[eng]
        self.dcnt[eng] += 1
        if j >= NDMA:
            self._need(eng, ("d", eng, j - NDMA), waits)
        tok = ("d", eng, j)
        self.q[eng].append(("d", fn, waits, j))
        self._commit(tok, reads, writes)
        return tok

    def _all_tokens(self):
        toks = []
        for e in ENGS:
            if self.cnt[e]:
                toks.append(("c", e, self.cnt[e]))
            n = self.dcnt[e]
            for s in range(min(NDMA, n)):
                j = n - 1
                while j % NDMA != s:
                    j -= 1
                toks.append(("d", e, j))
        return toks

    def barrier(self):
        toks = self._all_tokens()
        for e in ENGS:
            waits = []
            for t in toks:
                self._need(e, t, waits)
            if waits:
                self.q[e].append(("w", None, waits, None))

    def emit(self):
        nc = self.nc
        toks = [t for t in self._all_tokens() if t[0] == "d"]
        for e in ENGS:
            waits = []
            for t in toks:
                self._need(e, t, waits)
            if waits:
                self.q[e].append(("w", None, waits, None))
        sig = {e: set() for e in ENGS}
        for e in ENGS:
            for (_, _, waits, _) in self.q[e]:
                for (k2, e2, v) in waits:
                    if k2 == "c":
                        sig[e2].add(v)
        rank = {e: {p: i + 1 for i, p in enumerate(sorted(sig[e]))} for e in ENGS}
        with contextlib.ExitStack() as st:
            sems = {e: st.enter_context(nc.semaphore("s_" + e)) for e in ENGS}
            dsems = {e: [st.enter_context(nc.semaphore("d_%s%d" % (e, i))) for i in range(NDMA)]
                     for e in ENGS}
            block = st.enter_context(nc.Block())
            hooks = {"pe": block.tensor, "act": block.scalar, "dve": block.vector,
                     "pool": block.gpsimd, "sp": block.sync}
            for e in ENGS:
                def body(engine, e=e):
                    pos = 0
                    for kind, fn, waits, j in self.q[e]:
                        ws = []
                        for (k2, e2, v) in waits:
                            if k2 == "c":
                                ws.append((sems[e2], rank[e2][v]))
                            else:
                                ws.append((dsems[e2][v % NDMA], 16 * (v // NDMA + 1)))
                        attach = None
                        if kind == "c" and ws and ATTACH_WAIT:
                            attach = ws.pop()
                        for (sm, vv) in ws:
                            engine.wait_ge(sm, vv)
                        if kind == "c":
                            pos += 1
                            ins = fn(engine)
                            if attach is not None:
                                ins._wait_ge(attach[0], attach[1])
                            if pos in rank[e]:
                                ins.then_inc(sems[e], 1)
                        elif kind == "d":
                            fn(engine).then_inc(dsems[e][j % NDMA], 16)
                hooks[e](body)


D = 1024
KC = 8
DFF = 2816
NFC = 22
NH = 8
SLOPES = [2.0 ** (-(h + 1)) for h in range(8)]
NEG = -30000.0
C_Q, C_KSLC, C_KWIN, C_KCMP, C_VCMP, C_VSLC, C_VWIN, C_GATE, C_POOL, C_CB, C_CC, C_CX = (
    0, 512, 640, 768, 896, 1024, 1152, 1280, 1304, 1560, 1816, 2072)
SB_BASE = 16512
SB_END = 229376


def host_consts(T):
    NT = T // 128
    NCMP = (T - 32) // 16 + 1
    t = np.arange(T)
    c = {}
    qa = np.zeros((4, 8, T), np.float32)
    for h in range(8):
        s = SLOPES[h]
        qa[0, h] = -s * 128.0 * (t // 128)
        qa[1, h] = -s * (t % 128)
        qa[2, h] = s
        qa[3, h] = s
    c["c_qal"] = qa
    kw = np.zeros((36, T), np.float32)
    kw[32] = 1.0
    kw[33] = 1.0
    kw[34] = 128.0 * (t // 128)
    kw[35] = t % 128
    ks = kw.copy()
    for j in range(T // 64):
        ks[j, j * 64:(j + 1) * 64] = 1.0
    c["c_kwin"] = kw
    c["c_kslc"] = ks
    n = np.arange(NCMP)
    kc = np.zeros((36, NCMP), np.float32)
    kc[32] = 1.0
    kc[33] = 1.0
    kc[34] = 16.0 * n
    kc[35] = 31.0
    c["c_kcmp"] = kc
    mc = np.where((16 * n[:, None] + 31) <= t[None, :], 0.0, NEG).astype(np.float32)
    c["c_maskc"] = mc
    jj = np.arange(128)[:, None]
    ii = np.arange(128)[None, :]
    c["c_cmd"] = np.where(jj <= ii, 0.0, NEG).astype(np.float32)
    c["c_cmw"] = np.where(ii < jj, 0.0, NEG).astype(np.float32)
    c["c_ident"] = np.eye(128, dtype=np.float32)
    NS = T // 64
    c0 = n * 16
    s0 = np.arange(NS) * 64
    ov = np.minimum(c0[:, None] + 32, s0[None, :] + 64) - np.maximum(c0[:, None], s0[None, :])
    M = np.zeros((NCMP, 33), np.float32)
    M[:, :32][:, :NS] = np.clip(ov, 0, None) / 32.0
    M[:, 32] = 1.0
    c["c_maug"] = M
    blk = np.arange(32)[None, :]
    cur = (t // 64)[:, None]
    forced = (blk == 0) | (blk == cur) | (blk == cur - 1)
    causal = blk <= cur
    keep = (causal & ~forced).astype(np.float32)
    add = np.where(forced, 1e6, np.where(causal, 0.0, -1e6)).astype(np.float32)
    c["c_fkeep"] = keep.reshape(NT, 128, 32).transpose(1, 0, 2).copy()
    c["c_fadd"] = add.reshape(NT, 128, 32).transpose(1, 0, 2).copy()
    selb = np.zeros((25, 3, 64), np.float32)
    selb[0, :, :] = -1.0
    for b in range(3):
        selb[1 + b * 8:1 + (b + 1) * 8, b, :] = -1.0
    c["c_selb"] = selb
    oh = np.zeros((25, 2, 4), np.float32)
    for r in range(24):
        hh = r % 8
        oh[1 + r, hh // 4, hh % 4] = 1.0
    c["c_oh"] = oh
    fix = np.ones((128, 2, 16), np.float32)
    wins = (2, 4, 8, 16)
    for ch in range(2):
        for half in range(2):
            w = wins[2 * ch + half]
            tt = np.arange(16)
            fix[half * 64:(half + 1) * 64, ch, :] = w / np.minimum(tt + 1, w)
    c["c_pfix"] = fix
    return c


CONST_SHAPES = None


def build_program(T, NSEQ, NL, dbg=None, upto=99):
    NT = T // 128
    NTC = T // 512
    NCMP = (T - 32) // 16 + 1
    nc = bass.Bass("TRN2", target_bir_lowering=False)
    P = Prog(nc)

    def din(name, shape):
        return nc.dram_tensor(name, list(shape), F32, kind="ExternalInput").ap()

    xT_d = din("xT", [NSEQ, D, T])
    yT_d = nc.dram_tensor("yT", [NSEQ, D, T], F32, kind="ExternalOutput").ap()
    w_in_d = din("w_in", [NL, D, 2328])
    w_out_d = din("w_out", [NL, D, D])
    ffn_up_d = din("ffn_up", [NL, D, 2 * DFF])
    ffn_down_d = din("ffn_down", [NL, DFF, D])
    cmp_w1_d = din("cmp_w1", [NL, 2, 2048, 64])
    cmp_w2_d = din("cmp_w2", [NL, 2, 64, 64])
    cmp_posT_d = din("cmp_posT", [NL, 2, 64, 32])
    pool_w_d = din("pool_w", [NL, 4, 64, 64])
    g1_d = din("g1", [NL, 128, 8])
    g2_d = din("g2", [NL, 128, 8])
    qg_d = din("qg", [NL, 64, 1])
    kg_d = din("kg", [NL, 64, 3])
    pscale_d = din("pscale", [NL, 128, 2])
    sconv_d = din("sconv", [NL, 128, 2, 3])
    fconv_d = din("fconv", [NL, 128, NFC, 3])
    hc = host_consts(T)
    cd = {k: din(k, v.shape) for k, v in hc.items()}
    dbg_d = {}
    if dbg:
        for k, shp in dbg.items():
            dbg_d[k] = nc.dram_tensor("dbg_" + k, list(shp), F32, kind="ExternalOutput").ap()

    cur = [SB_BASE]

    def alloc(name, shape, dt, at=None):
        per = int(np.prod(shape[1:])) * (4 if dt == F32 else 2)
        per = (per + 31) // 32 * 32
        if at is None:
            off = cur[0]
            cur[0] += per
        else:
            off = at
        assert off + per <= SB_END, (name, off, per)
        return nc.alloc_sbuf_tensor_at(name, list(shape), dt, offset=off), off, per

    def A(name, shape, dt):
        return alloc(name, shape, dt)[0]

    xT = A("xT_sb", [128, KC, T], F32)
    hT = A("hT", [128, KC, T], BF16)
    ident = A("ident", [128, 128], BF16)
    zeros_b = A("zeros_b", [128, 128], BF16)
    ones_f = A("ones_f", [128, 128], F32)
    negones = A("negones", [128, 64], F32)
    cmd = A("cmd", [128, 128], BF16)
    cmw = A("cmw", [128, 128], BF16)
    maug = A("maug", [128, 33], BF16)
    maskc = A("maskc", [128, T], BF16)
    fkeep = A("fkeep", [128, NT, 32], F32)
    fadd = A("fadd", [128, NT, 32], F32)
    selb = A("selb", [128, 3, 64], F32)
    oh = A("oh", [128, 2, 4], F32)
    pfix = A("pfix", [128, 2, 16], F32)
    eps6 = A("eps6", [128, 1], F32)
    tiny = A("tiny", [128, 1], F32)
    onec = A("onec", [128, 1], F32)
    g1 = A("g1s", [128, 8], F32)
    g2 = A("g2s", [128, 8], F32)
    qg = A("qgs", [64, 1], F32)
    qg8 = A("qg8", [64, 1], F32)
    kg = A("kgs", [64, 3], F32)
    pscale = A("pscales", [128, 2], F32)
    sconv = A("sconvs", [128, 2, 3], F32)
    fconv = A("fconvs", [128, NFC, 3], F32)
    W2 = A("W2", [64, 2, 64], BF16)
    posT = A("posT", [64, 2, 32], BF16)
    PWblk = A("PWblk", [128, 2, 128], BF16)
    cb = A("cb", [64, 2], F32)
    kcA = A("kcA", [128, 2, NCMP], BF16)
    vcA = A("vcA", [128, 2, 65], BF16)
    MB = A("MB", [128, 96], BF16)
    small = A("small", [128, 64], F32)
    imp_t = A("imp_t", [128, 4, 32], F32)
    phase_base = cur[0]
    qA = A("qA", [128, NH, T], BF16)
    kA = A("kA", [128, 2, 2, T], BF16)
    Vaug = A("Vaug", [128, NT, 2, 2, 65], BF16)
    lngate = A("lngate", [128, T], F32)
    local_base = cur[0]
    def region(base):
        st = [base]

        def R(name, shape, dt):
            t_, off, per = alloc(name, shape, dt, at=st[0])
            st[0] += per
            return t_
        return R, st

    R2, st2 = region(phase_base)
    wb_pc = R2("wb_pc", [128, KC, 1024], BF16)
    WoPC = R2("WoPC", [128, 4, D], BF16)
    yPC = [R2("yPC0", [128, 4, 512], BF16), R2("yPC1", [128, 4, 512], BF16)]
    ubuf = R2("ubuf", [128, 2, 528], F32)
    sA = [R2("sA0", [128, 528], F32), R2("sA1", [128, 528], F32)]
    sB = [R2("sB0", [128, 528], F32), R2("sB1", [128, 528], F32)]
    dT = [R2("dT0", [128, 512], BF16), R2("dT1", [128, 512], BF16)]
    cxs = [R2("cxs0", [128, 512], F32), R2("cxs1", [128, 512], F32)]
    cbs = [R2("cbs0", [128, 512], F32), R2("cbs1", [128, 512], F32)]
    pbuf = R2("pbuf", [128, 2, 514], F32)
    ybuf = [R2("ybuf0", [128, 512], F32), R2("ybuf1", [128, 512], F32)]
    sq_n = R2("sq_n", [128, 2, 512], F32)
    rstd_n = R2("rstd_n", [128, 512], F32)
    R1, st1 = region(local_base)
    wbuf = [R1("wbuf0", [128, KC, 512], BF16), R1("wbuf1", [128, KC, 512], BF16)]
    W1 = R1("W1", [64, 32, 64], BF16)
    sq1 = R1("sq1", [128, 512], F32)
    sq2 = R1("sq2", [64, 512], F32)
    rstd1 = R1("rstd1", [64, 512], F32)
    rawT = R1("rawT", [64, T], BF16)
    gtmp = R1("gtmp", [64, 4, 128], F32)
    hidg = R1("hidg", [64, 128], BF16)
    RT, stT = region(local_base)
    Pt = [RT("Pt%d" % i, [128, 512], BF16) for i in range(7)]
    comb = [RT("comb%d" % i, [128, 512], F32) for i in range(4)]
    bc_sb = [RT("bc_sb0", [64, 512], F32), RT("bc_sb1", [64, 512], F32)]
    acc = [RT("acc%d" % i, [64, 512], F32) for i in range(4)]
    tmpo = RT("tmpo", [64, 512], F32)
    WoN = [RT("WoN0", [64, NH, 128], BF16), RT("WoN1", [64, NH, 128], BF16)]
    RF, stF = region(phase_base)
    mT = RF("mT", [128, 11, T], BF16)
    abuf = RF("abuf", [128, T + 2], F32)
    yb = [RF("yb0", [128, 512], F32), RF("yb1", [128, 512], F32)]
    sil = [RF("sil0", [128, 512], F32), RF("sil1", [128, 512], F32)]
    wup = [RF("wup0", [128, KC, 256], BF16), RF("wup1", [128, KC, 256], BF16)]
    wdn = [RF("wdn0", [128, 11, 128], BF16), RF("wdn1", [128, 11, 128], BF16)]
    sq_f = RF("sq_f", [128, 2, 512], F32)
    rstd_f = RF("rstd_f", [128, 512], F32)
    for s_ in (st2, st1, stT, stF):
        assert s_[0] <= SB_END, s_

    ps = [nc.alloc_psum_tensor("bank%d" % i, [128, 512], F32) for i in range(8)]
    rr = {}

    def rot(key, banks):
        i = rr.get(key, 0)
        rr[key] = i + 1
        return banks[i % len(banks)]

    def PS(i):
        return "ps%d" % i

    def mm(out, lhsT, rhs, start, stop, reads, writes):
        P.op("pe", lambda e: e.matmul(out, lhsT=lhsT, rhs=rhs, start=start, stop=stop), reads, writes)

    def act(out, in_, func, reads, writes, scale=None, bias=None):
        kw = {}
        if scale is not None:
            kw["scale"] = scale
        if bias is not None:
            kw["bias"] = bias
        P.op("act", lambda e: e.activation(out=out, in_=in_, func=func, **kw), reads, writes)

    def tt(out, in0, in1, op, reads, writes, eng="dve"):
        P.op(eng, lambda e: e.tensor_tensor(out=out, in0=in0, in1=in1, op=op), reads, writes)

    def ts(out, in0, s1, op0, reads, writes, s2=None, op1=None, eng="dve"):
        if op1 is None:
            P.op(eng, lambda e: e.tensor_scalar(out=out, in0=in0, scalar1=s1, scalar2=None, op0=op0), reads, writes)
        else:
            P.op(eng, lambda e: e.tensor_scalar(out=out, in0=in0, scalar1=s1, scalar2=s2, op0=op0, op1=op1), reads, writes)

    def stt(out, in0, scalar, in1, op0, op1, reads, writes):
        P.op("dve", lambda e: e.scalar_tensor_tensor(out=out, in0=in0, scalar=scalar, in1=in1, op0=op0, op1=op1),
             reads, writes)

    def cp(out, in_, reads, writes, eng="dve"):
        P.op(eng, lambda e: e.tensor_copy(out=out, in_=in_), reads, writes)

    def dma(eng, out, in_, reads, writes):
        P.dma(eng, lambda e: e.dma_start(out=out, in_=in_), reads, writes)

    def bcast_mid(ap2, n):
        p, f = ap2.shape
        return ap2.unsqueeze(1).to_broadcast([p, n, f])

    def load_consts():
        dma("pool", ident[:], cd["c_ident"], [], ["ident"])
        dma("pool", cmd[:], cd["c_cmd"], [], ["cmd"])
        dma("pool", cmw[:], cd["c_cmw"], [], ["cmw"])
        dma("pool", maug[0:NCMP, :], cd["c_maug"], [], ["maug"])
        dma("pool", maskc[0:NCMP, :], cd["c_maskc"], [], ["maskc"])
        dma("sp", fkeep[:], cd["c_fkeep"], [], ["fkeep"])
        dma("sp", fadd[:], cd["c_fadd"], [], ["fadd"])
        dma("sp", selb[64:89], cd["c_selb"], [], ["selb"])
        dma("sp", oh[64:89], cd["c_oh"], [], ["oh"])
        dma("sp", pfix[:], cd["c_pfix"], [], ["pfix"])
        P.op("dve", lambda e: e.memset(ones_f[:], 1.0), [], ["ones_f"])
        P.op("dve", lambda e: e.memset(negones[:], -1.0), [], ["negones"])
        P.op("dve", lambda e: e.memset(eps6[:], 1e-6), [], ["eps6"])
        P.op("dve", lambda e: e.memset(tiny[:], 1e-18), [], ["tiny"])
        P.op("dve", lambda e: e.memset(onec[:], 1.0), [], ["onec"])
        P.op("dve", lambda e: e.memset(MB[:], 0.0), [], ["MB"])
        P.op("dve", lambda e: e.memset(zeros_b[:], 0.0), [], ["zeros_b"])
        P.op("pool", lambda e: e.memset(vcA[:], 1.0), [], ["vcA"])

    def init_attn_consts():
        P.op("pool", lambda e: e.memset(qA[64:96, :, :], 0.0), [], ["qA_m"])
        dma("pool", qA[96:100, :, :], cd["c_qal"], [], ["qA_al"])
        for g in range(2):
            dma("pool", kA[64:100, 0, g, :], cd["c_kslc"], [], ["kA_c"])
            dma("pool", kA[64:100, 1, g, :], cd["c_kwin"], [], ["kA_c"])
        P.op("pool", lambda e: e.memset(Vaug[:, :, :, :, 64:65], 1.0), [], ["Vaug_1"])

    def load_layer_small(l):
        dma("sp", g1[:], g1_d[l], [], ["g1"])
        dma("sp", g2[:], g2_d[l], [], ["g2"])
        dma("sp", qg[:], qg_d[l], [], ["qg"])
        dma("sp", kg[:], kg_d[l], [], ["kg"])
        dma("sp", pscale[:], pscale_d[l], [], ["pscale"])
        dma("sp", sconv[:], sconv_d[l], [], ["sconv"])
        dma("sp", fconv[:], fconv_d[l], [], ["fconv"])
        dma("pool", W2[:], cmp_w2_d[l].rearrange("k e f -> e k f"), [], ["W2"])
        dma("pool", posT[:], cmp_posT_d[l].rearrange("k d l -> d k l"), [], ["posT"])
        P.op("pool", lambda e: e.memset(PWblk[:], 0.0), [], ["PWblk"])
        for gi in range(4):
            c_, hf = gi // 2, gi % 2
            dma("pool", PWblk[hf * 64:(hf + 1) * 64, c_, hf * 64:(hf + 1) * 64], pool_w_d[l, gi], [], ["PWblk"])
        ts(qg8[:], qg[:], 0.125, ALU.mult, ["qg"], ["qg8"])
        dma("pool", kcA[64:100, 0, :], cd["c_kcmp"], [], ["kcA_c"])
        dma("pool", kcA[64:100, 1, :], cd["c_kcmp"], [], ["kcA_c"])

    def rmsnorm_to_hT(gvec, gid, sq, rstd, tag):
        for tc in range(NTC):
            cs = slice(tc * 512, (tc + 1) * 512)
            for k in range(KC):
                sqk = sq[:, k % 2, :]
                act(sqk, xT[:, k, cs], AF.Square, [("xT", k, tc)], [(tag + "sq", k % 2)])
                mm(ps[7][:, :], ones_f[:, :], sqk, k == 0, k == KC - 1,
                   ["ones_f", (tag + "sq", k % 2)], [PS(7)])
            act(rstd[:, :], ps[7][:, :], AF.Ln, [PS(7), "eps6"], [tag + "rstd"], scale=1.0 / D, bias=eps6[:, 0:1])
            act(rstd[:, :], rstd[:, :], AF.Exp, [tag + "rstd"], [tag + "rstd"], scale=-0.5)
            for k in range(KC):
                stt(hT[:, k, cs], xT[:, k, cs], gvec[:, k:k + 1], rstd[:, :], ALU.mult, ALU.mult,
                    [("xT", k, tc), tag + "rstd", gid], [("hT", k, tc)])

    def hT_reads(tc):
        return [("hT", k, tc) for k in range(KC)]

    def phase_pool_conv(l):
        wv = w_in_d[l].rearrange("(k p) c -> p k c", p=128)
        dma("pool", wb_pc[:, :, 0:512], wv[:, :, C_POOL:C_POOL + 512], [], ["wb_pc0"])
        dma("pool", wb_pc[:, :, 512:1024], wv[:, :, C_POOL + 512:C_POOL + 1024], [], ["wb_pc1"])
        wo = w_out_d[l].rearrange("(j p) c -> p j c", p=128)
        dma("pool", WoPC[:, 0:2, :], wo[:, 0:2, :], [], ["WoPC"])
        dma("pool", WoPC[:, 2:4, :], wo[:, 6:8, :], [], ["WoPC"])
        wins = (2, 4, 8, 16)
        RB = [0, 1, 2, 3, 4, 5]

        def proj(col0, cs, tc):
            b_ = rot("p2r", RB)
            for k in range(KC):
                mm(ps[b_][:, :], wb_pc[:, k, col0:col0 + 128], hT[:, k, cs], k == 0, k == KC - 1,
                   ["wb_pc0", "wb_pc1", ("hT", k, tc)], [PS(b_)])
            return b_

        def wout_pc(tc):
            cs = slice(tc * 512, (tc + 1) * 512)
            yp_ = yPC[tc % 2]
            for dc in range(KC):
                ob = rot("p2o", [6, 7])
                for j in range(4):
                    mm(ps[ob][:, :], WoPC[:, j, dc * 128:(dc + 1) * 128], yp_[:, j, :], j == 0, j == 3,
                       ["WoPC", ("yPC", tc % 2, j)], [PS(ob)])
                tt(xT[:, dc, cs], xT[:, dc, cs], ps[ob][:, :], ALU.add, [("xT", dc, tc), PS(ob)], [("xT", dc, tc)])

        for tc in range(NTC + 1):
            if tc < NTC:
                cs = slice(tc * 512, (tc + 1) * 512)
                yp_ = yPC[tc % 2]
                for c_ in range(2):
                    ub = ubuf[:, c_, :]
                    pbc = pbuf[:, c_, :]
                    if tc == 0:
                        P.op("dve", lambda e, ub=ub: e.memset(ub[:, 0:16], 0.0), [], [("ubuf", c_)])
                        P.op("dve", lambda e, pbc=pbc: e.memset(pbc[:, 0:2], 0.0), [], [("pbuf", c_)])
                    else:
                        cp(ub[:, 0:16], ub[:, 512:528], [("ubuf", c_)], [("ubuf", c_)])
                        cp(pbc[:, 0:2], pbc[:, 512:514], [("pbuf", c_)], [("pbuf", c_)])
                ccb = {}
                for c_ in range(2):
                    b_ = proj(c_ * 128, cs, tc)
                    act(ubuf[:, c_, 16:528], ps[b_][:, :], AF.Copy, [PS(b_)], [("ubuf", c_)])
                for c_ in range(2):
                    b_ = proj(C_CX - C_POOL + c_ * 128, cs, tc)
                    act(cxs[c_][:, :], ps[b_][:, :], AF.Copy, [PS(b_)], [("cxs", c_)])
                    ccb[c_] = proj(C_CC - C_POOL + c_ * 128, cs, tc)
                    b_ = proj(C_CB - C_POOL + c_ * 128, cs, tc)
                    act(cbs[c_][:, :], ps[b_][:, :], AF.Copy, [PS(b_)], [("cbs", c_)])
                for c_ in range(2):
                    pbc = pbuf[:, c_, :]
                    tt(pbc[:, 2:514], ps[ccb[c_]][:, :], cxs[c_][:, :], ALU.mult, [PS(ccb[c_]), ("cxs", c_)], [("pbuf", c_)])
                for c_ in range(2):
                    ub = ubuf[:, c_, :]
                    sA_, sB_ = sA[c_], sB[c_]
                    nA, nB = ("sA", c_), ("sB", c_)
                    tt(sA_[:, 1:528], ub[:, 1:528], ub[:, 0:527], ALU.add, [("ubuf", c_)], [nA])
                    tt(sB_[:, 3:528], sA_[:, 3:528], sA_[:, 1:526], ALU.add, [nA], [nB])
                    if c_ == 1:
                        tt(sA_[:, 7:528], sB_[:, 7:528], sB_[:, 3:524], ALU.add, [nB], [nA])
                        tt(sB_[:, 15:528], sA_[:, 15:528], sA_[:, 7:520], ALU.add, [nA], [nB])
                    if tc == 0:
                        tt(sA_[0:64, 16:32], sA_[0:64, 16:32], pfix[0:64, c_, :], ALU.mult, [nA, "pfix"], [nA])
                        tt(sB_[64:128, 16:32], sB_[64:128, 16:32], pfix[64:128, c_, :], ALU.mult, [nB, "pfix"], [nB])
                    stt(dT[c_][0:64, :], sA_[0:64, 16:528], 1.0 / wins[2 * c_], ub[0:64, 16:528], ALU.mult, ALU.subtract,
                        [nA, ("ubuf", c_)], [("dT", c_)])
                    stt(dT[c_][64:128, :], sB_[64:128, 16:528], 1.0 / wins[2 * c_ + 1], ub[64:128, 16:528], ALU.mult,
                        ALU.subtract, [nB, ("ubuf", c_)], [("dT", c_)])
                for c_ in range(2):
                    pbc = pbuf[:, c_, :]
                    yb_ = ybuf[c_]
                    yid = ("ybuf", c_)
                    ts(yb_[:, :], pbc[:, 0:512], sconv[:, c_, 0:1], ALU.mult, [("pbuf", c_), "sconv"], [yid])
                    stt(yb_[:, :], pbc[:, 1:513], sconv[:, c_, 1:2], yb_[:, :], ALU.mult, ALU.add,
                        [("pbuf", c_), "sconv", yid], [yid])
                    stt(yb_[:, :], pbc[:, 2:514], sconv[:, c_, 2:3], yb_[:, :], ALU.mult, ALU.add,
                        [("pbuf", c_), "sconv", yid], [yid])
                    tt(yp_[:, 2 + c_, :], yb_[:, :], cbs[c_][:, :], ALU.mult, [yid, ("cbs", c_)], [("yPC", tc % 2, 2 + c_)])
            if tc >= 1:
                wout_pc(tc - 1)
            if tc < NTC:
                for c_ in range(2):
                    b_ = rot("p2r", RB)
                    mm(ps[b_][:, :], PWblk[:, c_, :], dT[c_][:, :], True, True, ["PWblk", ("dT", c_)], [PS(b_)])
                    act(yPC[tc % 2][:, c_, :], ps[b_][:, :], AF.Copy, [PS(b_), "pscale"], [("yPC", tc % 2, c_)],
                        scale=pscale[:, c_:c_ + 1])

    hn_pending = []

    def headnorm_flush():
        while hn_pending:
            pb, sqb, sqid, gain_ap, gain_id, out_ap, out_id = hn_pending.pop(0)
            sb_ = rot("p1s", [3, 4])
            mm(ps[sb_][0:64, :], ones_f[0:64, 0:64], sqb, True, True, ["ones_f", sqid], [PS(sb_)])
            act(rstd1[:, :], ps[sb_][0:64, :], AF.Ln, [PS(sb_), "eps6"], ["rstd1"], scale=1.0 / 64, bias=eps6[0:64, 0:1])
            act(rstd1[:, :], rstd1[:, :], AF.Exp, ["rstd1"], ["rstd1"], scale=-0.5)
            stt(out_ap, ps[pb][0:64, :], gain_ap, rstd1[:, :], ALU.mult, ALU.mult, [PS(pb), "rstd1", gain_id], [out_id])

    def headnorm_store(pb, gain_ap, gain_id, out_ap, out_id):
        i_ = rot("p1q", [0, 1])
        sqb = sq1[0:64, :] if i_ == 0 else sq2[:, :]
        sqid = ("sqh", i_)
        act(sqb, ps[pb][0:64, :], AF.Square, [PS(pb)], [sqid])
        hn_pending.append((pb, sqb, sqid, gain_ap, gain_id, out_ap, out_id))

    def phase_proj(l):
        wv = w_in_d[l].rearrange("(k p) c -> p k c", p=128)
        dma("pool", wbuf[0][:, :, 0:512], wv[:, :, C_Q:C_Q + 512], [], ["wbuf0"])
        dma("pool", wbuf[1][:, :, 0:512], wv[:, :, C_KSLC:C_KSLC + 512], [], ["wbuf1"])
        for h in range(NH):
            for tc in range(NTC):
                cs = slice(tc * 512, (tc + 1) * 512)
                pb = rot("p1a", [0, 1, 2])
                for k in range(KC):
                    mm(ps[pb][0:64, :], wbuf[0][:, k, h * 64:(h + 1) * 64], hT[:, k, cs], k == 0, k == KC - 1,
                       ["wbuf0", ("hT", k, tc)], [PS(pb)])
                headnorm_flush()
                headnorm_store(pb, qg8[:, 0:1], "qg8", qA[0:64, h, cs], ("qA", h, tc))
        dma("pool", wbuf[0][:, :, 0:280], wv[:, :, C_VSLC:C_VSLC + 280], [], ["wbuf0"])
        for br in range(2):
            for g in range(2):
                co = br * 128 + g * 64
                for tc in range(NTC):
                    cs = slice(tc * 512, (tc + 1) * 512)
                    pb = rot("p1a", [0, 1, 2])
                    for k in range(KC):
                        mm(ps[pb][0:64, :], wbuf[1][:, k, co:co + 64], hT[:, k, cs], k == 0, k == KC - 1,
                           ["wbuf1", ("hT", k, tc)], [PS(pb)])
                    headnorm_flush()
                    headnorm_store(pb, kg[:, 1 + br:2 + br], "kg", kA[0:64, br, g, cs], ("kA", br, g, tc))
        headnorm_flush()
        for kv in range(2):
            dma("pool", W1[:, :, :], cmp_w1_d[l, kv].rearrange("(l d) e -> d l e", d=64), [], ["W1"])
            for li in range(32):
                mm(ps[6][0:64, 0:1], W1[:, li, :], posT[:, kv, li:li + 1], li == 0, li == 31, ["W1", "posT"], [PS(6)])
            cp(cb[:, kv:kv + 1], ps[6][0:64, 0:1], [PS(6)], ["cb"])
            for g in range(2):
                co = 256 + kv * 128 + g * 64
                for tc in range(NTC):
                    cs = slice(tc * 512, (tc + 1) * 512)
                    pb = rot("p1a", [0, 1])
                    for k in range(KC):
                        mm(ps[pb][0:64, :], wbuf[1][:, k, co:co + 64], hT[:, k, cs], k == 0, k == KC - 1,
                           ["wbuf1", ("hT", k, tc)], [PS(pb)])
                    act(rawT[:, cs], ps[pb][0:64, :], AF.Copy, [PS(pb)], ["rawT"])
                for li in range(32):
                    rhs = rawT[:, li:li + 16 * (NCMP - 1) + 1:16]
                    mm(ps[6][0:64, 0:NCMP], W1[:, li, :], rhs, li == 0, li == 31, ["W1", "rawT"], [PS(6)])
                x_ = gtmp[:, 0, 0:NCMP]
                x2 = gtmp[:, 1, 0:NCMP]
                z_ = gtmp[:, 2, 0:NCMP]
                e_ = gtmp[:, 3, 0:NCMP]
                act(x_, ps[6][0:64, 0:NCMP], AF.Identity, [PS(6), "cb"], ["g_x"], bias=cb[:, kv:kv + 1])
                tt(x2, x_, x_, ALU.mult, ["g_x"], ["g_x2"])
                ts(x2, x2, 0.044715, ALU.mult, ["g_x2"], ["g_x2"], s2=1.0, op1=ALU.add)
                tt(z_, x2, x_, ALU.mult, ["g_x2", "g_x"], ["g_z"])
                act(e_, z_, AF.Exp, ["g_z"], ["g_e"], scale=-1.5957691216057308)
                ts(e_, e_, 1.0, ALU.add, ["g_e"], ["g_e"])
                P.op("dve", lambda e, e_=e_: e.reciprocal(out=e_, in_=e_), ["g_e"], ["g_e"])
                tt(hidg[:, 0:NCMP], x_, e_, ALU.mult, ["g_x", "g_e"], ["hidg"])
                if kv == 0:
                    mm(ps[7][0:64, 0:NCMP], W2[:, 0, :], hidg[:, 0:NCMP], True, True, ["W2", "hidg"], [PS(7)])
                    act(sq1[0:64, 0:NCMP], ps[7][0:64, 0:NCMP], AF.Square, [PS(7)], [("sqh", 0)])
                    sb_ = rot("p1s", [3, 4])
                    mm(ps[sb_][0:64, 0:NCMP], ones_f[0:64, 0:64], sq1[0:64, 0:NCMP], True, True, ["ones_f", ("sqh", 0)], [PS(sb_)])
                    act(rstd1[:, 0:NCMP], ps[sb_][0:64, 0:NCMP], AF.Ln, [PS(sb_), "eps6"], ["rstd1"], scale=1.0 / 64,
                        bias=eps6[0:64, 0:1])
                    act(rstd1[:, 0:NCMP], rstd1[:, 0:NCMP], AF.Exp, ["rstd1"], ["rstd1"], scale=-0.5)
                    stt(kcA[0:64, g, :], ps[7][0:64, 0:NCMP], kg[:, 0:1], rstd1[:, 0:NCMP], ALU.mult, ALU.mult,
                        [PS(7), "rstd1", "kg"], [("kcA", g)])
                else:
                    mm(ps[7][0:NCMP, 0:64], hidg[:, 0:NCMP], W2[:, 1, :], True, True, ["W2", "hidg"], [PS(7)])
                    cp(vcA[0:NCMP, g, 0:64], ps[7][0:NCMP, 0:64], [PS(7)], [("vcA", g)])
        for ti in range(NT):
            tsl = slice(ti * 128, (ti + 1) * 128)
            vb = rot("p1v", [4, 5])
            for br in range(2):
                for k in range(KC):
                    mm(ps[vb][:, br * 128:(br + 1) * 128], hT[:, k, tsl], wbuf[0][:, k, br * 128:(br + 1) * 128],
                       k == 0, k == KC - 1, ["wbuf0", ("hT", k, ti // 4)], [PS(vb)])
            cp(Vaug[:, ti, :, :, 0:64], ps[vb][:, 0:256].rearrange("p (b g d) -> p b g d", b=2, g=2),
               [PS(vb)], [("Vaug", ti)])
        for tc in range(NTC):
            cs = slice(tc * 512, (tc + 1) * 512)
            pb = rot("p1a", [0, 1])
            for k in range(KC):
                mm(ps[pb][0:89, :], wbuf[0][:, k, 191:280], hT[:, k, cs], k == 0, k == KC - 1,
                   ["wbuf0", ("hT", k, tc)], [PS(pb)])
            act(sq1[64:89, :], ps[pb][64:89, :], AF.Exp, [PS(pb)], ["sq1"], scale=-1.0)
            act(lngate[64:89, cs], sq1[64:89, :], AF.Ln, ["sq1", "onec"], [("lngate", tc)], scale=1.0, bias=onec[64:89, 0:1])

    def epi_ln(ob, cbuf):
        act(cbuf[64:65, :], ps[ob][64:65, :], AF.Ln, [PS(ob), "tiny"], [("lnD", id(cbuf))], scale=1.0, bias=tiny[64:65, 0:1])

    def epi_mm(b, cbuf, bk=6):
        mm(ps[bk][0:64, :], selb[64:89, b, :], cbuf[64:89, :], True, True,
           ["selb", ("lnD", id(cbuf)), ("GMq", id(cbuf))], [PS(bk)])

    def epi_fin(ob, b, g, qt, first, last, aj, bk=6):
        qs = slice(qt * 128, (qt + 1) * 128)
        bi = rot("bcs", [0, 1])
        bcs = bc_sb[bi]
        bid = ("bc_sb", bi)
        ac = acc[aj]
        aid = ("acc", aj)
        act(bcs[:, :], ps[bk][0:64, :], AF.Exp, [PS(bk)], [bid])
        o_out = hT[0:64, 4 * g:4 * g + 4, qs]
        o_id = [("hT", k, qt // 4) for k in range(4 * g, 4 * g + 4)]
        acc3 = ac[:, :].rearrange("p (h q) -> p h q", h=4)
        tmp3 = tmpo[:, :].rearrange("p (h q) -> p h q", h=4)
        if first:
            tt(ac[:, :], ps[ob][0:64, :], bcs[:, :], ALU.mult, [PS(ob), bid], [aid])
        else:
            tt(tmpo[:, :], ps[ob][0:64, :], bcs[:, :], ALU.mult, [PS(ob), bid], ["tmpo"])
            if last:
                tt(o_out, acc3, tmp3, ALU.add, [aid, "tmpo"], o_id)
            else:
                tt(ac[:, :], ac[:, :], tmpo[:, :], ALU.add, [aid, "tmpo"], [aid])

    def phase_attn(l):
        LAG = int(os.environ.get("LAG", "3"))
        SBK = [0, 1, 2, 6] if os.environ.get("SB4", "1") == "1" else [0, 1, 2]
        NPT = 7
        NB = 4
        NDUM = int(os.environ.get("NDUM", "0"))
        EPD1 = int(os.environ.get("EPD1", "2"))
        EPD2 = int(os.environ.get("EPD2", "3"))
        CHS = tuple(int(v) for v in os.environ.get("CHS", "0,3,4,6,7,10").split(","))
        DUMN = int(os.environ.get("DUMN", "128"))
        wo = w_out_d[l]
        tasks = []
        for r_ in range(NT // 2):
            tasks += [(0, NT - 1 - r_), (0, r_), (1, NT - 1 - r_), (1, r_)]
        done_tc = {}
        tiles = []
        tstart = []
        for j, (g, qt) in enumerate(tasks):
            tstart.append(len(tiles))
            for kt in range(qt + 1):
                tiles.append((j, 0, kt, kt == 0, kt == qt))
            k0 = max(0, qt - 4)
            for kt in range(k0, qt + 1):
                tiles.append((j, 1, kt, kt == k0, kt == qt))
        n = len(tiles)
        deferred = {}

        def at(step, fn):
            deferred.setdefault(step, []).append(fn)

        obank = {}
        eps = []

        def ep_new(ob, b, j, first, last, cbuf):
            e_ = {"ob": ob, "b": b, "j": j, "first": first, "last": last, "cbuf": cbuf, "stage": 0}
            eps.append(e_)
            return e_

        def ep_to(e_, target):
            while e_["stage"] < target:
                nxt = e_["stage"] + 1
                for p_ in eps:
                    if p_ is e_:
                        break
                    if nxt == 1 and p_["cbuf"] is e_["cbuf"]:
                        ep_to(p_, 2)
                    if nxt == 2:
                        ep_to(p_, 3)
                if nxt == 1:
                    epi_ln(e_["ob"], e_["cbuf"])
                elif nxt == 2:
                    e_["bk"] = rot("ts", SBK) if len(SBK) == 4 else 6
                    epi_mm(e_["b"], e_["cbuf"], e_["bk"])
                else:
                    g_, qt_ = tasks[e_["j"]]
                    epi_fin(e_["ob"], e_["b"], g_, qt_, e_["first"], e_["last"], e_["j"] % NB, e_["bk"])
                    owners.pop(e_["ob"], None)
                    if e_["last"]:
                        done_tc[qt_ // 4] = done_tc.get(qt_ // 4, 0) + 1
                        if done_tc[qt_ // 4] == 8:
                            sched_wout(qt_ // 4)
                e_["stage"] = nxt
            while eps and eps[0]["stage"] == 3:
                eps.pop(0)

        owners = {}

        def take_obank(hold=True):
            while True:
                for _ in range(3):
                    ob = rot("to", [3, 4, 5])
                    if ob not in owners:
                        if hold:
                            owners[ob] = True
                        return ob
                assert eps, "no free O bank and nothing to force"
                ep_to(eps[0], 3)

        cur_step = [0]

        def sched_wout(tc):
            cs = slice(tc * 512, (tc + 1) * 512)

            def piece(dc):
                def f():
                    wi = rot("won", [0, 1])
                    wb_ = WoN[wi]
                    wid = ("WoN", wi)
                    dma("pool", wb_[:, :, :],
                        wo[256:768, dc * 128:(dc + 1) * 128].rearrange("(h d) c -> d h c", d=64), [], [wid])
                    ob = take_obank(hold=False)
                    for h in range(NH):
                        mm(ps[ob][:, :], wb_[:, h, :], hT[0:64, h, cs], h == 0, h == NH - 1,
                           [wid, ("hT", h, tc)], [PS(ob)])
                    tt(xT[:, dc, cs], xT[:, dc, cs], ps[ob][:, :], ALU.add, [("xT", dc, tc), PS(ob)], [("xT", dc, tc)])
                return f
            for dc in range(KC):
                at(cur_step[0] + 2 + 2 * dc, piece(dc))

        def q_ops(g, qt):
            qs = slice(qt * 128, (qt + 1) * 128)
            q_rhs = qA[0:100, 4 * g:4 * g + 4, qs]
            q_ids = [("qA", h, qt // 4) for h in range(4 * g, 4 * g + 4)] + ["qA_al"]
            return qs, q_rhs, q_ids

        chains = {}

        def chain_run(j, k):
            c_ = chains[j]
            while c_["done"] <= k:
                c_["fns"][c_["done"]]()
                c_["done"] += 1

        def cmp_chain(j):
            g, qt = tasks[j]
            qs, q_rhs, q_ids = q_ops(g, qt)
            cbuf = comb[j % NB]
            st_ = {}

            def A_():
                for e2_ in list(eps):
                    if e2_["cbuf"] is cbuf:
                        ep_to(e2_, 3)
                tt(cbuf[64:89, :].rearrange("p (h q) -> p h q", h=4), bcast_mid(lngate[64:89, qs], 4),
                   oh[64:89, g, :].unsqueeze(2).to_broadcast([25, 4, 128]), ALU.mult,
                   [("lngate", qt // 4), "oh"], [("GMq", id(cbuf)), ("lnD", id(cbuf))])
                sb_ = rot("ts", SBK)
                mm(ps[sb_][0:NCMP, :], kcA[0:100, g, :], q_rhs, True, False, q_ids + [("kcA", g), "kcA_c"], [PS(sb_)])
                mm(ps[sb_][0:NCMP, :], ident[0:NCMP, 0:NCMP], bcast_mid(maskc[0:NCMP, qs], 4), False, True,
                   ["ident", "maskc"], [PS(sb_)])
                pi = rot("tp", list(range(NPT)))
                act(Pt[pi][0:NCMP, :], ps[sb_][0:NCMP, :], AF.Exp, [PS(sb_)], [("Pt", pi)])
                st_["pi"] = pi

            def B_():
                if j - 1 in chains:
                    chain_run(j - 1, 5)
                pi = st_["pi"]
                ob = take_obank()
                st_["ob"] = ob
                mm(ps[ob][0:65, :], vcA[0:NCMP, g, :], Pt[pi][0:NCMP, :], True, True,
                   [("vcA", g), "vcA", ("Pt", pi)], [PS(ob)])
                for hh in range(4):
                    mm(ps[7][:, hh * 33:(hh + 1) * 33], Pt[pi][0:NCMP, hh * 128:(hh + 1) * 128], maug[0:NCMP, :],
                       True, True, [("Pt", pi), "maug"], [PS(7)])
                U3 = ps[7][:, 0:132].rearrange("p (h c) -> p h c", c=33)
                den = small[:, 0:4]
                ts(den, U3[:, :, 32], 1e-18, ALU.add, [PS(7)], ["den"])
                P.op("dve", lambda e, den=den: e.reciprocal(out=den, in_=den), ["den"], ["den"])
                tt(imp_t[:, :, :], U3[:, :, 0:32], den.unsqueeze(2).to_broadcast([128, 4, 32]), ALU.mult,
                   [PS(7), "den"], ["imp_t"])
                tt(imp_t[:, 0:2, :], imp_t[:, 0:2, :], imp_t[:, 2:4, :], ALU.add, ["imp_t"], ["imp_t"])
                imp = small[:, 8:40]
                tt(imp, imp_t[:, 0, :], imp_t[:, 1, :], ALU.add, ["imp_t"], ["imp"])
                tt(imp, imp, fkeep[:, qt, :], ALU.mult, ["imp", "fkeep"], ["imp"])
                tt(imp, imp, fadd[:, qt, :], ALU.add, ["imp", "fadd"], ["imp"])
                m8 = small[:, 40:48]
                P.op("dve", lambda e, m8=m8, imp=imp: e.max(out=m8, in_=imp), ["imp"], ["m8"])
                ts(imp, imp, m8[:, 7:8], ALU.is_ge, ["imp", "m8"], ["imp"], s2=1.0, op1=ALU.subtract)
                ts(MB[:, 64:96], imp, -NEG, ALU.mult, ["imp"], ["MB"])

            def C1_():
                st_["ep"] = ep_new(st_["ob"], 0, j, True, False, cbuf)
                ep_to(st_["ep"], 1)

            def C2_():
                ep_to(st_["ep"], 2)

            def C3_():
                ep_to(st_["ep"], 3)

            def D_():
                mm(ps[7][0:96, 256:384], MB[:, 0:96], ident[:, :], True, True, ["MB", "ident"], [PS(7)])
                act(qA[64:96, 4 * g:4 * g + 4, qs], bcast_mid(ps[7][64:96, 256:384], 4), AF.Copy, [PS(7)],
                    [("qAm", g, qt)])

            chains[j] = {"fns": [A_, B_, C1_, C2_, C3_, D_], "done": 0}
            if j < 2:
                chain_run(j, 5)
            else:
                s0 = tstart[j - 2]
                lim = tstart[j] - 1
                for k_, dstep in enumerate(CHS):
                    at(min(s0 + dstep, lim), lambda k_=k_: chain_run(j, k_))

        info = {}

        def emit_qk(i):
            j, br, kt, first, last = tiles[i]
            g, qt = tasks[j]
            qs, q_rhs, q_ids = q_ops(g, qt)
            ks_ = slice(kt * 128, (kt + 1) * 128)
            sb_ = rot("ts", SBK)
            diag = kt == qt
            edge = (br == 1) and (kt == qt - 4)
            extra = [("qAm", g, qt)] if br == 0 else []
            mm(ps[sb_][:, :], kA[0:100, br, g, ks_], q_rhs, True, not (diag or edge),
               q_ids + extra + [("kA", br, g, kt // 4), "kA_c"], [PS(sb_)])
            if diag:
                mm(ps[sb_][:, :], ident[:, :], bcast_mid(cmd[:, :], 4), False, True, ["ident", "cmd"], [PS(sb_)])
            if edge:
                mm(ps[sb_][:, :], ident[:, :], bcast_mid(cmw[:, :], 4), False, True, ["ident", "cmw"], [PS(sb_)])
            pi = rot("tp", list(range(NPT)))
            act(Pt[pi][:, :], ps[sb_][:, :], AF.Exp, [PS(sb_)], [("Pt", pi)])
            info[i] = pi

        def emit_pv(i, step):
            j, br, kt, first, last = tiles[i]
            g, qt = tasks[j]
            pi = info.pop(i)
            if first:
                obank[(j, br)] = take_obank()
            ob = obank[(j, br)]
            mm(ps[ob][0:65, :], Vaug[:, kt, br, g, :], Pt[pi][:, :], first, last,
               [("Vaug", kt), "Vaug_1", ("Pt", pi)], [PS(ob)])
            if not last:
                for _d in range(NDUM):
                    mm(ps[ob][0:65, 0:DUMN], zeros_b[:, 0:65], ident[:, 0:DUMN], False, False, ["zeros_b", "ident"], [PS(ob)])
            if last:
                cbuf = comb[j % NB]
                ep_ = ep_new(ob, 1 + br, j, False, br == 1, cbuf)
                ep_to(ep_, 1)
                at(step + EPD1, lambda: ep_to(ep_, 2))
                at(step + EPD2, lambda: ep_to(ep_, 3))

        for j in range(len(tasks)):
            cmp_chain(j)
        i = 0
        while i < n + LAG or deferred:
            cur_step[0] = i
            if i < n:
                emit_qk(i)
            if 0 <= i - LAG < n:
                emit_pv(i - LAG, i)
            for fn in deferred.pop(i, []):
                fn()
            i += 1
        for e_ in list(eps):
            ep_to(e_, 3)
        while deferred:
            k_ = min(deferred)
            for fn in deferred.pop(k_):
                fn()

    def phase_ffn(l):
        up = ffn_up_d[l].rearrange("(k p) c -> p k c", p=128)
        dn = ffn_down_d[l]
        P.op("dve", lambda e: e.memset(abuf[:, 0:2], 0.0), [], ["abuf_h"])
        for half in range(2):
            for fi in range(11):
                fc = half * 11 + fi
                wb = wup[fc % 2]
                wid = ("wup", fc % 2)
                dma("pool", wb[:, :, 0:128], up[:, :, fc * 128:(fc + 1) * 128], [], [wid])
                dma("pool", wb[:, :, 128:256], up[:, :, DFF + fc * 128:DFF + (fc + 1) * 128], [], [wid])
                for tc in range(NTC):
                    cs = slice(tc * 512, (tc + 1) * 512)
                    ab = rot("fa", [0, 1])
                    gb = rot("fg", [2, 3])
                    for k in range(KC):
                        mm(ps[ab][:, :], wb[:, k, 0:128], hT[:, k, cs], k == 0, k == KC - 1, [wid, ("hT", k, tc)], [PS(ab)])
                    for k in range(KC):
                        mm(ps[gb][:, :], wb[:, k, 128:256], hT[:, k, cs], k == 0, k == KC - 1, [wid, ("hT", k, tc)], [PS(gb)])
                    act(abuf[:, 2 + tc * 512:2 + (tc + 1) * 512], ps[ab][:, :], AF.Copy, [PS(ab)], [("abuf", tc)])
                    rd = [("abuf", tc), "abuf_h", "fconv"] + ([("abuf", tc - 1)] if tc else [])
                    y_ = yb[tc % 2]
                    yid = ("yb", tc % 2)
                    o = tc * 512
                    act(y_[:, :], abuf[:, o:o + 512], AF.Copy, rd, [yid], scale=fconv[:, fc, 0:1])
                    stt(y_[:, :], abuf[:, o + 1:o + 513], fconv[:, fc, 1:2], y_[:, :], ALU.mult, ALU.add, rd + [yid], [yid])
                    stt(y_[:, :], abuf[:, o + 2:o + 514], fconv[:, fc, 2:3], y_[:, :], ALU.mult, ALU.add, rd + [yid], [yid])
                    s_ = sil[tc % 2]
                    sid = ("sil", tc % 2)
                    act(s_[:, :], y_[:, :], AF.Silu, [yid], [sid])
                    tt(mT[:, fi, cs], s_[:, :], ps[gb][:, :], ALU.mult, [sid, PS(gb)], [("mT", fi, tc)])
            for dc in range(KC):
                wd = wdn[dc % 2]
                wdid = ("wdn", dc % 2)
                dma("pool", wd[:, :, :],
                    dn[half * 1408:(half + 1) * 1408, dc * 128:(dc + 1) * 128].rearrange("(f p) c -> p f c", p=128),
                    [], [wdid])
                for tc in range(NTC):
                    cs = slice(tc * 512, (tc + 1) * 512)
                    ob = rot("fd", [4, 5])
                    for fi in range(11):
                        mm(ps[ob][:, :], wd[:, fi, :], mT[:, fi, cs], fi == 0, fi == 10, [wdid, ("mT", fi, tc)], [PS(ob)])
                    tt(xT[:, dc, cs], xT[:, dc, cs], ps[ob][:, :], ALU.add, [("xT", dc, tc), PS(ob)], [("xT", dc, tc)])

    load_consts()
    for s in range(NSEQ):
        xv = xT_d[s].rearrange("(k p) t -> p k t", p=128)
        for k in range(KC):
            dma("sp", xT[:, k, :], xv[:, k, :], [], [("xT", k, tc) for tc in range(NTC)])
        for l in range(NL):
            if upto >= 1:
                load_layer_small(l)
            if upto >= 2:
                rmsnorm_to_hT(g1, "g1", sq_n, rstd_n, "n1")
            if upto >= 3:
                phase_pool_conv(l)
            P.barrier()
            if upto >= 4:
                init_attn_consts()
            if upto >= 5:
                phase_proj(l)
            P.barrier()
            if upto >= 6:
                phase_attn(l)
            P.barrier()
            if upto >= 8:
                rmsnorm_to_hT(g2, "g2", sq_f, rstd_f, "n2")
            if upto >= 9:
                phase_ffn(l)
            P.barrier()
        yv = yT_d[s].rearrange("(k p) t -> p k t", p=128)
        for k in range(KC):
            dma("sp", yv[:, k, :], xT[:, k, :], [("xT", k, tc) for tc in range(NTC)], [])
    P.emit()
    return nc, hc


W_IN_PERM = None


def _perm_cols():
    o = {"pool": 0, "q": 256, "kcmp": 768, "vcmp": 896, "kslc": 1024, "vslc": 1152, "kwin": 1280,
         "vwin": 1408, "gate": 1536, "cb": 1560, "cc": 1816, "cx": 2072}
    order = [("q", 512), ("kslc", 128), ("kwin", 128), ("kcmp", 128), ("vcmp", 128), ("vslc", 128),
             ("vwin", 128), ("gate", 24), ("pool", 256), ("cb", 256), ("cc", 256), ("cx", 256)]
    idx = np.concatenate([np.arange(o[n], o[n] + w) for n, w in order])
    assert idx.size == 2328
    return idx


def prep_weights(inp, NL):
    f = lambda a: np.ascontiguousarray(np.asarray(a, dtype=np.float32))
    w = {}
    w["w_in"] = f(np.asarray(inp["w_in"])[:NL][:, :, _perm_cols()])
    w["w_out"] = f(np.asarray(inp["w_out"])[:NL])
    w["ffn_up"] = f(np.asarray(inp["ffn_up"])[:NL])
    w["ffn_down"] = f(np.asarray(inp["ffn_down"])[:NL])
    w["cmp_w1"] = f(np.asarray(inp["cmp_w1"])[:NL])
    w["cmp_w2"] = f(np.asarray(inp["cmp_w2"])[:NL])
    w["cmp_posT"] = f(np.asarray(inp["cmp_pos"])[:NL].transpose(0, 1, 3, 2))
    w["pool_w"] = f(np.asarray(inp["pool_w"])[:NL])
    w["g1"] = f(np.asarray(inp["norm1_g"])[:NL].reshape(NL, 8, 128).transpose(0, 2, 1))
    w["g2"] = f(np.asarray(inp["norm2_g"])[:NL].reshape(NL, 8, 128).transpose(0, 2, 1))
    w["qg"] = f(np.asarray(inp["q_norm_g"])[:NL].reshape(NL, 64, 1))
    w["kg"] = f(np.asarray(inp["k_norm_g"])[:NL].transpose(0, 2, 1))
    w["pscale"] = f(np.asarray(inp["pool_scale"])[:NL].reshape(NL, 2, 128).transpose(0, 2, 1))
    w["sconv"] = f(np.asarray(inp["sconv_w"])[:NL].reshape(NL, 3, 2, 128).transpose(0, 3, 2, 1))
    w["fconv"] = f(np.asarray(inp["ffn_conv"])[:NL].reshape(NL, 3, NFC, 128).transpose(0, 3, 2, 1))
    return w


_CACHE = {}


def run_model(inp, T, NSEQ, NL, n_cores, dbg=None, upto=99):
    key = (T, NSEQ, NL, tuple(sorted(dbg.items())) if dbg else None, upto)
    if key not in _CACHE:
        _CACHE[key] = build_program(T, NSEQ, NL, dbg, upto)
    nc, hc = _CACHE[key]
    w = prep_weights(inp, NL)
    x = np.asarray(inp["x"], dtype=np.float32)
    in_maps = []
    for c in range(n_cores):
        m = dict(w)
        m.update(hc)
        m["xT"] = np.ascontiguousarray(x[c * NSEQ:(c + 1) * NSEQ].transpose(0, 2, 1))
        in_maps.append(m)
    res = run_bass_kernel_spmd(nc, in_maps, core_ids=list(range(n_cores)))
    outs = [r["yT"].transpose(0, 2, 1) for r in res.results]
    return np.ascontiguousarray(np.concatenate(outs, axis=0)), res


def kernel(**inputs):
    out, _ = run_model(inputs, 2048, 2, 4, 8)
    return out.astype(np.float32)
```

```python
import contextlib
import os
import numpy as np
import concourse.bass as bass
import concourse.mybir as mybir
from concourse.bass_utils import run_bass_kernel_spmd

F32 = mybir.dt.float32
BF16 = mybir.dt.bfloat16
ALU = mybir.AluOpType
AF = mybir.ActivationFunctionType

ENGS = ("pe", "act", "dve", "pool", "sp")
NDMA = 8
ATTACH_WAIT = os.environ.get("ATTACH_WAIT", "1") == "1"


class Prog:
    def __init__(self, nc):
        self.nc = nc
        self.q = {e: [] for e in ENGS}
        self.cnt = {e: 0 for e in ENGS}
        self.dcnt = {e: 0 for e in ENGS}
        self.seen = {e: {} for e in ENGS}
        self.lastw = {}
        self.readers = {}

    def _need(self, eng, tok, waits):
        if tok is None:
            return
        kind, e2, v = tok
        if kind == "c":
            if e2 == eng and eng == "pe":
                return
            key = ("c", e2)
            val = v
        else:
            key = ("d", e2, v % NDMA)
            val = v // NDMA + 1
        if self.seen[eng].get(key, 0) >= val:
            return
        self.seen[eng][key] = val
        waits.append(tok)

    def _deps(self, eng, reads, writes):
        waits = []
        for b in reads:
            self._need(eng, self.lastw.get(b), waits)
        for b in writes:
            self._need(eng, self.lastw.get(b), waits)
            r = self.readers.get(b)
            if r:
                for t in r.values():
                    self._need(eng, t, waits)
        return waits

    def _commit(self, tok, reads, writes):
        kind, e, v = tok
        rk = (kind, e) if kind == "c" else (kind, e, v % NDMA)
        for b in reads:
            self.readers.setdefault(b, {})[rk] = tok
        for b in writes:
            self.lastw[b] = tok
            self.readers[b] = {}

    def op(self, eng, fn, reads=(), writes=()):
        waits = self._deps(eng, reads, writes)
        self.cnt[eng] += 1
        tok = ("c", eng, self.cnt[eng])
        self.q[eng].append(("c", fn, waits, None))
        self._commit(tok, reads, writes)
        return tok

    def dma(self, eng, fn, reads=(), writes=()):
        waits = self._deps(eng, reads, writes)
        j = self.dcnt[eng]
        self.dcnt[eng] += 1
        if j >= NDMA:
            self._need(eng, ("d", eng, j - NDMA), waits)
        tok = ("d", eng, j)
        self.q[eng].append(("d", fn, waits, j))
        self._commit(tok, reads, writes)
        return tok

    def _all_tokens(self):
        toks = []
        for e in ENGS:
            if self.cnt[e]:
                toks.append(("c", e, self.cnt[e]))
            n = self.dcnt[e]
            for s in range(min(NDMA, n)):
                j = n - 1
                while j % NDMA != s:
                    j -= 1
                toks.append(("d", e, j))
        return toks

    def barrier(self):
        toks = self._all_tokens()
        for e in ENGS:
            waits = []
            for t in toks:
                self._need(e, t, waits)
            if waits:
                self.q[e].append(("w", None, waits, None))

    def emit(self):
        nc = self.nc
        toks = [t for t in self._all_tokens() if t[0] == "d"]
        for e in ENGS:
            waits = []
            for t in toks:
                self._need(e, t, waits)
            if waits:
                self.q[e].append(("w", None, waits, None))
        sig = {e: set() for e in ENGS}
        for e in ENGS:
            for (_, _, waits, _) in self.q[e]:
                for (k2, e2, v) in waits:
                    if k2 == "c":
                        sig[e2].add(v)
        rank = {e: {p: i + 1 for i, p in enumerate(sorted(sig[e]))} for e in ENGS}
        with contextlib.ExitStack() as st:
            sems = {e: st.enter_context(nc.semaphore("s_" + e)) for e in ENGS}
            dsems = {e: [st.enter_context(nc.semaphore("d_%s%d" % (e, i))) for i in range(NDMA)]
                     for e in ENGS}
            block = st.enter_context(nc.Block())
            hooks = {"pe": block.tensor, "act": block.scalar, "dve": block.vector,
                     "pool": block.gpsimd, "sp": block.sync}
            for e in ENGS:
                def body(engine, e=e):
                    pos = 0
                    for kind, fn, waits, j in self.q[e]:
                        ws = []
                        for (k2, e2, v) in waits:
                            if k2 == "c":
                                ws.append((sems[e2], rank[e2][v]))
                            else:
                                ws.append((dsems[e2][v % NDMA], 16 * (v // NDMA + 1)))
                        attach = None
                        if kind == "c" and ws and ATTACH_WAIT:
                            attach = ws.pop()
                        for (sm, vv) in ws:
                            engine.wait_ge(sm, vv)
                        if kind == "c":
                            pos += 1
                            ins = fn(engine)
                            if attach is not None:
                                ins._wait_ge(attach[0], attach[1])
                            if pos in rank[e]:
                                ins.then_inc(sems[e], 1)
                        elif kind == "d":
                            fn(engine).then_inc(dsems[e][j % NDMA], 16)
                hooks[e](body)


D = 1024
KC = 8
DFF = 2816
NFC = 22
NH = 8
SLOPES = [2.0 ** (-(h + 1)) for h in range(8)]
NEG = -30000.0
C_Q, C_KSLC, C_KWIN, C_KCMP, C_VCMP, C_VSLC, C_VWIN, C_GATE, C_POOL, C_CB, C_CC, C_CX = (
    0, 512, 640, 768, 896, 1024, 1152, 1280, 1304, 1560, 1816, 2072)
SB_BASE = 16512
SB_END = 229376


def host_consts(T):
    NT = T // 128
    NCMP = (T - 32) // 16 + 1
    t = np.arange(T)
    c = {}
    qa = np.zeros((4, 8, T), np.float32)
    for h in range(8):
        s = SLOPES[h]
        qa[0, h] = -s * 128.0 * (t // 128)
        qa[1, h] = -s * (t % 128)
        qa[2, h] = s
        qa[3, h] = s
    c["c_qal"] = qa
    kw = np.zeros((36, T), np.float32)
    kw[32] = 1.0
    kw[33] = 1.0
    kw[34] = 128.0 * (t // 128)
    kw[35] = t % 128
    ks = kw.copy()
    for j in range(T // 64):
        ks[j, j * 64:(j + 1) * 64] = 1.0
    c["c_kwin"] = kw
    c["c_kslc"] = ks
    n = np.arange(NCMP)
    kc = np.zeros((36, NCMP), np.float32)
    kc[32] = 1.0
    kc[33] = 1.0
    kc[34] = 16.0 * n
    kc[35] = 31.0
    c["c_kcmp"] = kc
    mc = np.where((16 * n[:, None] + 31) <= t[None, :], 0.0, NEG).astype(np.float32)
    c["c_maskc"] = mc
    jj = np.arange(128)[:, None]
    ii = np.arange(128)[None, :]
    c["c_cmd"] = np.where(jj <= ii, 0.0, NEG).astype(np.float32)
    c["c_cmw"] = np.where(ii < jj, 0.0, NEG).astype(np.float32)
    c["c_ident"] = np.eye(128, dtype=np.float32)
    NS = T // 64
    c0 = n * 16
    s0 = np.arange(NS) * 64
    ov = np.minimum(c0[:, None] + 32, s0[None, :] + 64) - np.maximum(c0[:, None], s0[None, :])
    M = np.zeros((NCMP, 33), np.float32)
    M[:, :32][:, :NS] = np.clip(ov, 0, None) / 32.0
    M[:, 32] = 1.0
    c["c_maug"] = M
    blk = np.arange(32)[None, :]
    cur = (t // 64)[:, None]
    forced = (blk == 0) | (blk == cur) | (blk == cur - 1)
    causal = blk <= cur
    keep = (causal & ~forced).astype(np.float32)
    add = np.where(forced, 1e6, np.where(causal, 0.0, -1e6)).astype(np.float32)
    c["c_fkeep"] = keep.reshape(NT, 128, 32).transpose(1, 0, 2).copy()
    c["c_fadd"] = add.reshape(NT, 128, 32).transpose(1, 0, 2).copy()
    selb = np.zeros((25, 3, 64), np.float32)
    selb[0, :, :] = -1.0
    for b in range(3):
        selb[1 + b * 8:1 + (b + 1) * 8, b, :] = -1.0
    c["c_selb"] = selb
    oh = np.zeros((25, 2, 4), np.float32)
    for r in range(24):
        hh = r % 8
        oh[1 + r, hh // 4, hh % 4] = 1.0
    c["c_oh"] = oh
    fix = np.ones((128, 2, 16), np.float32)
    wins = (2, 4, 8, 16)
    for ch in range(2):
        for half in range(2):
            w = wins[2 * ch + half]
            tt = np.arange(16)
            fix[half * 64:(half + 1) * 64, ch, :] = w / np.minimum(tt + 1, w)
    c["c_pfix"] = fix
    return c


CONST_SHAPES = None


def build_program(T, NSEQ, NL, dbg=None, upto=99):
    NT = T // 128
    NTC = T // 512
    NCMP = (T - 32) // 16 + 1
    nc = bass.Bass("TRN2", target_bir_lowering=False)
    P = Prog(nc)

    def din(name, shape):
        return nc.dram_tensor(name, list(shape), F32, kind="ExternalInput").ap()

    xT_d = din("xT", [NSEQ, D, T])
    yT_d = nc.dram_tensor("yT", [NSEQ, D, T], F32, kind="ExternalOutput").ap()
    w_in_d = din("w_in", [NL, D, 2328])
    w_out_d = din("w_out", [NL, D, D])
    ffn_up_d = din("ffn_up", [NL, D, 2 * DFF])
    ffn_down_d = din("ffn_down", [NL, DFF, D])
    cmp_w1_d = din("cmp_w1", [NL, 2, 2048, 64])
    cmp_w2_d = din("cmp_w2", [NL, 2, 64, 64])
    cmp_posT_d = din("cmp_posT", [NL, 2, 64, 32])
    pool_w_d = din("pool_w", [NL, 4, 64, 64])
    g1_d = din("g1", [NL, 128, 8])
    g2_d = din("g2", [NL, 128, 8])
    qg_d = din("qg", [NL, 64, 1])
    kg_d = din("kg", [NL, 64, 3])
    pscale_d = din("pscale", [NL, 128, 2])
    sconv_d = din("sconv", [NL, 128, 2, 3])
    fconv_d = din("fconv", [NL, 128, NFC, 3])
    hc = host_consts(T)
    cd = {k: din(k, v.shape) for k, v in hc.items()}
    dbg_d = {}
    if dbg:
        for k, shp in dbg.items():
            dbg_d[k] = nc.dram_tensor("dbg_" + k, list(shp), F32, kind="ExternalOutput").ap()

    cur = [SB_BASE]

    def alloc(name, shape, dt, at=None):
        per = int(np.prod(shape[1:])) * (4 if dt == F32 else 2)
        per = (per + 31) // 32 * 32
        if at is None:
            off = cur[0]
            cur[0] += per
        else:
            off = at
        assert off + per <= SB_END, (name, off, per)
        return nc.alloc_sbuf_tensor_at(name, list(shape), dt, offset=off), off, per

    def A(name, shape, dt):
        return alloc(name, shape, dt)[0]

    xT = A("xT_sb", [128, KC, T], F32)
    hT = A("hT", [128, KC, T], BF16)
    ident = A("ident", [128, 128], BF16)
    zeros_b = A("zeros_b", [128, 128], BF16)
    ones_f = A("ones_f", [128, 128], F32)
    negones = A("negones", [128, 64], F32)
    onesblk = A("onesblk", [128, 128], F32)
    cmd = A("cmd", [128, 128], BF16)
    cmw = A("cmw", [128, 128], BF16)
    maug = A("maug", [128, 33], BF16)
    maskc = A("maskc", [128, T], BF16)
    fkeep = A("fkeep", [128, NT, 32], F32)
    fadd = A("fadd", [128, NT, 32], F32)
    selb = A("selb", [128, 3, 64], F32)
    oh = A("oh", [128, 2, 4], F32)
    pfix = A("pfix", [128, 2, 16], F32)
    eps6 = A("eps6", [128, 1], F32)
    tiny = A("tiny", [128, 1], F32)
    onec = A("onec", [128, 1], F32)
    g1 = A("g1s", [128, 8], F32)
    g2 = A("g2s", [128, 8], F32)
    qg = A("qgs", [128, 1], F32)
    qg8 = A("qg8", [128, 1], F32)
    kg = A("kgs", [128, 3], F32)
    pscale = A("pscales", [128, 2], F32)
    sconv = A("sconvs", [128, 2, 3], F32)
    fconv = A("fconvs", [128, NFC, 3], F32)
    W2 = A("W2", [64, 2, 64], BF16)
    posT = A("posT", [64, 2, 32], BF16)
    PWblk = A("PWblk", [128, 2, 128], BF16)
    cb = A("cb", [64, 2], F32)
    kcA = A("kcA", [128, 2, NCMP], BF16)
    vcA = A("vcA", [128, 2, 65], BF16)
    MB = A("MB", [128, 96], BF16)
    small = A("small", [128, 64], F32)
    imp_t = A("imp_t", [128, 4, 32], F32)
    phase_base = cur[0]
    qA = A("qA", [128, NH, T], BF16)
    kA = A("kA", [128, 2, 2, T], BF16)
    Vaug = A("Vaug", [128, NT, 2, 2, 65], BF16)
    lngate = A("lngate", [128, T], F32)
    local_base = cur[0]
    def region(base):
        st = [base]

        def R(name, shape, dt):
            t_, off, per = alloc(name, shape, dt, at=st[0])
            st[0] += per
            return t_
        return R, st

    R2, st2 = region(phase_base)
    wb_pc = R2("wb_pc", [128, KC, 1024], BF16)
    WoPC = R2("WoPC", [128, 4, D], BF16)
    yPC = [R2("yPC0", [128, 4, 512], BF16), R2("yPC1", [128, 4, 512], BF16)]
    ubuf = R2("ubuf", [128, 2, 528], F32)
    sA = [R2("sA0", [128, 528], F32), R2("sA1", [128, 528], F32)]
    sB = [R2("sB0", [128, 528], F32), R2("sB1", [128, 528], F32)]
    dT = [R2("dT0", [128, 512], BF16), R2("dT1", [128, 512], BF16)]
    cxs = [R2("cxs0", [128, 512], F32), R2("cxs1", [128, 512], F32)]
    cbs = [R2("cbs0", [128, 512], F32), R2("cbs1", [128, 512], F32)]
    pbuf = R2("pbuf", [128, 2, 514], F32)
    ybuf = [R2("ybuf0", [128, 512], F32), R2("ybuf1", [128, 512], F32)]
    sq_n = R2("sq_n", [128, 2, 512], F32)
    rstd_n = R2("rstd_n", [128, 512], F32)
    R1, st1 = region(local_base)
    wbuf = [R1("wbuf0", [128, KC, 512], BF16), R1("wbuf1", [128, KC, 512], BF16)]
    W1 = R1("W1", [64, 32, 64], BF16)
    sq1 = R1("sq1", [128, 512], F32)
    sq2 = R1("sq2", [128, 512], F32)
    stg = [R1("stg0", [128, 512], BF16), R1("stg1", [128, 512], BF16)]
    rawT = R1("rawT", [64, T], BF16)
    gtmp = R1("gtmp", [64, 4, 128], F32)
    hidg = R1("hidg", [64, 128], BF16)
    RT, stT = region(local_base)
    Pt = [RT("Pt%d" % i, [128, 512], BF16) for i in range(7)]
    comb = [RT("comb%d" % i, [128, 512], F32) for i in range(4)]
    bc_sb = [RT("bc_sb0", [64, 512], F32), RT("bc_sb1", [64, 512], F32)]
    acc = [RT("acc%d" % i, [64, 512], F32) for i in range(4)]
    tmpo = RT("tmpo", [64, 512], F32)
    WoN = [RT("WoN0", [64, NH, 128], BF16), RT("WoN1", [64, NH, 128], BF16)]
    RF, stF = region(phase_base)
    mT = RF("mT", [128, 11, T], BF16)
    abuf = RF("abuf", [128, T + 2], F32)
    yb = [RF("yb0", [128, 512], F32), RF("yb1", [128, 512], F32)]
    sil = [RF("sil0", [128, 512], F32), RF("sil1", [128, 512], F32)]
    wup = [RF("wup0", [128, KC, 256], BF16), RF("wup1", [128, KC, 256], BF16)]
    wdn = [RF("wdn0", [128, 11, 128], BF16), RF("wdn1", [128, 11, 128], BF16)]
    sq_f = RF("sq_f", [128, 2, 512], F32)
    rstd_f = RF("rstd_f", [128, 512], F32)
    for s_ in (st2, st1, stT, stF):
        assert s_[0] <= SB_END, s_

    ps = [nc.alloc_psum_tensor("bank%d" % i, [128, 512], F32) for i in range(8)]
    rr = {}

    def rot(key, banks):
        i = rr.get(key, 0)
        rr[key] = i + 1
        return banks[i % len(banks)]

    def PS(i):
        return "ps%d" % i

    def mm(out, lhsT, rhs, start, stop, reads, writes):
        P.op("pe", lambda e: e.matmul(out, lhsT=lhsT, rhs=rhs, start=start, stop=stop), reads, writes)

    def act(out, in_, func, reads, writes, scale=None, bias=None):
        kw = {}
        if scale is not None:
            kw["scale"] = scale
        if bias is not None:
            kw["bias"] = bias
        P.op("act", lambda e: e.activation(out=out, in_=in_, func=func, **kw), reads, writes)

    def tt(out, in0, in1, op, reads, writes, eng="dve"):
        P.op(eng, lambda e: e.tensor_tensor(out=out, in0=in0, in1=in1, op=op), reads, writes)

    def ts(out, in0, s1, op0, reads, writes, s2=None, op1=None, eng="dve"):
        if op1 is None:
            P.op(eng, lambda e: e.tensor_scalar(out=out, in0=in0, scalar1=s1, scalar2=None, op0=op0), reads, writes)
        else:
            P.op(eng, lambda e: e.tensor_scalar(out=out, in0=in0, scalar1=s1, scalar2=s2, op0=op0, op1=op1), reads, writes)

    def stt(out, in0, scalar, in1, op0, op1, reads, writes):
        P.op("dve", lambda e: e.scalar_tensor_tensor(out=out, in0=in0, scalar=scalar, in1=in1, op0=op0, op1=op1),
             reads, writes)

    def cp(out, in_, reads, writes, eng="dve"):
        P.op(eng, lambda e: e.tensor_copy(out=out, in_=in_), reads, writes)

    def dma(eng, out, in_, reads, writes):
        P.dma(eng, lambda e: e.dma_start(out=out, in_=in_), reads, writes)

    def bcast_mid(ap2, n):
        p, f = ap2.shape
        return ap2.unsqueeze(1).to_broadcast([p, n, f])

    def load_consts():
        dma("pool", ident[:], cd["c_ident"], [], ["ident"])
        dma("pool", cmd[:], cd["c_cmd"], [], ["cmd"])
        dma("pool", cmw[:], cd["c_cmw"], [], ["cmw"])
        dma("pool", maug[0:NCMP, :], cd["c_maug"], [], ["maug"])
        dma("pool", maskc[0:NCMP, :], cd["c_maskc"], [], ["maskc"])
        dma("sp", fkeep[:], cd["c_fkeep"], [], ["fkeep"])
        dma("sp", fadd[:], cd["c_fadd"], [], ["fadd"])
        dma("sp", selb[64:89], cd["c_selb"], [], ["selb"])
        dma("sp", oh[64:89], cd["c_oh"], [], ["oh"])
        dma("sp", pfix[:], cd["c_pfix"], [], ["pfix"])
        P.op("dve", lambda e: e.memset(ones_f[:], 1.0), [], ["ones_f"])
        P.op("dve", lambda e: e.memset(negones[:], -1.0), [], ["negones"])
        P.op("dve", lambda e: e.memset(onesblk[:], 0.0), [], ["onesblk"])
        P.op("dve", lambda e: e.memset(onesblk[0:64, 0:64], 1.0), [], ["onesblk"])
        P.op("dve", lambda e: e.memset(onesblk[64:128, 64:128], 1.0), [], ["onesblk"])
        P.op("dve", lambda e: e.memset(eps6[:], 1e-6), [], ["eps6"])
        P.op("dve", lambda e: e.memset(tiny[:], 1e-18), [], ["tiny"])
        P.op("dve", lambda e: e.memset(onec[:], 1.0), [], ["onec"])
        P.op("dve", lambda e: e.memset(MB[:], 0.0), [], ["MB"])
        P.op("dve", lambda e: e.memset(zeros_b[:], 0.0), [], ["zeros_b"])
        P.op("pool", lambda e: e.memset(vcA[:], 1.0), [], ["vcA"])

    def init_attn_consts():
        P.op("pool", lambda e: e.memset(qA[64:96, :, :], 0.0), [], ["qA_m"])
        dma("pool", qA[96:100, :, :], cd["c_qal"], [], ["qA_al"])
        for g in range(2):
            dma("pool", kA[64:100, 0, g, :], cd["c_kslc"], [], ["kA_c"])
            dma("pool", kA[64:100, 1, g, :], cd["c_kwin"], [], ["kA_c"])
        P.op("pool", lambda e: e.memset(Vaug[:, :, :, :, 64:65], 1.0), [], ["Vaug_1"])

    def load_layer_small(l):
        dma("sp", g1[:], g1_d[l], [], ["g1"])
        dma("sp", g2[:], g2_d[l], [], ["g2"])
        dma("sp", qg[0:64], qg_d[l], [], ["qg"])
        dma("sp", qg[64:128], qg_d[l], [], ["qg"])
        dma("sp", kg[0:64], kg_d[l], [], ["kg"])
        dma("sp", kg[64:128], kg_d[l], [], ["kg"])
        dma("sp", pscale[:], pscale_d[l], [], ["pscale"])
        dma("sp", sconv[:], sconv_d[l], [], ["sconv"])
        dma("sp", fconv[:], fconv_d[l], [], ["fconv"])
        dma("pool", W2[:], cmp_w2_d[l].rearrange("k e f -> e k f"), [], ["W2"])
        dma("pool", posT[:], cmp_posT_d[l].rearrange("k d l -> d k l"), [], ["posT"])
        P.op("pool", lambda e: e.memset(PWblk[:], 0.0), [], ["PWblk"])
        for gi in range(4):
            c_, hf = gi // 2, gi % 2
            dma("pool", PWblk[hf * 64:(hf + 1) * 64, c_, hf * 64:(hf + 1) * 64], pool_w_d[l, gi], [], ["PWblk"])
        ts(qg8[:], qg[:], 0.125, ALU.mult, ["qg"], ["qg8"])
        dma("pool", kcA[64:100, 0, :], cd["c_kcmp"], [], ["kcA_c"])
        dma("pool", kcA[64:100, 1, :], cd["c_kcmp"], [], ["kcA_c"])

    def rmsnorm_to_hT(gvec, gid, sq, rstd, tag):
        for tc in range(NTC):
            cs = slice(tc * 512, (tc + 1) * 512)
            for k in range(KC):
                sqk = sq[:, k % 2, :]
                act(sqk, xT[:, k, cs], AF.Square, [("xT", k, tc)], [(tag + "sq", k % 2)])
                mm(ps[7][:, :], ones_f[:, :], sqk, k == 0, k == KC - 1,
                   ["ones_f", (tag + "sq", k % 2)], [PS(7)])
            act(rstd[:, :], ps[7][:, :], AF.Ln, [PS(7), "eps6"], [tag + "rstd"], scale=1.0 / D, bias=eps6[:, 0:1])
            act(rstd[:, :], rstd[:, :], AF.Exp, [tag + "rstd"], [tag + "rstd"], scale=-0.5)
            for k in range(KC):
                stt(hT[:, k, cs], xT[:, k, cs], gvec[:, k:k + 1], rstd[:, :], ALU.mult, ALU.mult,
                    [("xT", k, tc), tag + "rstd", gid], [("hT", k, tc)])

    def hT_reads(tc):
        return [("hT", k, tc) for k in range(KC)]

    def phase_pool_conv(l):
        wv = w_in_d[l].rearrange("(k p) c -> p k c", p=128)
        dma("pool", wb_pc[:, :, 0:512], wv[:, :, C_POOL:C_POOL + 512], [], ["wb_pc0"])
        dma("pool", wb_pc[:, :, 512:1024], wv[:, :, C_POOL + 512:C_POOL + 1024], [], ["wb_pc1"])
        wo = w_out_d[l].rearrange("(j p) c -> p j c", p=128)
        dma("pool", WoPC[:, 0:2, :], wo[:, 0:2, :], [], ["WoPC"])
        dma("pool", WoPC[:, 2:4, :], wo[:, 6:8, :], [], ["WoPC"])
        wins = (2, 4, 8, 16)
        RB = [0, 1, 2, 3, 4, 5]

        def proj(col0, cs, tc):
            b_ = rot("p2r", RB)
            for k in range(KC):
                mm(ps[b_][:, :], wb_pc[:, k, col0:col0 + 128], hT[:, k, cs], k == 0, k == KC - 1,
                   ["wb_pc0", "wb_pc1", ("hT", k, tc)], [PS(b_)])
            return b_

        def wout_pc(tc):
            cs = slice(tc * 512, (tc + 1) * 512)
            yp_ = yPC[tc % 2]
            for dc in range(KC):
                ob = rot("p2o", [6, 7])
                for j in range(4):
                    mm(ps[ob][:, :], WoPC[:, j, dc * 128:(dc + 1) * 128], yp_[:, j, :], j == 0, j == 3,
                       ["WoPC", ("yPC", tc % 2, j)], [PS(ob)])
                tt(xT[:, dc, cs], xT[:, dc, cs], ps[ob][:, :], ALU.add, [("xT", dc, tc), PS(ob)], [("xT", dc, tc)])

        for tc in range(NTC + 1):
            if tc < NTC:
                cs = slice(tc * 512, (tc + 1) * 512)
                yp_ = yPC[tc % 2]
                for c_ in range(2):
                    ub = ubuf[:, c_, :]
                    pbc = pbuf[:, c_, :]
                    if tc == 0:
                        P.op("dve", lambda e, ub=ub: e.memset(ub[:, 0:16], 0.0), [], [("ubuf", c_)])
                        P.op("dve", lambda e, pbc=pbc: e.memset(pbc[:, 0:2], 0.0), [], [("pbuf", c_)])
                    else:
                        cp(ub[:, 0:16], ub[:, 512:528], [("ubuf", c_)], [("ubuf", c_)])
                        cp(pbc[:, 0:2], pbc[:, 512:514], [("pbuf", c_)], [("pbuf", c_)])
                ccb = {}
                for c_ in range(2):
                    b_ = proj(c_ * 128, cs, tc)
                    act(ubuf[:, c_, 16:528], ps[b_][:, :], AF.Copy, [PS(b_)], [("ubuf", c_)])
                for c_ in range(2):
                    b_ = proj(C_CX - C_POOL + c_ * 128, cs, tc)
                    act(cxs[c_][:, :], ps[b_][:, :], AF.Copy, [PS(b_)], [("cxs", c_)])
                    ccb[c_] = proj(C_CC - C_POOL + c_ * 128, cs, tc)
                    b_ = proj(C_CB - C_POOL + c_ * 128, cs, tc)
                    act(cbs[c_][:, :], ps[b_][:, :], AF.Copy, [PS(b_)], [("cbs", c_)])
                for c_ in range(2):
                    pbc = pbuf[:, c_, :]
                    tt(pbc[:, 2:514], ps[ccb[c_]][:, :], cxs[c_][:, :], ALU.mult, [PS(ccb[c_]), ("cxs", c_)], [("pbuf", c_)])
                for c_ in range(2):
                    ub = ubuf[:, c_, :]
                    sA_, sB_ = sA[c_], sB[c_]
                    nA, nB = ("sA", c_), ("sB", c_)
                    tt(sA_[:, 1:528], ub[:, 1:528], ub[:, 0:527], ALU.add, [("ubuf", c_)], [nA])
                    tt(sB_[:, 3:528], sA_[:, 3:528], sA_[:, 1:526], ALU.add, [nA], [nB])
                    if c_ == 1:
                        tt(sA_[:, 7:528], sB_[:, 7:528], sB_[:, 3:524], ALU.add, [nB], [nA])
                        tt(sB_[:, 15:528], sA_[:, 15:528], sA_[:, 7:520], ALU.add, [nA], [nB])
                    if tc == 0:
                        tt(sA_[0:64, 16:32], sA_[0:64, 16:32], pfix[0:64, c_, :], ALU.mult, [nA, "pfix"], [nA])
                        tt(sB_[64:128, 16:32], sB_[64:128, 16:32], pfix[64:128, c_, :], ALU.mult, [nB, "pfix"], [nB])
                    stt(dT[c_][0:64, :], sA_[0:64, 16:528], 1.0 / wins[2 * c_], ub[0:64, 16:528], ALU.mult, ALU.subtract,
                        [nA, ("ubuf", c_)], [("dT", c_)])
                    stt(dT[c_][64:128, :], sB_[64:128, 16:528], 1.0 / wins[2 * c_ + 1], ub[64:128, 16:528], ALU.mult,
                        ALU.subtract, [nB, ("ubuf", c_)], [("dT", c_)])
                for c_ in range(2):
                    pbc = pbuf[:, c_, :]
                    yb_ = ybuf[c_]
                    yid = ("ybuf", c_)
                    ts(yb_[:, :], pbc[:, 0:512], sconv[:, c_, 0:1], ALU.mult, [("pbuf", c_), "sconv"], [yid])
                    stt(yb_[:, :], pbc[:, 1:513], sconv[:, c_, 1:2], yb_[:, :], ALU.mult, ALU.add,
                        [("pbuf", c_), "sconv", yid], [yid])
                    stt(yb_[:, :], pbc[:, 2:514], sconv[:, c_, 2:3], yb_[:, :], ALU.mult, ALU.add,
                        [("pbuf", c_), "sconv", yid], [yid])
                    tt(yp_[:, 2 + c_, :], yb_[:, :], cbs[c_][:, :], ALU.mult, [yid, ("cbs", c_)], [("yPC", tc % 2, 2 + c_)])
            if tc >= 1:
                wout_pc(tc - 1)
            if tc < NTC:
                for c_ in range(2):
                    b_ = rot("p2r", RB)
                    mm(ps[b_][:, :], PWblk[:, c_, :], dT[c_][:, :], True, True, ["PWblk", ("dT", c_)], [PS(b_)])
                    act(yPC[tc % 2][:, c_, :], ps[b_][:, :], AF.Copy, [PS(b_), "pscale"], [("yPC", tc % 2, c_)],
                        scale=pscale[:, c_:c_ + 1])

    hn_pending = []

    def headnorm_flush():
        while hn_pending:
            pb, i_, gain, gain_id, out_lo, id_lo, out_hi, id_hi = hn_pending.pop(0)
            sqb = sq1 if i_ == 0 else sq2
            sqid = ("sqh", i_)
            sb_ = rot("p1s", [3, 4])
            mm(ps[sb_][:, :], onesblk[:, :], sqb[:, :], True, True, ["onesblk", sqid], [PS(sb_)])
            act(sqb[:, :], ps[sb_][:, :], AF.Ln, [PS(sb_), "eps6"], [sqid], scale=1.0 / 64, bias=eps6[:, 0:1])
            act(sqb[:, :], sqb[:, :], AF.Exp, [sqid], [sqid], scale=-0.5)
            stt(out_lo, ps[pb][0:64, :], gain[0:64], sqb[0:64, :], ALU.mult, ALU.mult, [PS(pb), sqid, gain_id], [id_lo])
            sg = stg[i_]
            stt(sg[64:128, :], ps[pb][64:128, :], gain[64:128], sqb[64:128, :], ALU.mult, ALU.mult,
                [PS(pb), sqid, gain_id], [("stg", i_)])
            dma("sp", out_hi, sg[64:128, :], [("stg", i_)], [id_hi])

    def headnorm_store(pb, gain, gain_id, out_lo, id_lo, out_hi, id_hi):
        i_ = rot("p1q", [0, 1])
        sqb = sq1 if i_ == 0 else sq2
        act(sqb[:, :], ps[pb][:, :], AF.Square, [PS(pb)], [("sqh", i_)])
        hn_pending.append((pb, i_, gain, gain_id, out_lo, id_lo, out_hi, id_hi))

    def phase_proj(l):
        wv = w_in_d[l].rearrange("(k p) c -> p k c", p=128)
        dma("pool", wbuf[0][:, :, 0:512], wv[:, :, C_Q:C_Q + 512], [], ["wbuf0"])
        dma("pool", wbuf[1][:, :, 0:512], wv[:, :, C_KSLC:C_KSLC + 512], [], ["wbuf1"])
        for hp in range(NH // 2):
            for tc in range(NTC):
                cs = slice(tc * 512, (tc + 1) * 512)
                pb = rot("p1a", [0, 1, 2])
                for k in range(KC):
                    mm(ps[pb][:, :], wbuf[0][:, k, hp * 128:(hp + 1) * 128], hT[:, k, cs], k == 0, k == KC - 1,
                       ["wbuf0", ("hT", k, tc)], [PS(pb)])
                headnorm_flush()
                headnorm_store(pb, qg8[:, 0:1], "qg8", qA[0:64, 2 * hp, cs], ("qA", 2 * hp, tc),
                               qA[0:64, 2 * hp + 1, cs], ("qA", 2 * hp + 1, tc))
        dma("pool", wbuf[0][:, :, 0:280], wv[:, :, C_VSLC:C_VSLC + 280], [], ["wbuf0"])
        for br in range(2):
            for tc in range(NTC):
                cs = slice(tc * 512, (tc + 1) * 512)
                pb = rot("p1a", [0, 1, 2])
                for k in range(KC):
                    mm(ps[pb][:, :], wbuf[1][:, k, br * 128:(br + 1) * 128], hT[:, k, cs], k == 0, k == KC - 1,
                       ["wbuf1", ("hT", k, tc)], [PS(pb)])
                headnorm_flush()
                headnorm_store(pb, kg[:, 1 + br:2 + br], "kg", kA[0:64, br, 0, cs], ("kA", br, 0, tc),
                               kA[0:64, br, 1, cs], ("kA", br, 1, tc))
        headnorm_flush()
        for kv in range(2):
            dma("pool", W1[:, :, :], cmp_w1_d[l, kv].rearrange("(l d) e -> d l e", d=64), [], ["W1"])
            for li in range(32):
                mm(ps[6][0:64, 0:1], W1[:, li, :], posT[:, kv, li:li + 1], li == 0, li == 31, ["W1", "posT"], [PS(6)])
            cp(cb[:, kv:kv + 1], ps[6][0:64, 0:1], [PS(6)], ["cb"])
            for g in range(2):
                co = 256 + kv * 128 + g * 64
                for tc in range(NTC):
                    cs = slice(tc * 512, (tc + 1) * 512)
                    pb = rot("p1a", [0, 1])
                    for k in range(KC):
                        mm(ps[pb][0:64, :], wbuf[1][:, k, co:co + 64], hT[:, k, cs], k == 0, k == KC - 1,
                           ["wbuf1", ("hT", k, tc)], [PS(pb)])
                    act(rawT[:, cs], ps[pb][0:64, :], AF.Copy, [PS(pb)], ["rawT"])
                for li in range(32):
                    rhs = rawT[:, li:li + 16 * (NCMP - 1) + 1:16]
                    mm(ps[6][0:64, 0:NCMP], W1[:, li, :], rhs, li == 0, li == 31, ["W1", "rawT"], [PS(6)])
                x_ = gtmp[:, 0, 0:NCMP]
                x2 = gtmp[:, 1, 0:NCMP]
                z_ = gtmp[:, 2, 0:NCMP]
                e_ = gtmp[:, 3, 0:NCMP]
                act(x_, ps[6][0:64, 0:NCMP], AF.Identity, [PS(6), "cb"], ["g_x"], bias=cb[:, kv:kv + 1])
                tt(x2, x_, x_, ALU.mult, ["g_x"], ["g_x2"])
                ts(x2, x2, 0.044715, ALU.mult, ["g_x2"], ["g_x2"], s2=1.0, op1=ALU.add)
                tt(z_, x2, x_, ALU.mult, ["g_x2", "g_x"], ["g_z"])
                act(e_, z_, AF.Exp, ["g_z"], ["g_e"], scale=-1.5957691216057308)
                ts(e_, e_, 1.0, ALU.add, ["g_e"], ["g_e"])
                P.op("dve", lambda e, e_=e_: e.reciprocal(out=e_, in_=e_), ["g_e"], ["g_e"])
                tt(hidg[:, 0:NCMP], x_, e_, ALU.mult, ["g_x", "g_e"], ["hidg"])
                if kv == 0:
                    mm(ps[7][0:64, 0:NCMP], W2[:, 0, :], hidg[:, 0:NCMP], True, True, ["W2", "hidg"], [PS(7)])
                    act(sq1[0:64, 0:NCMP], ps[7][0:64, 0:NCMP], AF.Square, [PS(7)], [("sqh", 0)])
                    sb_ = rot("p1s", [3, 4])
                    mm(ps[sb_][0:64, 0:NCMP], ones_f[0:64, 0:64], sq1[0:64, 0:NCMP], True, True, ["ones_f", ("sqh", 0)], [PS(sb_)])
                    act(sq2[0:64, 0:NCMP], ps[sb_][0:64, 0:NCMP], AF.Ln, [PS(sb_), "eps6"], [("sqh", 1)], scale=1.0 / 64,
                        bias=eps6[0:64, 0:1])
                    act(sq2[0:64, 0:NCMP], sq2[0:64, 0:NCMP], AF.Exp, [("sqh", 1)], [("sqh", 1)], scale=-0.5)
                    stt(kcA[0:64, g, :], ps[7][0:64, 0:NCMP], kg[0:64, 0:1], sq2[0:64, 0:NCMP], ALU.mult, ALU.mult,
                        [PS(7), ("sqh", 1), "kg"], [("kcA", g)])
                else:
                    mm(ps[7][0:NCMP, 0:64], hidg[:, 0:NCMP], W2[:, 1, :], True, True, ["W2", "hidg"], [PS(7)])
                    cp(vcA[0:NCMP, g, 0:64], ps[7][0:NCMP, 0:64], [PS(7)], [("vcA", g)])
        for ti in range(NT):
            tsl = slice(ti * 128, (ti + 1) * 128)
            vb = rot("p1v", [4, 5])
            for br in range(2):
                for k in range(KC):
                    mm(ps[vb][:, br * 128:(br + 1) * 128], hT[:, k, tsl], wbuf[0][:, k, br * 128:(br + 1) * 128],
                       k == 0, k == KC - 1, ["wbuf0", ("hT", k, ti // 4)], [PS(vb)])
            cp(Vaug[:, ti, :, :, 0:64], ps[vb][:, 0:256].rearrange("p (b g d) -> p b g d", b=2, g=2),
               [PS(vb)], [("Vaug", ti)])
        for tc in range(NTC):
            cs = slice(tc * 512, (tc + 1) * 512)
            pb = rot("p1a", [0, 1])
            for k in range(KC):
                mm(ps[pb][0:89, :], wbuf[0][:, k, 191:280], hT[:, k, cs], k == 0, k == KC - 1,
                   ["wbuf0", ("hT", k, tc)], [PS(pb)])
            act(sq1[64:89, :], ps[pb][64:89, :], AF.Exp, [PS(pb)], [("sqh", 0)], scale=-1.0)
            act(lngate[64:89, cs], sq1[64:89, :], AF.Ln, [("sqh", 0), "onec"], [("lngate", tc)], scale=1.0, bias=onec[64:89, 0:1])

    def epi_ln(ob, cbuf):
        act(cbuf[64:65, :], ps[ob][64:65, :], AF.Ln, [PS(ob), "tiny"], [("lnD", id(cbuf))], scale=1.0, bias=tiny[64:65, 0:1])

    def epi_mm(b, cbuf, bk=6):
        mm(ps[bk][0:64, :], selb[64:89, b, :], cbuf[64:89, :], True, True,
           ["selb", ("lnD", id(cbuf)), ("GMq", id(cbuf))], [PS(bk)])

    def epi_fin(ob, b, g, qt, first, last, aj, bk=6):
        qs = slice(qt * 128, (qt + 1) * 128)
        bi = rot("bcs", [0, 1])
        bcs = bc_sb[bi]
        bid = ("bc_sb", bi)
        ac = acc[aj]
        aid = ("acc", aj)
        act(bcs[:, :], ps[bk][0:64, :], AF.Exp, [PS(bk)], [bid])
        o_out = hT[0:64, 4 * g:4 * g + 4, qs]
        o_id = [("hT", k, qt // 4) for k in range(4 * g, 4 * g + 4)]
        acc3 = ac[:, :].rearrange("p (h q) -> p h q", h=4)
        tmp3 = tmpo[:, :].rearrange("p (h q) -> p h q", h=4)
        if first:
            tt(ac[:, :], ps[ob][0:64, :], bcs[:, :], ALU.mult, [PS(ob), bid], [aid])
        else:
            tt(tmpo[:, :], ps[ob][0:64, :], bcs[:, :], ALU.mult, [PS(ob), bid], ["tmpo"])
            if last:
                tt(o_out, acc3, tmp3, ALU.add, [aid, "tmpo"], o_id)
            else:
                tt(ac[:, :], ac[:, :], tmpo[:, :], ALU.add, [aid, "tmpo"], [aid])

    def phase_attn(l):
        LAG = int(os.environ.get("LAG", "3"))
        SBK = [0, 1, 2, 6] if os.environ.get("SB4", "1") == "1" else [0, 1, 2]
        NPT = 7
        NB = 4
        NDUM = int(os.environ.get("NDUM", "0"))
        EPD1 = int(os.environ.get("EPD1", "2"))
        EPD2 = int(os.environ.get("EPD2", "3"))
        CHS = tuple(int(v) for v in os.environ.get("CHS", "0,3,4,6,7,10").split(","))
        DUMN = int(os.environ.get("DUMN", "128"))
        wo = w_out_d[l]
        tasks = []
        for r_ in range(NT // 2):
            tasks += [(0, NT - 1 - r_), (0, r_), (1, NT - 1 - r_), (1, r_)]
        done_tc = {}
        tiles = []
        tstart = []
        for j, (g, qt) in enumerate(tasks):
            tstart.append(len(tiles))
            for kt in range(qt + 1):
                tiles.append((j, 0, kt, kt == 0, kt == qt))
            k0 = max(0, qt - 4)
            for kt in range(k0, qt + 1):
                tiles.append((j, 1, kt, kt == k0, kt == qt))
        n = len(tiles)
        deferred = {}

        def at(step, fn):
            deferred.setdefault(step, []).append(fn)

        obank = {}
        eps = []

        def ep_new(ob, b, j, first, last, cbuf):
            e_ = {"ob": ob, "b": b, "j": j, "first": first, "last": last, "cbuf": cbuf, "stage": 0}
            eps.append(e_)
            return e_

        def ep_to(e_, target):
            while e_["stage"] < target:
                nxt = e_["stage"] + 1
                for p_ in eps:
                    if p_ is e_:
                        break
                    if nxt == 1 and p_["cbuf"] is e_["cbuf"]:
                        ep_to(p_, 2)
                    if nxt == 2:
                        ep_to(p_, 3)
                if nxt == 1:
                    epi_ln(e_["ob"], e_["cbuf"])
                elif nxt == 2:
                    e_["bk"] = rot("ts", SBK) if len(SBK) == 4 else 6
                    epi_mm(e_["b"], e_["cbuf"], e_["bk"])
                else:
                    g_, qt_ = tasks[e_["j"]]
                    epi_fin(e_["ob"], e_["b"], g_, qt_, e_["first"], e_["last"], e_["j"] % NB, e_["bk"])
                    owners.pop(e_["ob"], None)
                    if e_["last"]:
                        done_tc[qt_ // 4] = done_tc.get(qt_ // 4, 0) + 1
                        if done_tc[qt_ // 4] == 8:
                            sched_wout(qt_ // 4)
                e_["stage"] = nxt
            while eps and eps[0]["stage"] == 3:
                eps.pop(0)

        owners = {}

        def take_obank(hold=True):
            while True:
                for _ in range(3):
                    ob = rot("to", [3, 4, 5])
                    if ob not in owners:
                        if hold:
                            owners[ob] = True
                        return ob
                assert eps, "no free O bank and nothing to force"
                ep_to(eps[0], 3)

        cur_step = [0]

        def sched_wout(tc):
            cs = slice(tc * 512, (tc + 1) * 512)

            def piece(dc):
                def f():
                    wi = rot("won", [0, 1])
                    wb_ = WoN[wi]
                    wid = ("WoN", wi)
                    dma("pool", wb_[:, :, :],
                        wo[256:768, dc * 128:(dc + 1) * 128].rearrange("(h d) c -> d h c", d=64), [], [wid])
                    ob = take_obank(hold=False)
                    for h in range(NH):
                        mm(ps[ob][:, :], wb_[:, h, :], hT[0:64, h, cs], h == 0, h == NH - 1,
                           [wid, ("hT", h, tc)], [PS(ob)])
                    tt(xT[:, dc, cs], xT[:, dc, cs], ps[ob][:, :], ALU.add, [("xT", dc, tc), PS(ob)], [("xT", dc, tc)])
                return f
            for dc in range(KC):
                at(cur_step[0] + 2 + 2 * dc, piece(dc))

        def q_ops(g, qt):
            qs = slice(qt * 128, (qt + 1) * 128)
            q_rhs = qA[0:100, 4 * g:4 * g + 4, qs]
            q_ids = [("qA", h, qt // 4) for h in range(4 * g, 4 * g + 4)] + ["qA_al"]
            return qs, q_rhs, q_ids

        chains = {}

        def chain_run(j, k):
            c_ = chains[j]
            while c_["done"] <= k:
                c_["fns"][c_["done"]]()
                c_["done"] += 1

        def cmp_chain(j):
            g, qt = tasks[j]
            qs, q_rhs, q_ids = q_ops(g, qt)
            cbuf = comb[j % NB]
            st_ = {}

            def A_():
                for e2_ in list(eps):
                    if e2_["cbuf"] is cbuf:
                        ep_to(e2_, 3)
                tt(cbuf[64:89, :].rearrange("p (h q) -> p h q", h=4), bcast_mid(lngate[64:89, qs], 4),
                   oh[64:89, g, :].unsqueeze(2).to_broadcast([25, 4, 128]), ALU.mult,
                   [("lngate", qt // 4), "oh"], [("GMq", id(cbuf)), ("lnD", id(cbuf))])
                sb_ = rot("ts", SBK)
                mm(ps[sb_][0:NCMP, :], kcA[0:100, g, :], q_rhs, True, False, q_ids + [("kcA", g), "kcA_c"], [PS(sb_)])
                mm(ps[sb_][0:NCMP, :], ident[0:NCMP, 0:NCMP], bcast_mid(maskc[0:NCMP, qs], 4), False, True,
                   ["ident", "maskc"], [PS(sb_)])
                pi = rot("tp", list(range(NPT)))
                act(Pt[pi][0:NCMP, :], ps[sb_][0:NCMP, :], AF.Exp, [PS(sb_)], [("Pt", pi)])
                st_["pi"] = pi

            def B_():
                if j - 1 in chains:
                    chain_run(j - 1, 5)
                pi = st_["pi"]
                ob = take_obank()
                st_["ob"] = ob
                mm(ps[ob][0:65, :], vcA[0:NCMP, g, :], Pt[pi][0:NCMP, :], True, True,
                   [("vcA", g), "vcA", ("Pt", pi)], [PS(ob)])
                for hh in range(4):
                    mm(ps[7][:, hh * 33:(hh + 1) * 33], Pt[pi][0:NCMP, hh * 128:(hh + 1) * 128], maug[0:NCMP, :],
                       True, True, [("Pt", pi), "maug"], [PS(7)])
                U3 = ps[7][:, 0:132].rearrange("p (h c) -> p h c", c=33)
                den = small[:, 0:4]
                ts(den, U3[:, :, 32], 1e-18, ALU.add, [PS(7)], ["den"])
                P.op("dve", lambda e, den=den: e.reciprocal(out=den, in_=den), ["den"], ["den"])
                tt(imp_t[:, :, :], U3[:, :, 0:32], den.unsqueeze(2).to_broadcast([128, 4, 32]), ALU.mult,
                   [PS(7), "den"], ["imp_t"])
                tt(imp_t[:, 0:2, :], imp_t[:, 0:2, :], imp_t[:, 2:4, :], ALU.add, ["imp_t"], ["imp_t"])
                imp = small[:, 8:40]
                tt(imp, imp_t[:, 0, :], imp_t[:, 1, :], ALU.add, ["imp_t"], ["imp"])
                tt(imp, imp, fkeep[:, qt, :], ALU.mult, ["imp", "fkeep"], ["imp"])
                tt(imp, imp, fadd[:, qt, :], ALU.add, ["imp", "fadd"], ["imp"])
                m8 = small[:, 40:48]
                P.op("dve", lambda e, m8=m8, imp=imp: e.max(out=m8, in_=imp), ["imp"], ["m8"])
                ts(imp, imp, m8[:, 7:8], ALU.is_ge, ["imp", "m8"], ["imp"], s2=1.0, op1=ALU.subtract)
                ts(MB[:, 64:96], imp, -NEG, ALU.mult, ["imp"], ["MB"])

            def C1_():
                st_["ep"] = ep_new(st_["ob"], 0, j, True, False, cbuf)
                ep_to(st_["ep"], 1)

            def C2_():
                ep_to(st_["ep"], 2)

            def C3_():
                ep_to(st_["ep"], 3)

            def D_():
                mm(ps[7][0:96, 256:384], MB[:, 0:96], ident[:, :], True, True, ["MB", "ident"], [PS(7)])
                act(qA[64:96, 4 * g:4 * g + 4, qs], bcast_mid(ps[7][64:96, 256:384], 4), AF.Copy, [PS(7)],
                    [("qAm", g, qt)])

            chains[j] = {"fns": [A_, B_, C1_, C2_, C3_, D_], "done": 0}
            if j < 2:
                chain_run(j, 5)
            else:
                s0 = tstart[j - 2]
                lim = tstart[j] - 1
                for k_, dstep in enumerate(CHS):
                    at(min(s0 + dstep, lim), lambda k_=k_: chain_run(j, k_))

        info = {}

        def emit_qk(i):
            j, br, kt, first, last = tiles[i]
            g, qt = tasks[j]
            qs, q_rhs, q_ids = q_ops(g, qt)
            ks_ = slice(kt * 128, (kt + 1) * 128)
            sb_ = rot("ts", SBK)
            diag = kt == qt
            edge = (br == 1) and (kt == qt - 4)
            extra = [("qAm", g, qt)] if br == 0 else []
            mm(ps[sb_][:, :], kA[0:100, br, g, ks_], q_rhs, True, not (diag or edge),
               q_ids + extra + [("kA", br, g, kt // 4), "kA_c"], [PS(sb_)])
            if diag:
                mm(ps[sb_][:, :], ident[:, :], bcast_mid(cmd[:, :], 4), False, True, ["ident", "cmd"], [PS(sb_)])
            if edge:
                mm(ps[sb_][:, :], ident[:, :], bcast_mid(cmw[:, :], 4), False, True, ["ident", "cmw"], [PS(sb_)])
            pi = rot("tp", list(range(NPT)))
            act(Pt[pi][:, :], ps[sb_][:, :], AF.Exp, [PS(sb_)], [("Pt", pi)])
            info[i] = pi

        def emit_pv(i, step):
            j, br, kt, first, last = tiles[i]
            g, qt = tasks[j]
            pi = info.pop(i)
            if first:
                obank[(j, br)] = take_obank()
            ob = obank[(j, br)]
            mm(ps[ob][0:65, :], Vaug[:, kt, br, g, :], Pt[pi][:, :], first, last,
               [("Vaug", kt), "Vaug_1", ("Pt", pi)], [PS(ob)])
            if not last:
                for _d in range(NDUM):
                    mm(ps[ob][0:65, 0:DUMN], zeros_b[:, 0:65], ident[:, 0:DUMN], False, False, ["zeros_b", "ident"], [PS(ob)])
            if last:
                cbuf = comb[j % NB]
                ep_ = ep_new(ob, 1 + br, j, False, br == 1, cbuf)
                ep_to(ep_, 1)
                at(step + EPD1, lambda: ep_to(ep_, 2))
                at(step + EPD2, lambda: ep_to(ep_, 3))

        for j in range(len(tasks)):
            cmp_chain(j)
        i = 0
        while i < n + LAG or deferred:
            cur_step[0] = i
            if i < n:
                emit_qk(i)
            if 0 <= i - LAG < n:
                emit_pv(i - LAG, i)
            for fn in deferred.pop(i, []):
                fn()
            i += 1
        for e_ in list(eps):
            ep_to(e_, 3)
        while deferred:
            k_ = min(deferred)
            for fn in deferred.pop(k_):
                fn()

    def phase_ffn(l):
        up = ffn_up_d[l].rearrange("(k p) c -> p k c", p=128)
        dn = ffn_down_d[l]
        P.op("dve", lambda e: e.memset(abuf[:, 0:2], 0.0), [], ["abuf_h"])
        for half in range(2):
            for fi in range(11):
                fc = half * 11 + fi
                wb = wup[fc % 2]
                wid = ("wup", fc % 2)
                dma("pool", wb[:, :, 0:128], up[:, :, fc * 128:(fc + 1) * 128], [], [wid])
                dma("pool", wb[:, :, 128:256], up[:, :, DFF + fc * 128:DFF + (fc + 1) * 128], [], [wid])
                for tc in range(NTC):
                    cs = slice(tc * 512, (tc + 1) * 512)
                    ab = rot("fa", [0, 1])
                    gb = rot("fg", [2, 3])
                    for k in range(KC):
                        mm(ps[ab][:, :], wb[:, k, 0:128], hT[:, k, cs], k == 0, k == KC - 1, [wid, ("hT", k, tc)], [PS(ab)])
                    for k in range(KC):
                        mm(ps[gb][:, :], wb[:, k, 128:256], hT[:, k, cs], k == 0, k == KC - 1, [wid, ("hT", k, tc)], [PS(gb)])
                    act(abuf[:, 2 + tc * 512:2 + (tc + 1) * 512], ps[ab][:, :], AF.Copy, [PS(ab)], [("abuf", tc)])
                    rd = [("abuf", tc), "abuf_h", "fconv"] + ([("abuf", tc - 1)] if tc else [])
                    y_ = yb[tc % 2]
                    yid = ("yb", tc % 2)
                    o = tc * 512
                    act(y_[:, :], abuf[:, o:o + 512], AF.Copy, rd, [yid], scale=fconv[:, fc, 0:1])
                    stt(y_[:, :], abuf[:, o + 1:o + 513], fconv[:, fc, 1:2], y_[:, :], ALU.mult, ALU.add, rd + [yid], [yid])
                    stt(y_[:, :], abuf[:, o + 2:o + 514], fconv[:, fc, 2:3], y_[:, :], ALU.mult, ALU.add, rd + [yid], [yid])
                    s_ = sil[tc % 2]
                    sid = ("sil", tc % 2)
                    act(s_[:, :], y_[:, :], AF.Silu, [yid], [sid])
                    tt(mT[:, fi, cs], s_[:, :], ps[gb][:, :], ALU.mult, [sid, PS(gb)], [("mT", fi, tc)])
            for dc in range(KC):
                wd = wdn[dc % 2]
                wdid = ("wdn", dc % 2)
                dma("pool", wd[:, :, :],
                    dn[half * 1408:(half + 1) * 1408, dc * 128:(dc + 1) * 128].rearrange("(f p) c -> p f c", p=128),
                    [], [wdid])
                for tc in range(NTC):
                    cs = slice(tc * 512, (tc + 1) * 512)
                    ob = rot("fd", [4, 5])
                    for fi in range(11):
                        mm(ps[ob][:, :], wd[:, fi, :], mT[:, fi, cs], fi == 0, fi == 10, [wdid, ("mT", fi, tc)], [PS(ob)])
                    tt(xT[:, dc, cs], xT[:, dc, cs], ps[ob][:, :], ALU.add, [("xT", dc, tc), PS(ob)], [("xT", dc, tc)])

    load_consts()
    for s in range(NSEQ):
        xv = xT_d[s].rearrange("(k p) t -> p k t", p=128)
        for k in range(KC):
            dma("sp", xT[:, k, :], xv[:, k, :], [], [("xT", k, tc) for tc in range(NTC)])
        for l in range(NL):
            if upto >= 1:
                load_layer_small(l)
            if upto >= 2:
                rmsnorm_to_hT(g1, "g1", sq_n, rstd_n, "n1")
            if upto >= 3:
                phase_pool_conv(l)
            P.barrier()
            if upto >= 4:
                init_attn_consts()
            if upto >= 5:
                phase_proj(l)
            P.barrier()
            if upto >= 6:
                phase_attn(l)
            P.barrier()
            if upto >= 8:
                rmsnorm_to_hT(g2, "g2", sq_f, rstd_f, "n2")
            if upto >= 9:
                phase_ffn(l)
            P.barrier()
        yv = yT_d[s].rearrange("(k p) t -> p k t", p=128)
        for k in range(KC):
            dma("sp", yv[:, k, :], xT[:, k, :], [("xT", k, tc) for tc in range(NTC)], [])
    P.emit()
    return nc, hc


W_IN_PERM = None


def _perm_cols():
    o = {"pool": 0, "q": 256, "kcmp": 768, "vcmp": 896, "kslc": 1024, "vslc": 1152, "kwin": 1280,
         "vwin": 1408, "gate": 1536, "cb": 1560, "cc": 1816, "cx": 2072}
    order = [("q", 512), ("kslc", 128), ("kwin", 128), ("kcmp", 128), ("vcmp", 128), ("vslc", 128),
             ("vwin", 128), ("gate", 24), ("pool", 256), ("cb", 256), ("cc", 256), ("cx", 256)]
    idx = np.concatenate([np.arange(o[n], o[n] + w) for n, w in order])
    assert idx.size == 2328
    return idx


def prep_weights(inp, NL):
    f = lambda a: np.ascontiguousarray(np.asarray(a, dtype=np.float32))
    w = {}
    w["w_in"] = f(np.asarray(inp["w_in"])[:NL][:, :, _perm_cols()])
    w["w_out"] = f(np.asarray(inp["w_out"])[:NL])
    w["ffn_up"] = f(np.asarray(inp["ffn_up"])[:NL])
    w["ffn_down"] = f(np.asarray(inp["ffn_down"])[:NL])
    w["cmp_w1"] = f(np.asarray(inp["cmp_w1"])[:NL])
    w["cmp_w2"] = f(np.asarray(inp["cmp_w2"])[:NL])
    w["cmp_posT"] = f(np.asarray(inp["cmp_pos"])[:NL].transpose(0, 1, 3, 2))
    w["pool_w"] = f(np.asarray(inp["pool_w"])[:NL])
    w["g1"] = f(np.asarray(inp["norm1_g"])[:NL].reshape(NL, 8, 128).transpose(0, 2, 1))
    w["g2"] = f(np.asarray(inp["norm2_g"])[:NL].reshape(NL, 8, 128).transpose(0, 2, 1))
    w["qg"] = f(np.asarray(inp["q_norm_g"])[:NL].reshape(NL, 64, 1))
    w["kg"] = f(np.asarray(inp["k_norm_g"])[:NL].transpose(0, 2, 1))
    w["pscale"] = f(np.asarray(inp["pool_scale"])[:NL].reshape(NL, 2, 128).transpose(0, 2, 1))
    w["sconv"] = f(np.asarray(inp["sconv_w"])[:NL].reshape(NL, 3, 2, 128).transpose(0, 3, 2, 1))
    w["fconv"] = f(np.asarray(inp["ffn_conv"])[:NL].reshape(NL, 3, NFC, 128).transpose(0, 3, 2, 1))
    return w


_CACHE = {}


def run_model(inp, T, NSEQ, NL, n_cores, dbg=None, upto=99):
    key = (T, NSEQ, NL, tuple(sorted(dbg.items())) if dbg else None, upto)
    if key not in _CACHE:
        _CACHE[key] = build_program(T, NSEQ, NL, dbg, upto)
    nc, hc = _CACHE[key]
    w = prep_weights(inp, NL)
    x = np.asarray(inp["x"], dtype=np.float32)
    in_maps = []
    for c in range(n_cores):
        m = dict(w)
        m.update(hc)
        m["xT"] = np.ascontiguousarray(x[c * NSEQ:(c + 1) * NSEQ].transpose(0, 2, 1))
        in_maps.append(m)
    res = run_bass_kernel_spmd(nc, in_maps, core_ids=list(range(n_cores)))
    outs = [r["yT"].transpose(0, 2, 1) for r in res.results]
    return np.ascontiguousarray(np.concatenate(outs, axis=0)), res


def kernel(**inputs):
    out, _ = run_model(inputs, 2048, 2, 4, 8)
    return out.astype(np.float32)
```

```python
import contextlib
import os
import numpy as np
import concourse.bass as bass
import concourse.mybir as mybir
from concourse.bass_utils import run_bass_kernel_spmd

F32 = mybir.dt.float32
BF16 = mybir.dt.bfloat16
ALU = mybir.AluOpType
AF = mybir.ActivationFunctionType

ENGS = ("pe", "act", "dve", "pool", "sp")
NDMA = 8
ATTACH_WAIT = os.environ.get("ATTACH_WAIT", "1") == "1"


class Prog:
    def __init__(self, nc):
        self.nc = nc
        self.q = {e: [] for e in ENGS}
        self.cnt = {e: 0 for e in ENGS}
        self.dcnt = {e: 0 for e in ENGS}
        self.seen = {e: {} for e in ENGS}
        self.lastw = {}
        self.readers = {}

    def _need(self, eng, tok, waits):
        if tok is None:
            return
        kind, e2, v = tok
        if kind == "c":
            if e2 == eng and eng == "pe":
                return
            key = ("c", e2)
            val = v
        else:
            key = ("d", e2, v % NDMA)
            val = v // NDMA + 1
        if self.seen[eng].get(key, 0) >= val:
            return
        self.seen[eng][key] = val
        waits.append(tok)

    def _deps(self, eng, reads, writes):
        waits = []
        for b in reads:
            self._need(eng, self.lastw.get(b), waits)
        for b in writes:
            self._need(eng, self.lastw.get(b), waits)
            r = self.readers.get(b)
            if r:
                for t in r.values():
                    self._need(eng, t, waits)
        return waits

    def _commit(self, tok, reads, writes):
        kind, e, v = tok
        rk = (kind, e) if kind == "c" else (kind, e, v % NDMA)
        for b in reads:
            self.readers.setdefault(b, {})[rk] = tok
        for b in writes:
            self.lastw[b] = tok
            self.readers[b] = {}

    def op(self, eng, fn, reads=(), writes=()):
        waits = self._deps(eng, reads, writes)
        self.cnt[eng] += 1
        tok = ("c", eng, self.cnt[eng])
        self.q[eng].append(("c", fn, waits, None))
        self._commit(tok, reads, writes)
        return tok

    def dma(self, eng, fn, reads=(), writes=()):
        waits = self._deps(eng, reads, writes)
        j = self.dcnt[eng]
        self.dcnt[eng] += 1
        if j >= NDMA:
            self._need(eng, ("d", eng, j - NDMA), waits)
        tok = ("d", eng, j)
        self.q[eng].append(("d", fn, waits, j))
        self._commit(tok, reads, writes)
        return tok

    def _all_tokens(self):
        toks = []
        for e in ENGS:
            if self.cnt[e]:
                toks.append(("c", e, self.cnt[e]))
            n = self.dcnt[e]
            for s in range(min(NDMA, n)):
                j = n - 1
                while j % NDMA != s:
                    j -= 1
                toks.append(("d", e, j))
        return toks

    def barrier(self):
        toks = self._all_tokens()
        for e in ENGS:
            waits = []
            for t in toks:
                self._need(e, t, waits)
            if waits:
                self.q[e].append(("w", None, waits, None))

    def emit(self):
        nc = self.nc
        toks = [t for t in self._all_tokens() if t[0] == "d"]
        for e in ENGS:
            waits = []
            for t in toks:
                self._need(e, t, waits)
            if waits:
                self.q[e].append(("w", None, waits, None))
        sig = {e: set() for e in ENGS}
        for e in ENGS:
            for (_, _, waits, _) in self.q[e]:
                for (k2, e2, v) in waits:
                    if k2 == "c":
                        sig[e2].add(v)
        rank = {e: {p: i + 1 for i, p in enumerate(sorted(sig[e]))} for e in ENGS}
        with contextlib.ExitStack() as st:
            sems = {e: st.enter_context(nc.semaphore("s_" + e)) for e in ENGS}
            dsems = {e: [st.enter_context(nc.semaphore("d_%s%d" % (e, i))) for i in range(NDMA)]
                     for e in ENGS}
            block = st.enter_context(nc.Block())
            hooks = {"pe": block.tensor, "act": block.scalar, "dve": block.vector,
                     "pool": block.gpsimd, "sp": block.sync}
            for e in ENGS:
                def body(engine, e=e):
                    pos = 0
                    for kind, fn, waits, j in self.q[e]:
                        ws = []
                        for (k2, e2, v) in waits:
                            if k2 == "c":
                                ws.append((sems[e2], rank[e2][v]))
                            else:
                                ws.append((dsems[e2][v % NDMA], 16 * (v // NDMA + 1)))
                        attach = None
                        if kind == "c" and ws and ATTACH_WAIT:
                            attach = ws.pop()
                        for (sm, vv) in ws:
                            engine.wait_ge(sm, vv)
                        if kind == "c":
                            pos += 1
                            ins = fn(engine)
                            if attach is not None:
                                ins._wait_ge(attach[0], attach[1])
                            if pos in rank[e]:
                                ins.then_inc(sems[e], 1)
                        elif kind == "d":
                            fn(engine).then_inc(dsems[e][j % NDMA], 16)
                hooks[e](body)


D = 1024
KC = 8
DFF = 2816
NFC = 22
NH = 8
SLOPES = [2.0 ** (-(h + 1)) for h in range(8)]
NEG = -30000.0
C_Q, C_KSLC, C_KWIN, C_KCMP, C_VCMP, C_VSLC, C_VWIN, C_GATE, C_POOL, C_CB, C_CC, C_CX = (
    0, 512, 640, 768, 896, 1024, 1152, 1280, 1304, 1560, 1816, 2072)
SB_BASE = 16512
SB_END = 229376


def host_consts(T):
    NT = T // 128
    NCMP = (T - 32) // 16 + 1
    t = np.arange(T)
    c = {}
    qa = np.zeros((4, 8, T), np.float32)
    for h in range(8):
        s = SLOPES[h]
        qa[0, h] = -s * 128.0 * (t // 128)
        qa[1, h] = -s * (t % 128)
        qa[2, h] = s
        qa[3, h] = s
    c["c_qal"] = qa
    kw = np.zeros((36, T), np.float32)
    kw[32] = 1.0
    kw[33] = 1.0
    kw[34] = 128.0 * (t // 128)
    kw[35] = t % 128
    ks = kw.copy()
    for j in range(T // 64):
        ks[j, j * 64:(j + 1) * 64] = 1.0
    c["c_kwin"] = kw
    c["c_kslc"] = ks
    n = np.arange(NCMP)
    kc = np.zeros((36, NCMP), np.float32)
    kc[32] = 1.0
    kc[33] = 1.0
    kc[34] = 16.0 * n
    kc[35] = 31.0
    c["c_kcmp"] = kc
    mc = np.where((16 * n[:, None] + 31) <= t[None, :], 0.0, NEG).astype(np.float32)
    c["c_maskc"] = mc
    jj = np.arange(128)[:, None]
    ii = np.arange(128)[None, :]
    c["c_cmd"] = np.where(jj <= ii, 0.0, NEG).astype(np.float32)
    c["c_cmw"] = np.where(ii < jj, 0.0, NEG).astype(np.float32)
    c["c_ident"] = np.eye(128, dtype=np.float32)
    NS = T // 64
    c0 = n * 16
    s0 = np.arange(NS) * 64
    ov = np.minimum(c0[:, None] + 32, s0[None, :] + 64) - np.maximum(c0[:, None], s0[None, :])
    M = np.zeros((NCMP, 33), np.float32)
    M[:, :32][:, :NS] = np.clip(ov, 0, None) / 32.0
    M[:, 32] = 1.0
    c["c_maug"] = M
    blk = np.arange(32)[None, :]
    cur = (t // 64)[:, None]
    forced = (blk == 0) | (blk == cur) | (blk == cur - 1)
    causal = blk <= cur
    keep = (causal & ~forced).astype(np.float32)
    add = np.where(forced, 1e6, np.where(causal, 0.0, -1e6)).astype(np.float32)
    c["c_fkeep"] = keep.reshape(NT, 128, 32).transpose(1, 0, 2).copy()
    c["c_fadd"] = add.reshape(NT, 128, 32).transpose(1, 0, 2).copy()
    selb = np.zeros((25, 3, 64), np.float32)
    selb[0, :, :] = -1.0
    for b in range(3):
        selb[1 + b * 8:1 + (b + 1) * 8, b, :] = -1.0
    c["c_selb"] = selb
    oh = np.zeros((25, 2, 4), np.float32)
    for r in range(24):
        hh = r % 8
        oh[1 + r, hh // 4, hh % 4] = 1.0
    c["c_oh"] = oh
    fix = np.ones((128, 2, 16), np.float32)
    wins = (2, 4, 8, 16)
    for ch in range(2):
        for half in range(2):
            w = wins[2 * ch + half]
            tt = np.arange(16)
            fix[half * 64:(half + 1) * 64, ch, :] = w / np.minimum(tt + 1, w)
    c["c_pfix"] = fix
    return c


CONST_SHAPES = None


def build_program(T, NSEQ, NL, dbg=None, upto=99):
    NT = T // 128
    NTC = T // 512
    NCMP = (T - 32) // 16 + 1
    nc = bass.Bass("TRN2", target_bir_lowering=False)
    P = Prog(nc)

    def din(name, shape):
        return nc.dram_tensor(name, list(shape), F32, kind="ExternalInput").ap()

    xT_d = din("xT", [NSEQ, D, T])
    yT_d = nc.dram_tensor("yT", [NSEQ, D, T], F32, kind="ExternalOutput").ap()
    w_in_d = din("w_in", [NL, D, 2328])
    w_out_d = din("w_out", [NL, D, D])
    ffn_up_d = din("ffn_up", [NL, D, 2 * DFF])
    ffn_down_d = din("ffn_down", [NL, DFF, D])
    cmp_w1_d = din("cmp_w1", [NL, 2, 2048, 64])
    cmp_w2_d = din("cmp_w2", [NL, 2, 64, 64])
    cmp_posT_d = din("cmp_posT", [NL, 2, 64, 32])
    pool_w_d = din("pool_w", [NL, 4, 64, 64])
    g1_d = din("g1", [NL, 128, 8])
    g2_d = din("g2", [NL, 128, 8])
    qg_d = din("qg", [NL, 64, 1])
    kg_d = din("kg", [NL, 64, 3])
    pscale_d = din("pscale", [NL, 128, 2])
    sconv_d = din("sconv", [NL, 128, 2, 3])
    fconv_d = din("fconv", [NL, 128, NFC, 3])
    hc = host_consts(T)
    cd = {k: din(k, v.shape) for k, v in hc.items()}
    dbg_d = {}
    if dbg:
        for k, shp in dbg.items():
            dbg_d[k] = nc.dram_tensor("dbg_" + k, list(shp), F32, kind="ExternalOutput").ap()

    cur = [SB_BASE]

    def alloc(name, shape, dt, at=None):
        per = int(np.prod(shape[1:])) * (4 if dt == F32 else 2)
        per = (per + 31) // 32 * 32
        if at is None:
            off = cur[0]
            cur[0] += per
        else:
            off = at
        assert off + per <= SB_END, (name, off, per)
        return nc.alloc_sbuf_tensor_at(name, list(shape), dt, offset=off), off, per

    def A(name, shape, dt):
        return alloc(name, shape, dt)[0]

    xT = A("xT_sb", [128, KC, T], F32)
    hT = A("hT", [128, KC, T], BF16)
    ident = A("ident", [128, 128], BF16)
    zeros_b = A("zeros_b", [128, 128], BF16)
    ones_f = A("ones_f", [128, 128], F32)
    negones = A("negones", [128, 64], F32)
    onesblk = A("onesblk", [128, 128], F32)
    cmd = A("cmd", [128, 128], BF16)
    cmw = A("cmw", [128, 128], BF16)
    maug = A("maug", [128, 33], BF16)
    maskc = A("maskc", [128, T], BF16)
    fkeep = A("fkeep", [128, NT, 32], F32)
    fadd = A("fadd", [128, NT, 32], F32)
    selb = A("selb", [128, 3, 64], F32)
    oh = A("oh", [128, 2, 4], F32)
    pfix = A("pfix", [128, 2, 16], F32)
    eps6 = A("eps6", [128, 1], F32)
    tiny = A("tiny", [128, 1], F32)
    onec = A("onec", [128, 1], F32)
    g1 = A("g1s", [128, 8], F32)
    g2 = A("g2s", [128, 8], F32)
    qg = A("qgs", [128, 1], F32)
    qg8 = A("qg8", [128, 1], F32)
    kg = A("kgs", [128, 3], F32)
    pscale = A("pscales", [128, 2], F32)
    sconv = A("sconvs", [128, 2, 3], F32)
    fconv = A("fconvs", [128, NFC, 3], F32)
    W2 = A("W2", [64, 2, 64], BF16)
    posT = A("posT", [64, 2, 32], BF16)
    PWblk = A("PWblk", [128, 2, 128], BF16)
    cb = A("cb", [64, 2], F32)
    kcA = A("kcA", [128, 2, NCMP], BF16)
    vcA = A("vcA", [128, 2, 65], BF16)
    MB = A("MB", [128, 96], BF16)
    small = A("small", [128, 64], F32)
    imp_t = A("imp_t", [128, 4, 32], F32)
    phase_base = cur[0]
    qA = A("qA", [128, NH, T], BF16)
    kA = A("kA", [128, 2, 2, T], BF16)
    Vaug = A("Vaug", [128, NT, 2, 2, 65], BF16)
    lngate = A("lngate", [128, T], F32)
    local_base = cur[0]
    def region(base):
        st = [base]

        def R(name, shape, dt):
            t_, off, per = alloc(name, shape, dt, at=st[0])
            st[0] += per
            return t_
        return R, st

    R2, st2 = region(phase_base)
    wb_pc = R2("wb_pc", [128, KC, 1024], BF16)
    WoPC = R2("WoPC", [128, 4, D], BF16)
    yPC = [R2("yPC0", [128, 4, 512], BF16), R2("yPC1", [128, 4, 512], BF16)]
    ubuf = R2("ubuf", [128, 2, 528], F32)
    sA = [R2("sA0", [128, 528], F32), R2("sA1", [128, 528], F32)]
    sB = [R2("sB0", [128, 528], F32), R2("sB1", [128, 528], F32)]
    dT = [R2("dT0", [128, 512], BF16), R2("dT1", [128, 512], BF16)]
    cxs = [R2("cxs0", [128, 512], F32), R2("cxs1", [128, 512], F32)]
    cbs = [R2("cbs0", [128, 512], F32), R2("cbs1", [128, 512], F32)]
    pbuf = R2("pbuf", [128, 2, 514], F32)
    ybuf = [R2("ybuf0", [128, 512], F32), R2("ybuf1", [128, 512], F32)]
    sq_n = R2("sq_n", [128, 2, 512], F32)
    rstd_n = R2("rstd_n", [128, 512], F32)
    R1, st1 = region(local_base)
    wbuf = [R1("wbuf0", [128, KC, 512], BF16), R1("wbuf1", [128, KC, 512], BF16)]
    W1 = R1("W1", [64, 32, 64], BF16)
    sq1 = R1("sq1", [128, 512], F32)
    sq2 = R1("sq2", [128, 512], F32)
    stg = [R1("stg0", [128, 512], BF16), R1("stg1", [128, 512], BF16)]
    rawT = R1("rawT", [64, T], BF16)
    gtmp = R1("gtmp", [64, 4, 128], F32)
    hidg = R1("hidg", [64, 128], BF16)
    RT, stT = region(local_base)
    Pt = [RT("Pt%d" % i, [128, 512], BF16) for i in range(7)]
    comb = [RT("comb%d" % i, [128, 512], F32) for i in range(4)]
    bc_sb = [RT("bc_sb0", [64, 512], F32), RT("bc_sb1", [64, 512], F32)]
    acc = [RT("acc%d" % i, [64, 512], F32) for i in range(4)]
    tmpo = RT("tmpo", [64, 512], F32)
    WoN = [RT("WoN0", [128, 4, 128], BF16), RT("WoN1", [128, 4, 128], BF16)]
    ostg = [RT("ostg0", [64, 2, 128], BF16), RT("ostg1", [64, 2, 128], BF16)]
    RF, stF = region(phase_base)
    mT = RF("mT", [128, 11, T], BF16)
    abuf = RF("abuf", [128, T + 2], F32)
    yb = [RF("yb0", [128, 512], F32), RF("yb1", [128, 512], F32)]
    sil = [RF("sil0", [128, 512], F32), RF("sil1", [128, 512], F32)]
    wup = [RF("wup0", [128, KC, 256], BF16), RF("wup1", [128, KC, 256], BF16)]
    wdn = [RF("wdn0", [128, 11, 128], BF16), RF("wdn1", [128, 11, 128], BF16)]
    sq_f = RF("sq_f", [128, 2, 512], F32)
    rstd_f = RF("rstd_f", [128, 512], F32)
    for s_ in (st2, st1, stT, stF):
        assert s_[0] <= SB_END, s_

    ps = [nc.alloc_psum_tensor("bank%d" % i, [128, 512], F32) for i in range(8)]
    rr = {}

    def rot(key, banks):
        i = rr.get(key, 0)
        rr[key] = i + 1
        return banks[i % len(banks)]

    def PS(i):
        return "ps%d" % i

    def mm(out, lhsT, rhs, start, stop, reads, writes):
        P.op("pe", lambda e: e.matmul(out, lhsT=lhsT, rhs=rhs, start=start, stop=stop), reads, writes)

    def act(out, in_, func, reads, writes, scale=None, bias=None):
        kw = {}
        if scale is not None:
            kw["scale"] = scale
        if bias is not None:
            kw["bias"] = bias
        P.op("act", lambda e: e.activation(out=out, in_=in_, func=func, **kw), reads, writes)

    def tt(out, in0, in1, op, reads, writes, eng="dve"):
        P.op(eng, lambda e: e.tensor_tensor(out=out, in0=in0, in1=in1, op=op), reads, writes)

    def ts(out, in0, s1, op0, reads, writes, s2=None, op1=None, eng="dve"):
        if op1 is None:
            P.op(eng, lambda e: e.tensor_scalar(out=out, in0=in0, scalar1=s1, scalar2=None, op0=op0), reads, writes)
        else:
            P.op(eng, lambda e: e.tensor_scalar(out=out, in0=in0, scalar1=s1, scalar2=s2, op0=op0, op1=op1), reads, writes)

    def stt(out, in0, scalar, in1, op0, op1, reads, writes):
        P.op("dve", lambda e: e.scalar_tensor_tensor(out=out, in0=in0, scalar=scalar, in1=in1, op0=op0, op1=op1),
             reads, writes)

    def cp(out, in_, reads, writes, eng="dve"):
        P.op(eng, lambda e: e.tensor_copy(out=out, in_=in_), reads, writes)

    def dma(eng, out, in_, reads, writes):
        P.dma(eng, lambda e: e.dma_start(out=out, in_=in_), reads, writes)

    def bcast_mid(ap2, n):
        p, f = ap2.shape
        return ap2.unsqueeze(1).to_broadcast([p, n, f])

    def load_consts():
        dma("pool", ident[:], cd["c_ident"], [], ["ident"])
        dma("pool", cmd[:], cd["c_cmd"], [], ["cmd"])
        dma("pool", cmw[:], cd["c_cmw"], [], ["cmw"])
        dma("pool", maug[0:NCMP, :], cd["c_maug"], [], ["maug"])
        dma("pool", maskc[0:NCMP, :], cd["c_maskc"], [], ["maskc"])
        dma("sp", fkeep[:], cd["c_fkeep"], [], ["fkeep"])
        dma("sp", fadd[:], cd["c_fadd"], [], ["fadd"])
        dma("sp", selb[64:89], cd["c_selb"], [], ["selb"])
        dma("sp", oh[64:89], cd["c_oh"], [], ["oh"])
        dma("sp", pfix[:], cd["c_pfix"], [], ["pfix"])
        P.op("dve", lambda e: e.memset(ones_f[:], 1.0), [], ["ones_f"])
        P.op("dve", lambda e: e.memset(negones[:], -1.0), [], ["negones"])
        P.op("dve", lambda e: e.memset(onesblk[:], 0.0), [], ["onesblk"])
        P.op("dve", lambda e: e.memset(onesblk[0:64, 0:64], 1.0), [], ["onesblk"])
        P.op("dve", lambda e: e.memset(onesblk[64:128, 64:128], 1.0), [], ["onesblk"])
        P.op("dve", lambda e: e.memset(eps6[:], 1e-6), [], ["eps6"])
        P.op("dve", lambda e: e.memset(tiny[:], 1e-18), [], ["tiny"])
        P.op("dve", lambda e: e.memset(onec[:], 1.0), [], ["onec"])
        P.op("dve", lambda e: e.memset(MB[:], 0.0), [], ["MB"])
        P.op("dve", lambda e: e.memset(zeros_b[:], 0.0), [], ["zeros_b"])
        P.op("pool", lambda e: e.memset(vcA[:], 1.0), [], ["vcA"])

    def init_attn_consts():
        P.op("pool", lambda e: e.memset(qA[64:96, :, :], 0.0), [], ["qA_m"])
        dma("pool", qA[96:100, :, :], cd["c_qal"], [], ["qA_al"])
        for g in range(2):
            dma("pool", kA[64:100, 0, g, :], cd["c_kslc"], [], ["kA_c"])
            dma("pool", kA[64:100, 1, g, :], cd["c_kwin"], [], ["kA_c"])
        P.op("pool", lambda e: e.memset(Vaug[:, :, :, :, 64:65], 1.0), [], ["Vaug_1"])

    def load_layer_small(l):
        dma("sp", g1[:], g1_d[l], [], ["g1"])
        dma("sp", g2[:], g2_d[l], [], ["g2"])
        dma("sp", qg[0:64], qg_d[l], [], ["qg"])
        dma("sp", qg[64:128], qg_d[l], [], ["qg"])
        dma("sp", kg[0:64], kg_d[l], [], ["kg"])
        dma("sp", kg[64:128], kg_d[l], [], ["kg"])
        dma("sp", pscale[:], pscale_d[l], [], ["pscale"])
        dma("sp", sconv[:], sconv_d[l], [], ["sconv"])
        dma("sp", fconv[:], fconv_d[l], [], ["fconv"])
        dma("pool", W2[:], cmp_w2_d[l].rearrange("k e f -> e k f"), [], ["W2"])
        dma("pool", posT[:], cmp_posT_d[l].rearrange("k d l -> d k l"), [], ["posT"])
        P.op("pool", lambda e: e.memset(PWblk[:], 0.0), [], ["PWblk"])
        for gi in range(4):
            c_, hf = gi // 2, gi % 2
            dma("pool", PWblk[hf * 64:(hf + 1) * 64, c_, hf * 64:(hf + 1) * 64], pool_w_d[l, gi], [], ["PWblk"])
        ts(qg8[:], qg[:], 0.125, ALU.mult, ["qg"], ["qg8"])
        dma("pool", kcA[64:100, 0, :], cd["c_kcmp"], [], ["kcA_c"])
        dma("pool", kcA[64:100, 1, :], cd["c_kcmp"], [], ["kcA_c"])

    def rmsnorm_to_hT(gvec, gid, sq, rstd, tag):
        for tc in range(NTC):
            cs = slice(tc * 512, (tc + 1) * 512)
            for k in range(KC):
                sqk = sq[:, k % 2, :]
                act(sqk, xT[:, k, cs], AF.Square, [("xT", k, tc)], [(tag + "sq", k % 2)])
                mm(ps[7][:, :], ones_f[:, :], sqk, k == 0, k == KC - 1,
                   ["ones_f", (tag + "sq", k % 2)], [PS(7)])
            act(rstd[:, :], ps[7][:, :], AF.Ln, [PS(7), "eps6"], [tag + "rstd"], scale=1.0 / D, bias=eps6[:, 0:1])
            act(rstd[:, :], rstd[:, :], AF.Exp, [tag + "rstd"], [tag + "rstd"], scale=-0.5)
            for k in range(KC):
                stt(hT[:, k, cs], xT[:, k, cs], gvec[:, k:k + 1], rstd[:, :], ALU.mult, ALU.mult,
                    [("xT", k, tc), tag + "rstd", gid], [("hT", k, tc)])

    def hT_reads(tc):
        return [("hT", k, tc) for k in range(KC)]

    def phase_pool_conv(l):
        wv = w_in_d[l].rearrange("(k p) c -> p k c", p=128)
        dma("pool", wb_pc[:, :, 0:512], wv[:, :, C_POOL:C_POOL + 512], [], ["wb_pc0"])
        dma("pool", wb_pc[:, :, 512:1024], wv[:, :, C_POOL + 512:C_POOL + 1024], [], ["wb_pc1"])
        wo = w_out_d[l].rearrange("(j p) c -> p j c", p=128)
        dma("pool", WoPC[:, 0:2, :], wo[:, 0:2, :], [], ["WoPC"])
        dma("pool", WoPC[:, 2:4, :], wo[:, 6:8, :], [], ["WoPC"])
        wins = (2, 4, 8, 16)
        RB = [0, 1, 2, 3, 4, 5]

        def proj(col0, cs, tc):
            b_ = rot("p2r", RB)
            for k in range(KC):
                mm(ps[b_][:, :], wb_pc[:, k, col0:col0 + 128], hT[:, k, cs], k == 0, k == KC - 1,
                   ["wb_pc0", "wb_pc1", ("hT", k, tc)], [PS(b_)])
            return b_

        def wout_pc(tc):
            cs = slice(tc * 512, (tc + 1) * 512)
            yp_ = yPC[tc % 2]
            for dc in range(KC):
                ob = rot("p2o", [6, 7])
                for j in range(4):
                    mm(ps[ob][:, :], WoPC[:, j, dc * 128:(dc + 1) * 128], yp_[:, j, :], j == 0, j == 3,
                       ["WoPC", ("yPC", tc % 2, j)], [PS(ob)])
                tt(xT[:, dc, cs], xT[:, dc, cs], ps[ob][:, :], ALU.add, [("xT", dc, tc), PS(ob)], [("xT", dc, tc)])

        for tc in range(NTC + 1):
            if tc < NTC:
                cs = slice(tc * 512, (tc + 1) * 512)
                yp_ = yPC[tc % 2]
                for c_ in range(2):
                    ub = ubuf[:, c_, :]
                    pbc = pbuf[:, c_, :]
                    if tc == 0:
                        P.op("dve", lambda e, ub=ub: e.memset(ub[:, 0:16], 0.0), [], [("ubuf", c_)])
                        P.op("dve", lambda e, pbc=pbc: e.memset(pbc[:, 0:2], 0.0), [], [("pbuf", c_)])
                    else:
                        cp(ub[:, 0:16], ub[:, 512:528], [("ubuf", c_)], [("ubuf", c_)])
                        cp(pbc[:, 0:2], pbc[:, 512:514], [("pbuf", c_)], [("pbuf", c_)])
                ccb = {}
                for c_ in range(2):
                    b_ = proj(c_ * 128, cs, tc)
                    act(ubuf[:, c_, 16:528], ps[b_][:, :], AF.Copy, [PS(b_)], [("ubuf", c_)])
                for c_ in range(2):
                    b_ = proj(C_CX - C_POOL + c_ * 128, cs, tc)
                    act(cxs[c_][:, :], ps[b_][:, :], AF.Copy, [PS(b_)], [("cxs", c_)])
                    ccb[c_] = proj(C_CC - C_POOL + c_ * 128, cs, tc)
                    b_ = proj(C_CB - C_POOL + c_ * 128, cs, tc)
                    act(cbs[c_][:, :], ps[b_][:, :], AF.Copy, [PS(b_)], [("cbs", c_)])
                for c_ in range(2):
                    pbc = pbuf[:, c_, :]
                    tt(pbc[:, 2:514], ps[ccb[c_]][:, :], cxs[c_][:, :], ALU.mult, [PS(ccb[c_]), ("cxs", c_)], [("pbuf", c_)])
                for c_ in range(2):
                    ub = ubuf[:, c_, :]
                    sA_, sB_ = sA[c_], sB[c_]
                    nA, nB = ("sA", c_), ("sB", c_)
                    tt(sA_[:, 1:528], ub[:, 1:528], ub[:, 0:527], ALU.add, [("ubuf", c_)], [nA])
                    tt(sB_[:, 3:528], sA_[:, 3:528], sA_[:, 1:526], ALU.add, [nA], [nB])
                    if c_ == 1:
                        tt(sA_[:, 7:528], sB_[:, 7:528], sB_[:, 3:524], ALU.add, [nB], [nA])
                        tt(sB_[:, 15:528], sA_[:, 15:528], sA_[:, 7:520], ALU.add, [nA], [nB])
                    if tc == 0:
                        tt(sA_[0:64, 16:32], sA_[0:64, 16:32], pfix[0:64, c_, :], ALU.mult, [nA, "pfix"], [nA])
                        tt(sB_[64:128, 16:32], sB_[64:128, 16:32], pfix[64:128, c_, :], ALU.mult, [nB, "pfix"], [nB])
                    stt(dT[c_][0:64, :], sA_[0:64, 16:528], 1.0 / wins[2 * c_], ub[0:64, 16:528], ALU.mult, ALU.subtract,
                        [nA, ("ubuf", c_)], [("dT", c_)])
                    stt(dT[c_][64:128, :], sB_[64:128, 16:528], 1.0 / wins[2 * c_ + 1], ub[64:128, 16:528], ALU.mult,
                        ALU.subtract, [nB, ("ubuf", c_)], [("dT", c_)])
                for c_ in range(2):
                    pbc = pbuf[:, c_, :]
                    yb_ = ybuf[c_]
                    yid = ("ybuf", c_)
                    ts(yb_[:, :], pbc[:, 0:512], sconv[:, c_, 0:1], ALU.mult, [("pbuf", c_), "sconv"], [yid])
                    stt(yb_[:, :], pbc[:, 1:513], sconv[:, c_, 1:2], yb_[:, :], ALU.mult, ALU.add,
                        [("pbuf", c_), "sconv", yid], [yid])
                    stt(yb_[:, :], pbc[:, 2:514], sconv[:, c_, 2:3], yb_[:, :], ALU.mult, ALU.add,
                        [("pbuf", c_), "sconv", yid], [yid])
                    tt(yp_[:, 2 + c_, :], yb_[:, :], cbs[c_][:, :], ALU.mult, [yid, ("cbs", c_)], [("yPC", tc % 2, 2 + c_)])
            if tc >= 1:
                wout_pc(tc - 1)
            if tc < NTC:
                for c_ in range(2):
                    b_ = rot("p2r", RB)
                    mm(ps[b_][:, :], PWblk[:, c_, :], dT[c_][:, :], True, True, ["PWblk", ("dT", c_)], [PS(b_)])
                    act(yPC[tc % 2][:, c_, :], ps[b_][:, :], AF.Copy, [PS(b_), "pscale"], [("yPC", tc % 2, c_)],
                        scale=pscale[:, c_:c_ + 1])

    hn_pending = []

    def headnorm_flush():
        while hn_pending:
            pb, i_, gain, gain_id, out_lo, id_lo, out_hi, id_hi = hn_pending.pop(0)
            sqb = sq1 if i_ == 0 else sq2
            sqid = ("sqh", i_)
            sb_ = rot("p1s", [3, 4])
            mm(ps[sb_][:, :], onesblk[:, :], sqb[:, :], True, True, ["onesblk", sqid], [PS(sb_)])
            act(sqb[:, :], ps[sb_][:, :], AF.Ln, [PS(sb_), "eps6"], [sqid], scale=1.0 / 64, bias=eps6[:, 0:1])
            act(sqb[:, :], sqb[:, :], AF.Exp, [sqid], [sqid], scale=-0.5)
            stt(out_lo, ps[pb][0:64, :], gain[0:64], sqb[0:64, :], ALU.mult, ALU.mult, [PS(pb), sqid, gain_id], [id_lo])
            sg = stg[i_]
            stt(sg[64:128, :], ps[pb][64:128, :], gain[64:128], sqb[64:128, :], ALU.mult, ALU.mult,
                [PS(pb), sqid, gain_id], [("stg", i_)])
            dma("sp", out_hi, sg[64:128, :], [("stg", i_)], [id_hi])

    def headnorm_store(pb, gain, gain_id, out_lo, id_lo, out_hi, id_hi):
        i_ = rot("p1q", [0, 1])
        sqb = sq1 if i_ == 0 else sq2
        act(sqb[:, :], ps[pb][:, :], AF.Square, [PS(pb)], [("sqh", i_)])
        hn_pending.append((pb, i_, gain, gain_id, out_lo, id_lo, out_hi, id_hi))

    def phase_proj(l):
        wv = w_in_d[l].rearrange("(k p) c -> p k c", p=128)
        dma("pool", wbuf[0][:, :, 0:512], wv[:, :, C_Q:C_Q + 512], [], ["wbuf0"])
        dma("pool", wbuf[1][:, :, 0:512], wv[:, :, C_KSLC:C_KSLC + 512], [], ["wbuf1"])
        for hp in range(NH // 2):
            for tc in range(NTC):
                cs = slice(tc * 512, (tc + 1) * 512)
                pb = rot("p1a", [0, 1, 2])
                for k in range(KC):
                    mm(ps[pb][:, :], wbuf[0][:, k, hp * 128:(hp + 1) * 128], hT[:, k, cs], k == 0, k == KC - 1,
                       ["wbuf0", ("hT", k, tc)], [PS(pb)])
                headnorm_flush()
                headnorm_store(pb, qg8[:, 0:1], "qg8", qA[0:64, 2 * hp, cs], ("qA", 2 * hp, tc),
                               qA[0:64, 2 * hp + 1, cs], ("qA", 2 * hp + 1, tc))
        dma("pool", wbuf[0][:, :, 0:280], wv[:, :, C_VSLC:C_VSLC + 280], [], ["wbuf0"])
        for br in range(2):
            for tc in range(NTC):
                cs = slice(tc * 512, (tc + 1) * 512)
                pb = rot("p1a", [0, 1, 2])
                for k in range(KC):
                    mm(ps[pb][:, :], wbuf[1][:, k, br * 128:(br + 1) * 128], hT[:, k, cs], k == 0, k == KC - 1,
                       ["wbuf1", ("hT", k, tc)], [PS(pb)])
                headnorm_flush()
                headnorm_store(pb, kg[:, 1 + br:2 + br], "kg", kA[0:64, br, 0, cs], ("kA", br, 0, tc),
                               kA[0:64, br, 1, cs], ("kA", br, 1, tc))
        headnorm_flush()
        for kv in range(2):
            dma("pool", W1[:, :, :], cmp_w1_d[l, kv].rearrange("(l d) e -> d l e", d=64), [], ["W1"])
            for li in range(32):
                mm(ps[6][0:64, 0:1], W1[:, li, :], posT[:, kv, li:li + 1], li == 0, li == 31, ["W1", "posT"], [PS(6)])
            cp(cb[:, kv:kv + 1], ps[6][0:64, 0:1], [PS(6)], ["cb"])
            for g in range(2):
                co = 256 + kv * 128 + g * 64
                for tc in range(NTC):
                    cs = slice(tc * 512, (tc + 1) * 512)
                    pb = rot("p1a", [0, 1])
                    for k in range(KC):
                        mm(ps[pb][0:64, :], wbuf[1][:, k, co:co + 64], hT[:, k, cs], k == 0, k == KC - 1,
                           ["wbuf1", ("hT", k, tc)], [PS(pb)])
                    act(rawT[:, cs], ps[pb][0:64, :], AF.Copy, [PS(pb)], ["rawT"])
                for li in range(32):
                    rhs = rawT[:, li:li + 16 * (NCMP - 1) + 1:16]
                    mm(ps[6][0:64, 0:NCMP], W1[:, li, :], rhs, li == 0, li == 31, ["W1", "rawT"], [PS(6)])
                x_ = gtmp[:, 0, 0:NCMP]
                x2 = gtmp[:, 1, 0:NCMP]
                z_ = gtmp[:, 2, 0:NCMP]
                e_ = gtmp[:, 3, 0:NCMP]
                act(x_, ps[6][0:64, 0:NCMP], AF.Identity, [PS(6), "cb"], ["g_x"], bias=cb[:, kv:kv + 1])
                tt(x2, x_, x_, ALU.mult, ["g_x"], ["g_x2"])
                ts(x2, x2, 0.044715, ALU.mult, ["g_x2"], ["g_x2"], s2=1.0, op1=ALU.add)
                tt(z_, x2, x_, ALU.mult, ["g_x2", "g_x"], ["g_z"])
                act(e_, z_, AF.Exp, ["g_z"], ["g_e"], scale=-1.5957691216057308)
                ts(e_, e_, 1.0, ALU.add, ["g_e"], ["g_e"])
                P.op("dve", lambda e, e_=e_: e.reciprocal(out=e_, in_=e_), ["g_e"], ["g_e"])
                tt(hidg[:, 0:NCMP], x_, e_, ALU.mult, ["g_x", "g_e"], ["hidg"])
                if kv == 0:
                    mm(ps[7][0:64, 0:NCMP], W2[:, 0, :], hidg[:, 0:NCMP], True, True, ["W2", "hidg"], [PS(7)])
                    act(sq1[0:64, 0:NCMP], ps[7][0:64, 0:NCMP], AF.Square, [PS(7)], [("sqh", 0)])
                    sb_ = rot("p1s", [3, 4])
                    mm(ps[sb_][0:64, 0:NCMP], ones_f[0:64, 0:64], sq1[0:64, 0:NCMP], True, True, ["ones_f", ("sqh", 0)], [PS(sb_)])
                    act(sq2[0:64, 0:NCMP], ps[sb_][0:64, 0:NCMP], AF.Ln, [PS(sb_), "eps6"], [("sqh", 1)], scale=1.0 / 64,
                        bias=eps6[0:64, 0:1])
                    act(sq2[0:64, 0:NCMP], sq2[0:64, 0:NCMP], AF.Exp, [("sqh", 1)], [("sqh", 1)], scale=-0.5)
                    stt(kcA[0:64, g, :], ps[7][0:64, 0:NCMP], kg[0:64, 0:1], sq2[0:64, 0:NCMP], ALU.mult, ALU.mult,
                        [PS(7), ("sqh", 1), "kg"], [("kcA", g)])
                else:
                    mm(ps[7][0:NCMP, 0:64], hidg[:, 0:NCMP], W2[:, 1, :], True, True, ["W2", "hidg"], [PS(7)])
                    cp(vcA[0:NCMP, g, 0:64], ps[7][0:NCMP, 0:64], [PS(7)], [("vcA", g)])
        for ti in range(NT):
            tsl = slice(ti * 128, (ti + 1) * 128)
            vb = rot("p1v", [4, 5])
            for br in range(2):
                for k in range(KC):
                    mm(ps[vb][:, br * 128:(br + 1) * 128], hT[:, k, tsl], wbuf[0][:, k, br * 128:(br + 1) * 128],
                       k == 0, k == KC - 1, ["wbuf0", ("hT", k, ti // 4)], [PS(vb)])
            cp(Vaug[:, ti, :, :, 0:64], ps[vb][:, 0:256].rearrange("p (b g d) -> p b g d", b=2, g=2),
               [PS(vb)], [("Vaug", ti)])
        for tc in range(NTC):
            cs = slice(tc * 512, (tc + 1) * 512)
            pb = rot("p1a", [0, 1])
            for k in range(KC):
                mm(ps[pb][0:89, :], wbuf[0][:, k, 191:280], hT[:, k, cs], k == 0, k == KC - 1,
                   ["wbuf0", ("hT", k, tc)], [PS(pb)])
            act(sq1[64:89, :], ps[pb][64:89, :], AF.Exp, [PS(pb)], [("sqh", 0)], scale=-1.0)
            act(lngate[64:89, cs], sq1[64:89, :], AF.Ln, [("sqh", 0), "onec"], [("lngate", tc)], scale=1.0, bias=onec[64:89, 0:1])

    def epi_ln(ob, cbuf):
        act(cbuf[64:65, :], ps[ob][64:65, :], AF.Ln, [PS(ob), "tiny"], [("lnD", id(cbuf))], scale=1.0, bias=tiny[64:65, 0:1])

    def epi_mm(b, cbuf, bk=6):
        mm(ps[bk][0:64, :], selb[64:89, b, :], cbuf[64:89, :], True, True,
           ["selb", ("lnD", id(cbuf)), ("GMq", id(cbuf))], [PS(bk)])

    def epi_fin(ob, b, g, qt, first, last, aj, bk=6):
        qs = slice(qt * 128, (qt + 1) * 128)
        bi = rot("bcs", [0, 1])
        bcs = bc_sb[bi]
        bid = ("bc_sb", bi)
        ac = acc[aj]
        aid = ("acc", aj)
        act(bcs[:, :], ps[bk][0:64, :], AF.Exp, [PS(bk)], [bid])
        acc3 = ac[:, :].rearrange("p (h q) -> p h q", h=4)
        tmp3 = tmpo[:, :].rearrange("p (h q) -> p h q", h=4)
        if first:
            tt(ac[:, :], ps[ob][0:64, :], bcs[:, :], ALU.mult, [PS(ob), bid], [aid])
        else:
            tt(tmpo[:, :], ps[ob][0:64, :], bcs[:, :], ALU.mult, [PS(ob), bid], ["tmpo"])
            if last:
                tcq = qt // 4
                e_ids = [("oT", 2 * g + pp, 0, tcq) for pp in range(2)]
                o_ids = [("oT", 2 * g + pp, 1, tcq) for pp in range(2)]
                tt(hT[0:64, 2 * g:2 * g + 2, qs], acc3[:, 0:4:2, :], tmp3[:, 0:4:2, :], ALU.add, [aid, "tmpo"], e_ids)
                si = rot("ostg", [0, 1])
                tt(ostg[si][:, :, :], acc3[:, 1:4:2, :], tmp3[:, 1:4:2, :], ALU.add, [aid, "tmpo"], [("ostg", si)])
                dma("sp", hT[64:128, 2 * g:2 * g + 2, qs], ostg[si][:, :, :], [("ostg", si)], o_ids)
            else:
                tt(ac[:, :], ac[:, :], tmpo[:, :], ALU.add, [aid, "tmpo"], [aid])

    def phase_attn(l):
        LAG = int(os.environ.get("LAG", "3"))
        SBK = [0, 1, 2, 6] if os.environ.get("SB4", "1") == "1" else [0, 1, 2]
        NPT = 7
        NB = 4
        NDUM = int(os.environ.get("NDUM", "0"))
        EPD1 = int(os.environ.get("EPD1", "2"))
        EPD2 = int(os.environ.get("EPD2", "3"))
        CHS = tuple(int(v) for v in os.environ.get("CHS", "0,3,4,6,7,10").split(","))
        DUMN = int(os.environ.get("DUMN", "128"))
        wo = w_out_d[l]
        tasks = []
        for r_ in range(NT // 2):
            tasks += [(0, NT - 1 - r_), (0, r_), (1, NT - 1 - r_), (1, r_)]
        done_tc = {}
        tiles = []
        tstart = []
        for j, (g, qt) in enumerate(tasks):
            tstart.append(len(tiles))
            for kt in range(qt + 1):
                tiles.append((j, 0, kt, kt == 0, kt == qt))
            k0 = max(0, qt - 4)
            for kt in range(k0, qt + 1):
                tiles.append((j, 1, kt, kt == k0, kt == qt))
        n = len(tiles)
        deferred = {}

        def at(step, fn):
            deferred.setdefault(step, []).append(fn)

        obank = {}
        eps = []

        def ep_new(ob, b, j, first, last, cbuf):
            e_ = {"ob": ob, "b": b, "j": j, "first": first, "last": last, "cbuf": cbuf, "stage": 0}
            eps.append(e_)
            return e_

        def ep_to(e_, target):
            while e_["stage"] < target:
                nxt = e_["stage"] + 1
                for p_ in eps:
                    if p_ is e_:
                        break
                    if nxt == 1 and p_["cbuf"] is e_["cbuf"]:
                        ep_to(p_, 2)
                    if nxt == 2:
                        ep_to(p_, 3)
                if nxt == 1:
                    epi_ln(e_["ob"], e_["cbuf"])
                elif nxt == 2:
                    e_["bk"] = rot("ts", SBK) if len(SBK) == 4 else 6
                    epi_mm(e_["b"], e_["cbuf"], e_["bk"])
                else:
                    g_, qt_ = tasks[e_["j"]]
                    epi_fin(e_["ob"], e_["b"], g_, qt_, e_["first"], e_["last"], e_["j"] % NB, e_["bk"])
                    owners.pop(e_["ob"], None)
                    if e_["last"]:
                        done_tc[qt_ // 4] = done_tc.get(qt_ // 4, 0) + 1
                        if done_tc[qt_ // 4] == 8:
                            sched_wout(qt_ // 4)
                e_["stage"] = nxt
            while eps and eps[0]["stage"] == 3:
                eps.pop(0)

        owners = {}

        def take_obank(hold=True):
            while True:
                for _ in range(3):
                    ob = rot("to", [3, 4, 5])
                    if ob not in owners:
                        if hold:
                            owners[ob] = True
                        return ob
                assert eps, "no free O bank and nothing to force"
                ep_to(eps[0], 3)

        cur_step = [0]

        def sched_wout(tc):
            cs = slice(tc * 512, (tc + 1) * 512)

            def piece(dc):
                def f():
                    wi = rot("won", [0, 1])
                    wb_ = WoN[wi]
                    wid = ("WoN", wi)
                    dma("pool", wb_[:, :, :],
                        wo[256:768, dc * 128:(dc + 1) * 128].rearrange("(p r) c -> r p c", r=128), [], [wid])
                    ob = take_obank(hold=False)
                    for p_ in range(4):
                        mm(ps[ob][:, :], wb_[:, p_, :], hT[:, p_, cs], p_ == 0, p_ == 3,
                           [wid, ("oT", p_, 0, tc), ("oT", p_, 1, tc)], [PS(ob)])
                    tt(xT[:, dc, cs], xT[:, dc, cs], ps[ob][:, :], ALU.add, [("xT", dc, tc), PS(ob)], [("xT", dc, tc)])
                return f
            for dc in range(KC):
                at(cur_step[0] + 2 + 2 * dc, piece(dc))

        def q_ops(g, qt):
            qs = slice(qt * 128, (qt + 1) * 128)
            q_rhs = qA[0:100, 4 * g:4 * g + 4, qs]
            q_ids = [("qA", h, qt // 4) for h in range(4 * g, 4 * g + 4)] + ["qA_al"]
            return qs, q_rhs, q_ids

        chains = {}

        def chain_run(j, k):
            c_ = chains[j]
            while c_["done"] <= k:
                c_["fns"][c_["done"]]()
                c_["done"] += 1

        def cmp_chain(j):
            g, qt = tasks[j]
            qs, q_rhs, q_ids = q_ops(g, qt)
            cbuf = comb[j % NB]
            st_ = {}

            def A_():
                for e2_ in list(eps):
                    if e2_["cbuf"] is cbuf:
                        ep_to(e2_, 3)
                tt(cbuf[64:89, :].rearrange("p (h q) -> p h q", h=4), bcast_mid(lngate[64:89, qs], 4),
                   oh[64:89, g, :].unsqueeze(2).to_broadcast([25, 4, 128]), ALU.mult,
                   [("lngate", qt // 4), "oh"], [("GMq", id(cbuf)), ("lnD", id(cbuf))])
                sb_ = rot("ts", SBK)
                mm(ps[sb_][0:NCMP, :], kcA[0:100, g, :], q_rhs, True, False, q_ids + [("kcA", g), "kcA_c"], [PS(sb_)])
                mm(ps[sb_][0:NCMP, :], ident[0:NCMP, 0:NCMP], bcast_mid(maskc[0:NCMP, qs], 4), False, True,
                   ["ident", "maskc"], [PS(sb_)])
                pi = rot("tp", list(range(NPT)))
                act(Pt[pi][0:NCMP, :], ps[sb_][0:NCMP, :], AF.Exp, [PS(sb_)], [("Pt", pi)])
                st_["pi"] = pi

            def B_():
                if j - 1 in chains:
                    chain_run(j - 1, 5)
                pi = st_["pi"]
                ob = take_obank()
                st_["ob"] = ob
                mm(ps[ob][0:65, :], vcA[0:NCMP, g, :], Pt[pi][0:NCMP, :], True, True,
                   [("vcA", g), "vcA", ("Pt", pi)], [PS(ob)])
                for hh in range(4):
                    mm(ps[7][:, hh * 33:(hh + 1) * 33], Pt[pi][0:NCMP, hh * 128:(hh + 1) * 128], maug[0:NCMP, :],
                       True, True, [("Pt", pi), "maug"], [PS(7)])
                U3 = ps[7][:, 0:132].rearrange("p (h c) -> p h c", c=33)
                den = small[:, 0:4]
                ts(den, U3[:, :, 32], 1e-18, ALU.add, [PS(7)], ["den"])
                P.op("dve", lambda e, den=den: e.reciprocal(out=den, in_=den), ["den"], ["den"])
                tt(imp_t[:, :, :], U3[:, :, 0:32], den.unsqueeze(2).to_broadcast([128, 4, 32]), ALU.mult,
                   [PS(7), "den"], ["imp_t"])
                tt(imp_t[:, 0:2, :], imp_t[:, 0:2, :], imp_t[:, 2:4, :], ALU.add, ["imp_t"], ["imp_t"])
                imp = small[:, 8:40]
                tt(imp, imp_t[:, 0, :], imp_t[:, 1, :], ALU.add, ["imp_t"], ["imp"])
                tt(imp, imp, fkeep[:, qt, :], ALU.mult, ["imp", "fkeep"], ["imp"])
                tt(imp, imp, fadd[:, qt, :], ALU.add, ["imp", "fadd"], ["imp"])
                m8 = small[:, 40:48]
                P.op("dve", lambda e, m8=m8, imp=imp: e.max(out=m8, in_=imp), ["imp"], ["m8"])
                ts(imp, imp, m8[:, 7:8], ALU.is_ge, ["imp", "m8"], ["imp"], s2=1.0, op1=ALU.subtract)
                ts(MB[:, 64:96], imp, -NEG, ALU.mult, ["imp"], ["MB"])

            def C1_():
                st_["ep"] = ep_new(st_["ob"], 0, j, True, False, cbuf)
                ep_to(st_["ep"], 1)

            def C2_():
                ep_to(st_["ep"], 2)

            def C3_():
                ep_to(st_["ep"], 3)

            def D_():
                mm(ps[7][0:96, 256:384], MB[:, 0:96], ident[:, :], True, True, ["MB", "ident"], [PS(7)])
                act(qA[64:96, 4 * g:4 * g + 4, qs], bcast_mid(ps[7][64:96, 256:384], 4), AF.Copy, [PS(7)],
                    [("qAm", g, qt)])

            chains[j] = {"fns": [A_, B_, C1_, C2_, C3_, D_], "done": 0}
            if j < 2:
                chain_run(j, 5)
            else:
                s0 = tstart[j - 2]
                lim = tstart[j] - 1
                for k_, dstep in enumerate(CHS):
                    at(min(s0 + dstep, lim), lambda k_=k_: chain_run(j, k_))

        info = {}

        def emit_qk(i):
            j, br, kt, first, last = tiles[i]
            g, qt = tasks[j]
            qs, q_rhs, q_ids = q_ops(g, qt)
            ks_ = slice(kt * 128, (kt + 1) * 128)
            sb_ = rot("ts", SBK)
            diag = kt == qt
            edge = (br == 1) and (kt == qt - 4)
            extra = [("qAm", g, qt)] if br == 0 else []
            mm(ps[sb_][:, :], kA[0:100, br, g, ks_], q_rhs, True, not (diag or edge),
               q_ids + extra + [("kA", br, g, kt // 4), "kA_c"], [PS(sb_)])
            if diag:
                mm(ps[sb_][:, :], ident[:, :], bcast_mid(cmd[:, :], 4), False, True, ["ident", "cmd"], [PS(sb_)])
            if edge:
                mm(ps[sb_][:, :], ident[:, :], bcast_mid(cmw[:, :], 4), False, True, ["ident", "cmw"], [PS(sb_)])
            pi = rot("tp", list(range(NPT)))
            act(Pt[pi][:, :], ps[sb_][:, :], AF.Exp, [PS(sb_)], [("Pt", pi)])
            info[i] = pi

        def emit_pv(i, step):
            j, br, kt, first, last = tiles[i]
            g, qt = tasks[j]
            pi = info.pop(i)
            if first:
                obank[(j, br)] = take_obank()
            ob = obank[(j, br)]
            mm(ps[ob][0:65, :], Vaug[:, kt, br, g, :], Pt[pi][:, :], first, last,
               [("Vaug", kt), "Vaug_1", ("Pt", pi)], [PS(ob)])
            if not last:
                for _d in range(NDUM):
                    mm(ps[ob][0:65, 0:DUMN], zeros_b[:, 0:65], ident[:, 0:DUMN], False, False, ["zeros_b", "ident"], [PS(ob)])
            if last:
                cbuf = comb[j % NB]
                ep_ = ep_new(ob, 1 + br, j, False, br == 1, cbuf)
                ep_to(ep_, 1)
                at(step + EPD1, lambda: ep_to(ep_, 2))
                at(step + EPD2, lambda: ep_to(ep_, 3))

        for j in range(len(tasks)):
            cmp_chain(j)
        i = 0
        while i < n + LAG or deferred:
            cur_step[0] = i
            if i < n:
                emit_qk(i)
            if 0 <= i - LAG < n:
                emit_pv(i - LAG, i)
            for fn in deferred.pop(i, []):
                fn()
            i += 1
        for e_ in list(eps):
            ep_to(e_, 3)
        while deferred:
            k_ = min(deferred)
            for fn in deferred.pop(k_):
                fn()

    def phase_ffn(l):
        up = ffn_up_d[l].rearrange("(k p) c -> p k c", p=128)
        dn = ffn_down_d[l]
        P.op("dve", lambda e: e.memset(abuf[:, 0:2], 0.0), [], ["abuf_h"])
        for half in range(2):
            for fi in range(11):
                fc = half * 11 + fi
                wb = wup[fc % 2]
                wid = ("wup", fc % 2)
                dma("pool", wb[:, :, 0:128], up[:, :, fc * 128:(fc + 1) * 128], [], [wid])
                dma("pool", wb[:, :, 128:256], up[:, :, DFF + fc * 128:DFF + (fc + 1) * 128], [], [wid])
                for tc in range(NTC):
                    cs = slice(tc * 512, (tc + 1) * 512)
                    ab = rot("fa", [0, 1])
                    gb = rot("fg", [2, 3])
                    for k in range(KC):
                        mm(ps[ab][:, :], wb[:, k, 0:128], hT[:, k, cs], k == 0, k == KC - 1, [wid, ("hT", k, tc)], [PS(ab)])
                    for k in range(KC):
                        mm(ps[gb][:, :], wb[:, k, 128:256], hT[:, k, cs], k == 0, k == KC - 1, [wid, ("hT", k, tc)], [PS(gb)])
                    act(abuf[:, 2 + tc * 512:2 + (tc + 1) * 512], ps[ab][:, :], AF.Copy, [PS(ab)], [("abuf", tc)])
                    rd = [("abuf", tc), "abuf_h", "fconv"] + ([("abuf", tc - 1)] if tc else [])
                    y_ = yb[tc % 2]
                    yid = ("yb", tc % 2)
                    o = tc * 512
                    act(y_[:, :], abuf[:, o:o + 512], AF.Copy, rd, [yid], scale=fconv[:, fc, 0:1])
                    stt(y_[:, :], abuf[:, o + 1:o + 513], fconv[:, fc, 1:2], y_[:, :], ALU.mult, ALU.add, rd + [yid], [yid])
                    stt(y_[:, :], abuf[:, o + 2:o + 514], fconv[:, fc, 2:3], y_[:, :], ALU.mult, ALU.add, rd + [yid], [yid])
                    s_ = sil[tc % 2]
                    sid = ("sil", tc % 2)
                    act(s_[:, :], y_[:, :], AF.Silu, [yid], [sid])
                    tt(mT[:, fi, cs], s_[:, :], ps[gb][:, :], ALU.mult, [sid, PS(gb)], [("mT", fi, tc)])
            for dc in range(KC):
                wd = wdn[dc % 2]
                wdid = ("wdn", dc % 2)
                dma("pool", wd[:, :, :],
                    dn[half * 1408:(half + 1) * 1408, dc * 128:(dc + 1) * 128].rearrange("(f p) c -> p f c", p=128),
                    [], [wdid])
                for tc in range(NTC):
                    cs = slice(tc * 512, (tc + 1) * 512)
                    ob = rot("fd", [4, 5])
                    for fi in range(11):
                        mm(ps[ob][:, :], wd[:, fi, :], mT[:, fi, cs], fi == 0, fi == 10, [wdid, ("mT", fi, tc)], [PS(ob)])
                    tt(xT[:, dc, cs], xT[:, dc, cs], ps[ob][:, :], ALU.add, [("xT", dc, tc), PS(ob)], [("xT", dc, tc)])

    load_consts()
    for s in range(NSEQ):
        xv = xT_d[s].rearrange("(k p) t -> p k t", p=128)
        for k in range(KC):
            dma("sp", xT[:, k, :], xv[:, k, :], [], [("xT", k, tc) for tc in range(NTC)])
        for l in range(NL):
            if upto >= 1:
                load_layer_small(l)
            if upto >= 2:
                rmsnorm_to_hT(g1, "g1", sq_n, rstd_n, "n1")
            if upto >= 3:
                phase_pool_conv(l)
            P.barrier()
            if upto >= 4:
                init_attn_consts()
            if upto >= 5:
                phase_proj(l)
            P.barrier()
            if upto >= 6:
                phase_attn(l)
            P.barrier()
            if upto >= 8:
                rmsnorm_to_hT(g2, "g2", sq_f, rstd_f, "n2")
            if upto >= 9:
                phase_ffn(l)
            P.barrier()
        yv = yT_d[s].rearrange("(k p) t -> p k t", p=128)
        for k in range(KC):
            dma("sp", yv[:, k, :], xT[:, k, :], [("xT", k, tc) for tc in range(NTC)], [])
    P.emit()
    return nc, hc


W_IN_PERM = None


def _perm_cols():
    o = {"pool": 0, "q": 256, "kcmp": 768, "vcmp": 896, "kslc": 1024, "vslc": 1152, "kwin": 1280,
         "vwin": 1408, "gate": 1536, "cb": 1560, "cc": 1816, "cx": 2072}
    order = [("q", 512), ("kslc", 128), ("kwin", 128), ("kcmp", 128), ("vcmp", 128), ("vslc", 128),
             ("vwin", 128), ("gate", 24), ("pool", 256), ("cb", 256), ("cc", 256), ("cx", 256)]
    idx = np.concatenate([np.arange(o[n], o[n] + w) for n, w in order])
    assert idx.size == 2328
    return idx


def prep_weights(inp, NL):
    f = lambda a: np.ascontiguousarray(np.asarray(a, dtype=np.float32))
    w = {}
    w["w_in"] = f(np.asarray(inp["w_in"])[:NL][:, :, _perm_cols()])
    w["w_out"] = f(np.asarray(inp["w_out"])[:NL])
    w["ffn_up"] = f(np.asarray(inp["ffn_up"])[:NL])
    w["ffn_down"] = f(np.asarray(inp["ffn_down"])[:NL])
    w["cmp_w1"] = f(np.asarray(inp["cmp_w1"])[:NL])
    w["cmp_w2"] = f(np.asarray(inp["cmp_w2"])[:NL])
    w["cmp_posT"] = f(np.asarray(inp["cmp_pos"])[:NL].transpose(0, 1, 3, 2))
    w["pool_w"] = f(np.asarray(inp["pool_w"])[:NL])
    w["g1"] = f(np.asarray(inp["norm1_g"])[:NL].reshape(NL, 8, 128).transpose(0, 2, 1))
    w["g2"] = f(np.asarray(inp["norm2_g"])[:NL].reshape(NL, 8, 128).transpose(0, 2, 1))
    w["qg"] = f(np.asarray(inp["q_norm_g"])[:NL].reshape(NL, 64, 1))
    w["kg"] = f(np.asarray(inp["k_norm_g"])[:NL].transpose(0, 2, 1))
    w["pscale"] = f(np.asarray(inp["pool_scale"])[:NL].reshape(NL, 2, 128).transpose(0, 2, 1))
    w["sconv"] = f(np.asarray(inp["sconv_w"])[:NL].reshape(NL, 3, 2, 128).transpose(0, 3, 2, 1))
    w["fconv"] = f(np.asarray(inp["ffn_conv"])[:NL].reshape(NL, 3, NFC, 128).transpose(0, 3, 2, 1))
    return w


_CACHE = {}


def run_model(inp, T, NSEQ, NL, n_cores, dbg=None, upto=99):
    key = (T, NSEQ, NL, tuple(sorted(dbg.items())) if dbg else None, upto)
    if key not in _CACHE:
        _CACHE[key] = build_program(T, NSEQ, NL, dbg, upto)
    nc, hc = _CACHE[key]
    w = prep_weights(inp, NL)
    x = np.asarray(inp["x"], dtype=np.float32)
    in_maps = []
    for c in range(n_cores):
        m = dict(w)
        m.update(hc)
        m["xT"] = np.ascontiguousarray(x[c * NSEQ:(c + 1) * NSEQ].transpose(0, 2, 1))
        in_maps.append(m)
    res = run_bass_kernel_spmd(nc, in_maps, core_ids=list(range(n_cores)))
    outs = [r["yT"].transpose(0, 2, 1) for r in res.results]
    return np.ascontiguousarray(np.concatenate(outs, axis=0)), res


def kernel(**inputs):
    out, _ = run_model(inputs, 2048, 2, 4, 8)
    return out.astype(np.float32)
```
